# Optimizing a Trainium2 kernel written in Bass

```python
import math
import jax, jax.numpy as jnp
from jax import lax
import numpy as np

D_MODEL = 1024
BATCH = 4
SEQ = 4096
DEPTH = 1

S5_WIDTH = D_MODEL // 2
S5_GROUP = 16
S5_GROUPS = S5_WIDTH // S5_GROUP
S5_STATE = 64
ATTN_HEADS = 8
HEAD_DIM = 64
ATTN_WIDTH = ATTN_HEADS * HEAD_DIM
Q_BLOCK = 128
FFN_HIDDEN = int(math.ceil(8 * D_MODEL / 3 / 256)) * 256
N_ADA = 6
RMS_EPS = 1e-6
IN_SIZES = (S5_WIDTH, ATTN_WIDTH, ATTN_WIDTH, ATTN_WIDTH, D_MODEL, D_MODEL)
IN_SPLITS = tuple(int(s) for s in np.cumsum(IN_SIZES)[:-1])
IN_WIDTH = int(sum(IN_SIZES))

kernel_name = "hybrid_s5_stickbreaking_adaln_block"


def rmsnorm(x, g):
    xf = x.astype(jnp.float32)
    xf = xf * lax.rsqrt(jnp.mean(xf * xf, axis=-1, keepdims=True) + RMS_EPS)
    return (xf * g.astype(jnp.float32)).astype(x.dtype)


def modulate(h, shift, scale):
    return h * (1.0 + scale) + shift


def _linear_recurrence(e1, e2):
    a1, b1 = e1
    a2, b2 = e2
    return a2 * a1, a2 * b1 + b2


def s5_branch(u, lam_re, lam_im, log_dt, b_re, b_im, c_re, c_im, d_skip, w_glu, b_glu):
    bsz, seq, _ = u.shape
    uf = u.astype(jnp.float32).reshape(bsz, seq, S5_GROUPS, S5_GROUP)
    lam = lax.complex(lam_re.astype(jnp.float32), lam_im.astype(jnp.float32))
    dt = jnp.exp(log_dt.astype(jnp.float32))[:, None]
    lam_bar = jnp.exp(lam * dt)
    b = lax.complex(b_re.astype(jnp.float32), b_im.astype(jnp.float32))
    b_bar = ((lam_bar - 1.0) / lam)[..., None] * b
    bu = jnp.einsum('btgi,gpi->btgp', uf.astype(jnp.complex64), b_bar)
    a = jnp.broadcast_to(lam_bar, bu.shape)
    _, states = lax.associative_scan(_linear_recurrence, (a, bu), axis=1)
    cmat = lax.complex(c_re.astype(jnp.float32), c_im.astype(jnp.float32))
    y = jnp.einsum('btgp,gip->btgi', states, cmat).real + d_skip.astype(jnp.float32) * uf
    y = jax.nn.gelu(y.reshape(bsz, seq, S5_WIDTH))
    y = y * jax.nn.sigmoid(y @ w_glu.astype(jnp.float32) + b_glu.astype(jnp.float32))
    return y.astype(u.dtype)


def stick_breaking_attention(q, k, v):
    bsz, seq, _ = q.shape
    n_blocks = seq // Q_BLOCK
    scale = 1.0 / math.sqrt(HEAD_DIM)
    qh = q.astype(jnp.float32).reshape(bsz, n_blocks, Q_BLOCK, ATTN_HEADS, HEAD_DIM)
    qh = qh.transpose(1, 0, 3, 2, 4)
    kh = k.astype(jnp.float32).reshape(bsz, seq, ATTN_HEADS, HEAD_DIM).transpose(0, 2, 1, 3)
    vh = v.astype(jnp.float32).reshape(bsz, seq, ATTN_HEADS, HEAD_DIM).transpose(0, 2, 1, 3)
    starts = jnp.arange(n_blocks, dtype=jnp.int32) * Q_BLOCK
    key_pos = jnp.arange(seq, dtype=jnp.int32)[None, :]

    def one_block(args):
        qb, t0 = args
        z = jnp.einsum('bhqd,bhkd->bhqk', qb, kh) * scale
        q_pos = t0 + jnp.arange(Q_BLOCK, dtype=jnp.int32)[:, None]
        mask = key_pos < q_pos
        log_not = jnp.where(mask, jax.nn.log_sigmoid(-z), 0.0)
        suffix = lax.cumsum(log_not, axis=3, reverse=True) - log_not
        weights = jnp.where(mask, jnp.exp(jax.nn.log_sigmoid(z) + suffix), 0.0)
        return jnp.einsum('bhqk,bhkd->bhqd', weights, vh)

    out = lax.map(one_block, (qh, starts))
    out = out.transpose(1, 0, 3, 2, 4).reshape(bsz, seq, ATTN_WIDTH)
    return out.astype(q.dtype)


def setup_inputs(seed: int = 0) -> dict:
    key = jax.random.key(seed)
    ks = jax.random.split(key, 26)
    f32 = jnp.float32
    D, L, G, P, GS = D_MODEL, DEPTH, S5_GROUPS, S5_STATE, S5_GROUP

    def nrm(k, shape, s):
        return jax.random.normal(k, shape, f32) * s

    return {
        "x": nrm(ks[0], (BATCH, SEQ, D), 1.0),
        "c": nrm(ks[1], (BATCH, D), 1.0),
        "w_ada": nrm(ks[2], (L, D, N_ADA * D), 0.1 * D ** -0.5),
        "b_ada": nrm(ks[3], (L, N_ADA * D), 0.1),
        "norm1_g": 1.0 + nrm(ks[4], (L, D), 0.01),
        "w_in": nrm(ks[5], (L, D, IN_WIDTH), D ** -0.5),
        "lam_re": -0.5 + nrm(ks[6], (L, G, P), 0.01),
        "lam_im": jnp.pi * jnp.arange(P, dtype=f32)[None, None, :] + nrm(ks[7], (L, G, P), 0.01),
        "log_dt": jax.random.uniform(ks[8], (L, G), f32, math.log(1e-3), math.log(1e-1)),
        "b_re": nrm(ks[9], (L, G, P, GS), (2.0 * GS) ** -0.5),
        "b_im": nrm(ks[10], (L, G, P, GS), (2.0 * GS) ** -0.5),
        "c_re": nrm(ks[11], (L, G, GS, P), (2.0 * P) ** -0.5),
        "c_im": nrm(ks[12], (L, G, GS, P), (2.0 * P) ** -0.5),
        "d_skip": nrm(ks[13], (L, G, GS), 1.0),
        "w_glu": nrm(ks[14], (L, S5_WIDTH, S5_WIDTH), S5_WIDTH ** -0.5),
        "b_glu": nrm(ks[15], (L, S5_WIDTH), 0.02),
        "w_a": nrm(ks[16], (L, S5_WIDTH, D), S5_WIDTH ** -0.5),
        "w_b": nrm(ks[17], (L, ATTN_WIDTH, D), ATTN_WIDTH ** -0.5),
        "w_o": nrm(ks[18], (L, D, D), D ** -0.5),
        "norm2_g": 1.0 + nrm(ks[19], (L, D), 0.01),
        "w_ffn_gate": nrm(ks[20], (L, D, FFN_HIDDEN), D ** -0.5),
        "w_ffn_up": nrm(ks[21], (L, D, FFN_HIDDEN), D ** -0.5),
        "w_ffn_down": nrm(ks[22], (L, FFN_HIDDEN, D), FFN_HIDDEN ** -0.5),
        "norm_f_g": 1.0 + nrm(ks[23], (D,), 0.01),
    }


def reference(x, c, w_ada, b_ada, norm1_g, w_in, lam_re, lam_im, log_dt, b_re, b_im, c_re, c_im,
              d_skip, w_glu, b_glu, w_a, w_b, w_o, norm2_g, w_ffn_gate, w_ffn_up, w_ffn_down, norm_f_g):
    cond = jax.nn.silu(c)
    for l in range(DEPTH):
        mod = (cond @ w_ada[l] + b_ada[l])[:, None, :]
        sh1, sc1, g1, sh2, sc2, g2 = jnp.split(mod, N_ADA, axis=-1)

        h = modulate(rmsnorm(x, norm1_g[l]), sh1, sc1)
        proj = h @ w_in[l]
        u, q, k, v, gate_a, gate_b = jnp.split(proj, IN_SPLITS, axis=-1)
        y_a = s5_branch(u, lam_re[l], lam_im[l], log_dt[l], b_re[l], b_im[l], c_re[l], c_im[l],
                        d_skip[l], w_glu[l], b_glu[l]) @ w_a[l]
        y_b = stick_breaking_attention(q, k, v) @ w_b[l]
        merged = (jax.nn.sigmoid(gate_a) * y_a + jax.nn.sigmoid(gate_b) * y_b) @ w_o[l]
        x = x + g1 * merged

        h = modulate(rmsnorm(x, norm2_g[l]), sh2, sc2)
        ffn = (jax.nn.silu(h @ w_ffn_gate[l]) * (h @ w_ffn_up[l])) @ w_ffn_down[l]
        x = x + g2 * ffn
    return rmsnorm(x, norm_f_g)
```

```python
import math
import os
from contextlib import ExitStack

import numpy as np
import ml_dtypes
import concourse.bass as bass
import concourse.mybir as mybir
from concourse.bass_utils import run_bass_kernel_spmd

F32 = mybir.dt.float32
BF16 = mybir.dt.bfloat16
AF = mybir.ActivationFunctionType
ALU = mybir.AluOpType
NPBF = ml_dtypes.bfloat16

D = 1024
T = 4096
NB = 4
G = 32
FF = 2816
NFT = FF // 128
EPS = 1e-6
MAGIC = 12582912.0
C1 = 6.28125
C2 = 2.0 * math.pi - 6.28125
PI_LO = 3.1415925
NEG_BIG = -30000.0
OWN = {0: [0, 3, 4, 7], 1: [1, 2, 5, 6]}
DEBUG = False


class Prog:
    ENGS = ("pe", "act", "dve", "pool", "sp")

    def __init__(self):
        self.streams = {e: [] for e in self.ENGS}
        self.count = {}
        self.known = {e: {} for e in self.ENGS}
        self.lastw = {}
        self.readers = {}
        self.dma_sems = set()
        self.enabled = True
        self.phase = 0

    def _deps(self, eng, reads, writes):
        deps = {}

        def add(tok):
            if tok is None:
                return
            s, v = tok
            if deps.get(s, 0) < v:
                deps[s] = v

        for k in reads:
            add(self.lastw.get(k))
        for k in writes:
            add(self.lastw.get(k))
            for s, v in self.readers.get(k, {}).items():
                add((s, v))
        out = []
        for s, v in deps.items():
            if s == "pe" and eng == "pe":
                continue
            if self.known[eng].get(s, 0) >= v:
                continue
            self.known[eng][s] = v
            out.append((s, v))
        return out

    def _commit(self, tok, reads, writes):
        s, v = tok
        for k in reads:
            d = self.readers.setdefault(k, {})
            if d.get(s, 0) < v:
                d[s] = v
        for k in writes:
            self.lastw[k] = tok
            self.readers[k] = {}

    def op(self, eng, fn, reads=(), writes=()):
        if not self.enabled:
            return
        waits = self._deps(eng, reads, writes)
        v = self.count.get(eng, 0) + 1
        self.count[eng] = v
        self.streams[eng].append((waits, fn, (eng, 1)))
        self._commit((eng, v), reads, writes)

    def dma(self, fn, sem, reads=(), writes=(), q="sp"):
        if not self.enabled:
            return
        self.dma_sems.add(sem)
        waits = self._deps(q, reads, writes)
        v = self.count.get(sem, 0) + 16
        self.count[sem] = v
        self.streams[q].append((waits, fn, (sem, 16)))
        self._commit((sem, v), reads, writes)

    def barrier(self, exclude=()):
        if not self.enabled:
            return
        toks = [(s, v) for s, v in self.count.items() if s not in exclude and v > 0]
        for e in self.ENGS:
            if e in exclude:
                continue
            waits = []
            for s, v in toks:
                if s == e:
                    continue
                if self.known[e].get(s, 0) >= v:
                    continue
                self.known[e][s] = v
                waits.append((s, v))
            if waits:
                self.streams[e].append((waits, None, None))

    def final_wait(self, eng, semnames):
        waits = [(s, self.count[s]) for s in semnames if self.count.get(s, 0) > 0]
        self.streams[eng].append((waits, None, None))

    def emit(self, block, sems):
        engmap = {"pe": block.tensor, "act": block.scalar, "dve": block.vector,
                  "pool": block.gpsimd, "sp": block.sync}
        for e in self.ENGS:
            stream = self.streams[e]

            def body(engine, stream=stream):
                for waits, fn, inc in stream:
                    for s, v in waits:
                        engine.wait_ge(sems[s], v)
                    if fn is not None:
                        ins = fn(engine)
                        ins.then_inc(sems[inc[0]], inc[1])

            engmap[e](body)


def bc_last(ap, n):
    shp = list(ap.shape)
    if len(shp) == 2:
        return ap.rearrange("p (a o) -> p a o", o=1).to_broadcast([shp[0], shp[1], n])
    return ap.rearrange("p a (b o) -> p a b o", o=1).to_broadcast([shp[0], shp[1], shp[2], n])


def build_program():
    nc = bass.Bass("TRN2", target_bir_lowering=False)
    p = Prog()

    def din(name, shape, dt=F32):
        return nc.dram_tensor(name, list(shape), dt, kind="ExternalInput").ap()

    x_all = din("x_all", [T, D])
    x_own = din("x_own", [2048, D])
    cT_d = din("cT", [128, 8])
    w_ada = din("w_ada", [D, 6 * D])
    b_adaT_d = din("b_adaT", [128, 48])
    b_ada_row = din("b_ada_row", [1, 6 * D])
    n1gT_d = din("n1gT", [128, 8])
    n2gT_d = din("n2gT", [128, 8])
    nfg_row = din("nfg_row", [1, D])
    w_in = din("w_in", [D, 4096])
    lamre_d = din("lamre_p", [128, 16])
    lamim_d = din("lamim_p", [128, 16])
    logdt_d = din("logdt_p", [128, 16])
    bre_d = din("bre_p", [128, 16, 16])
    bim_d = din("bim_p", [128, 16, 16])
    cre_d = din("cre_p", [128, 16, 16])
    cim_d = din("cim_p", [128, 16, 16])
    dvec_d = din("dvec", [128, 32])
    w_glu = din("w_glu", [512, 512])
    b_gluT_d = din("b_gluT", [128, 4])
    w_a = din("w_a", [512, D])
    w_b = din("w_b", [512, D])
    w_o = din("w_o", [D, D])
    wg = din("wg", [D, FF])
    wu = din("wu", [D, FF])
    wd = din("wd", [FF, D])
    identf_d = din("identf", [128, 128])
    atri_d = din("atri", [128, 128], BF16)
    neguincl_d = din("neguincl", [128, 128], BF16)
    neglow_d = din("neglow", [128, 128], BF16)
    cmask_d = din("cmask", [128, 128])
    kv_d = din("kv", [128, 32])
    maskB_d = din("maskB", [128, 4, 8, 512], BF16)
    sel_d = din("sel", [128, 4, 64], BF16)
    ones_row_d = din("ones_row", [1, 128])
    hmask_d = din("hmask", [128, 2])
    out_d = nc.dram_tensor("out", [2048, D], F32, kind="ExternalOutput").ap()
    dbg = {}

    def scratch(name, shape, dt=BF16):
        return nc.dram_tensor(name, list(shape), dt).ap()

    win_s = scratch("win_s", [D, 4096])
    wglu_s = scratch("wglu_s", [512, 512])
    wa_s = scratch("wa_s", [512, D])
    wb_s = scratch("wb_s", [512, D])
    wo_s = scratch("wo_s", [D, D])
    wg_s = scratch("wg_s", [NFT, 128, 8, 128])
    wu_s = scratch("wu_s", [NFT, 128, 8, 128])
    wd_s = scratch("wd_s", [FF, D])
    x1_s = scratch("x1_s", [2048, D], F32)
    sglu_s = scratch("sglu_s", [128, 4, 2048])
    attn_s = scratch("attn_s", [128, 4, 2048])

    semnames = ["pe", "act", "dve", "pool", "st_dbg", "st_sg", "st_at"]
    RING_SEMS = {"ld_cast": 2, "ld_x": 8, "ld_w": 6, "ld_wada": 2, "ld_mask": 2, "ld_wgu": 3, "ld_x1": 2,
                 "st_cast": 2, "st_out": 2, "st_x1": 2, "ldc": 30, "ld_sa": 2, "ld_sb": 2, "ld_wgt": 3, "ld_wgv": 3}
    for k, n in RING_SEMS.items():
        for i in range(n):
            semnames.append(f"{k}{i}")
    CEX = ("pool", "ld_cast0", "ld_cast1", "st_cast0", "st_cast1")

    with ExitStack() as top:
        sems = {s: top.enter_context(nc.semaphore(s)) for s in semnames}
        block = top.enter_context(nc.Block())

        def sbt(st, name, shape, dt):
            return st.enter_context(nc.sbuf_tensor("sb_" + name, list(shape), dt))

        def pst(st, name, shape, dt=F32):
            return st.enter_context(nc.psum_tensor("pp_" + name, list(shape), dt))

        uid = [0]

        def newkey(prefix="k"):
            uid[0] += 1
            return f"{prefix}#{uid[0]}"

        def phase_end(dumps=(), exclude=CEX):
            p.barrier(exclude=exclude)
            p.phase += 1
            if p.phase >= int(os.environ.get("K_STOP", "99")):
                p.enabled = False
            if DEBUG and dumps:
                for (name, ap, shape, dt) in dumps:
                    o = nc.dram_tensor("dbg_" + name, list(shape), dt, kind="ExternalOutput").ap()
                    dbg[name] = o
                    p.dma(lambda e, o=o, ap=ap: e.dma_start(out=o, in_=ap), "st_dbg")
                p.barrier(exclude=exclude)

        subc = [0]

        def ckpt():
            subc[0] += 1
            if subc[0] >= int(os.environ.get("K_SUB", "999")):
                p.enabled = False

        cctr = [0]

        def cload(tile, src, key):
            i = cctr[0]
            cctr[0] += 1
            p.dma(lambda e: e.dma_start(out=tile, in_=src), f"ldc{i}", writes=[key])

        identf = sbt(top, "identf", [128, 128], F32)
        identb = sbt(top, "identb", [128, 128], BF16)
        cload(identf[:], identf_d, "identf")
        p.op("dve", lambda e: e.tensor_copy(out=identb[:], in_=identf[:]), reads=["identf"], writes=["identb"])
        modT = sbt(top, "modT", [128, 4, 8], F32)
        geff1 = sbt(top, "geff1", [128, 8], F32)
        geff2 = sbt(top, "geff2", [128, 8], F32)
        g1bc = sbt(top, "g1bc", [128, D], F32)
        g2bc = sbt(top, "g2bc", [128, D], F32)
        vecs = sbt(top, "vecs", [128, 96], F32)
        vctr = [0]

        def newvec():
            i = vctr[0] % 96
            vctr[0] += 1
            return vecs[:, i:i + 1], f"vec{i}"

        epsv = sbt(top, "epsv", [128, 1], F32)
        p.op("dve", lambda e: e.memset(epsv[:], EPS), writes=["epsv"])
        junk = sbt(top, "junk", [128, D], BF16)
        sh1 = modT[:, 0, :]
        sh2 = modT[:, 2, :]

        CW = 1408
        cst_in = [sbt(top, f"cst_in{i}", [128, CW], F32) for i in range(2)]
        cst_out = [sbt(top, f"cst_out{i}", [128, CW], BF16) for i in range(2)]
        cast_jobs = []
        cast_done = {}
        cast_pos = [0]

        def add_cast(src_ap, ncols, dst_fn, done_key):
            cast_jobs.append((src_ap, ncols, dst_fn, done_key))

        for kt in range(8):
            for h in range(2):
                add_cast(w_in[kt * 128:(kt + 1) * 128, h * 1024:(h + 1) * 1024], 1024,
                         (lambda kt=kt, h=h: win_s[kt * 128:(kt + 1) * 128, h * 1024:(h + 1) * 1024]), "win_qkv")
        for kt in range(4):
            add_cast(w_glu[kt * 128:(kt + 1) * 128, :], 512, (lambda kt=kt: wglu_s[kt * 128:(kt + 1) * 128, :]), "wglu_s")
        for kt in range(8):
            for h in range(2, 4):
                add_cast(w_in[kt * 128:(kt + 1) * 128, h * 1024:(h + 1) * 1024], 1024,
                         (lambda kt=kt, h=h: win_s[kt * 128:(kt + 1) * 128, h * 1024:(h + 1) * 1024]), "win_s")
        for kt in range(4):
            add_cast(w_a[kt * 128:(kt + 1) * 128, :], 1024, (lambda kt=kt: wa_s[kt * 128:(kt + 1) * 128, :]), "wa_s")
        for kt in range(4):
            add_cast(w_b[kt * 128:(kt + 1) * 128, :], 1024, (lambda kt=kt: wb_s[kt * 128:(kt + 1) * 128, :]), "wb_s")
        for kt in range(8):
            add_cast(w_o[kt * 128:(kt + 1) * 128, :], 1024, (lambda kt=kt: wo_s[kt * 128:(kt + 1) * 128, :]), "wo_s")
        for ft in range(NFT):
            add_cast(wd[ft * 128:(ft + 1) * 128, :], 1024, (lambda ft=ft: wd_s[ft * 128:(ft + 1) * 128, :]), "wd_s")
        for (wsrc, wdst, key) in ((wg, wg_s, "wg_s"), (wu, wu_s, "wu_s")):
            for kt in range(8):
                for h in range(2):
                    add_cast(wsrc[kt * 128:(kt + 1) * 128, h * 1408:(h + 1) * 1408], 1408,
                             (lambda kt=kt, h=h, wdst=wdst: wdst[h * 11:(h + 1) * 11, :, kt, :].rearrange("ft p f -> p ft f")),
                             key)

        pace = sbt(top, "pace", [128, 16], F32)
        gated = [None]
        pace_ctr = [0]

        def cast_step(n, paced=False):
            if paced and p.enabled and cast_pos[0] < len(cast_jobs):
                pc = pace_ctr[0]
                pace_ctr[0] += 1
                gated[0] = f"pace#{pc}"
                p.op("dve", lambda e, pc=pc: e.memset(pace[:, pc % 16:pc % 16 + 1], 0.0), writes=[gated[0]])
            for _ in range(n):
                j = cast_pos[0]
                if j >= len(cast_jobs):
                    return
                cast_pos[0] += 1
                src, ncols, dst_fn, key = cast_jobs[j]
                r = j % 2
                p.dma(lambda e, src=src, r=r, ncols=ncols: e.dma_start(out=cst_in[r][:, 0:ncols], in_=src),
                      f"ld_cast{r}", reads=([gated[0]] if gated[0] else []), writes=[f"cst_in{r}"], q="pool")
                p.op("pool", lambda e, r=r, ncols=ncols: e.tensor_copy(out=cst_out[r][:, 0:ncols], in_=cst_in[r][:, 0:ncols]),
                     reads=[f"cst_in{r}"], writes=[f"cst_out{r}"])
                dst = dst_fn()
                if len(dst.shape) == 3:
                    srcv = cst_out[r][:, 0:ncols].rearrange("p (ft f) -> p ft f", f=128)
                else:
                    srcv = cst_out[r][:, 0:ncols]
                p.dma(lambda e, dst=dst, srcv=srcv: e.dma_start(out=dst, in_=srcv), f"st_cast{r}",
                      reads=[f"cst_out{r}"], writes=[f"{key}#{j}"], q="pool")
                cast_done.setdefault(key, []).append(f"{key}#{j}")

        def cast_until(key_names):
            need = [i for i, jb in enumerate(cast_jobs) if jb[3] in key_names]
            if need:
                last = max(need)
                if cast_pos[0] <= last:
                    cast_step(last + 1 - cast_pos[0])
            ks = []
            for k in key_names:
                ks += cast_done.get(k, [])
            return ks

        def load_w(tile, src_ap, key, deps, semname):
            p.dma(lambda e: e.dma_start(out=tile, in_=src_ap), semname, reads=deps, writes=[key])

        cT = sbt(top, "cT", [128, 8], F32)
        condf = sbt(top, "condf", [128, 8], F32)
        cond_rep = sbt(top, "cond_rep", [128, 8, 128], F32)
        b_adaT = sbt(top, "b_adaT", [128, 48], F32)
        n1gT = sbt(top, "n1gT", [128, 8], F32)
        n2gT = sbt(top, "n2gT", [128, 8], F32)
        ones_row = sbt(top, "ones_row", [1, 128], F32)
        cload(cT[:], cT_d, "cT")
        cload(b_adaT[:], b_adaT_d, "b_adaT")
        cload(n1gT[:], n1gT_d, "n1gT")
        cload(n2gT[:], n2gT_d, "n2gT")
        cload(ones_row[:], ones_row_d, "ones_row")
        p.op("act", lambda e: e.activation(out=condf[:], in_=cT[:], func=AF.Silu), reads=["cT"], writes=["condf"])
        p.op("dve", lambda e: e.tensor_copy(out=cond_rep[:], in_=bc_last(condf[:], 128)), reads=["condf"], writes=["cond_rep"])
        w_ada_v = w_ada.rearrange("(kt p) c -> p kt c", p=128)
        slot_of = {0: 0, 1: 1, 3: 2, 4: 3}

        def ada_half(grp, h, wt, wtk, ps, pk, brow, browk):
            if grp in slot_of:
                sl = slot_of[grp]
                for ct in range(4):
                    for kt in range(8):
                        p.op("pe", lambda e, ct=ct, kt=kt: e.matmul(
                            ps[:, 2 * ct:2 * ct + 2], lhsT=wt[:, kt, ct * 128:(ct + 1) * 128],
                            rhs=cond_rep[:, kt, 0:2], start=(kt == 0), stop=(kt == 7)),
                             reads=[wtk, "cond_rep"], writes=[pk])
                for ct in range(4):
                    col = grp * 8 + h * 4 + ct
                    p.op("act", lambda e, ct=ct, col=col: e.activation(
                        out=modT[:, sl, h * 4 + ct:h * 4 + ct + 1], in_=ps[:, 2 * ct:2 * ct + 1], func=AF.Identity, bias=b_adaT[:, col:col + 1]),
                         reads=[pk, "b_adaT"], writes=[f"modT{sl}_{h}" if ct == 3 else newkey("modT")])
            else:
                dst = g1bc if grp == 2 else g2bc
                dk = "g1bc" if grp == 2 else "g2bc"
                for kt in range(8):
                    p.op("pe", lambda e, kt=kt: e.matmul(
                        ps[:, :], lhsT=cond_rep[:, kt, :], rhs=wt[:, kt, :],
                        start=(kt == 0), stop=False), reads=[wtk, "cond_rep"], writes=[pk])
                p.op("pe", lambda e: e.matmul(
                    ps[:, :], lhsT=ones_row[0:1, :], rhs=brow[0:1, :],
                    start=False, stop=True), reads=["ones_row", browk], writes=[pk])
                p.op("act", lambda e: e.activation(out=dst[:, h * 512:(h + 1) * 512], in_=ps[:, :], func=AF.Copy),
                     reads=[pk], writes=[f"{dk}_{h}"])

        def ada_load(grp, h, wt, wtk, sem, brow=None, browk=None, bsem=None):
            p.dma(lambda e: e.dma_start(out=wt[:], in_=w_ada_v[:, :, grp * 1024 + h * 512: grp * 1024 + (h + 1) * 512]),
                  sem, writes=[wtk])
            if grp not in slot_of:
                p.dma(lambda e: e.dma_start(out=brow[0:1, :], in_=b_ada_row[0:1, grp * 1024 + h * 512: grp * 1024 + (h + 1) * 512]),
                      bsem, writes=[browk])

        def normA(xtile, xn4, xnk, extra_reads=(), scale_eng="dve"):
            for tt in range(4):
                xa, xk = xtile(tt)
                ss, ssk = newvec()
                sd, sdk = newvec()
                rs, rsk = newvec()
                p.op("act", lambda e, xa=xa, ss=ss: e.activation(out=junk[:], in_=xa, func=AF.Square, accum_out=ss),
                     reads=[xk] + list(extra_reads), writes=[ssk, "junk"])
                p.op("act", lambda e, ss=ss, sd=sd: e.activation(out=sd, in_=ss, func=AF.Sqrt, scale=1.0 / D, bias=epsv[:, 0:1]),
                     reads=[ssk, "epsv"], writes=[sdk])
                p.op("dve", lambda e, sd=sd, rs=rs: e.reciprocal(out=rs, in_=sd), reads=[sdk], writes=[rsk])
                if scale_eng == "act":
                    p.op("act", lambda e, tt=tt, xa=xa, rs=rs: e.activation(out=xn4[:, tt, :], in_=xa, func=AF.Identity, scale=rs),
                         reads=[xk, rsk], writes=[f"{xnk}_{tt}"])
                else:
                    p.op("dve", lambda e, tt=tt, xa=xa, rs=rs: e.tensor_scalar(out=xn4[:, tt, :], in0=xa, scalar1=rs, scalar2=None, op0=ALU.mult),
                         reads=[xk, rsk], writes=[f"{xnk}_{tt}"])

        def normB(ctx, xn4, xnk, hT, hTk, geff, geffk, sh, shk, col0=0, perm_half=None):
            psb = ctx["psb"]
            for kp in range(4):
                b = ctx["psb_ctr"][0] % len(psb)
                ctx["psb_ctr"][0] += 1
                pb = psb[b]; pbk = f"{ctx['name']}psb{b}"
                for kk in range(2):
                    kt = 2 * kp + kk
                    for tt in range(4):
                        p.op("pe", lambda e, pb=pb, kk=kk, tt=tt, kt=kt: e.transpose(
                            pb[:, kk * 512 + tt * 128: kk * 512 + (tt + 1) * 128], xn4[:, tt, kt * 128:(kt + 1) * 128], identb[:]),
                             reads=[f"{xnk}_{tt}", "identb"], writes=[pbk])
                for kk in range(2):
                    kt = 2 * kp + kk
                    if perm_half is None:
                        oap = hT[:, kt, col0:col0 + 512]
                        iap = pb[:, kk * 512:(kk + 1) * 512]
                    else:
                        oap = hT[:, kt, :].rearrange("p (i n) -> p i n", i=8)[:, :, 64 * perm_half:64 * perm_half + 64]
                        iap = pb[:, kk * 512:(kk + 1) * 512].rearrange("p (n i) -> p i n", i=8)
                    if kp % 2 == 0:
                        p.op("dve", lambda e, oap=oap, iap=iap, kt=kt: e.tensor_scalar(
                            out=oap, in0=iap,
                            scalar1=geff[:, kt:kt + 1], scalar2=sh[:, kt:kt + 1], op0=ALU.mult, op1=ALU.add),
                             reads=[pbk, geffk, shk], writes=[f"{hTk}_{kt}"])
                    else:
                        p.op("act", lambda e, oap=oap, iap=iap, kt=kt: e.activation(
                            out=oap, in_=iap, func=AF.Identity,
                            scale=geff[:, kt:kt + 1], bias=sh[:, kt:kt + 1]),
                             reads=[pbk, geffk, shk], writes=[f"{hTk}_{kt}"])

        xring_ctr = [0]

        def make_xloader(xt_tiles, prefix, rows_ap):
            cache = {}

            def xtile(tt):
                if tt not in cache:
                    i = xring_ctr[0] % len(xt_tiles)
                    xring_ctr[0] += 1
                    key = f"{prefix}{i}"
                    p.dma(lambda e, i=i, tt=tt: e.dma_start(out=xt_tiles[i][:], in_=rows_ap[tt * 128:(tt + 1) * 128, :]),
                          f"ld_x{i}", writes=[key])
                    cache[tt] = (xt_tiles[i][:], key)
                return cache[tt]

            return xtile

        s5o = ExitStack()
        Mbf = sbt(s5o, "Mbf", [128, G, 128], BF16)
        Gre = sbt(s5o, "Gre", [128, G, 64], BF16)
        Gim = sbt(s5o, "Gim", [128, G, 64], BF16)
        Hre = sbt(s5o, "Hre", [128, 16, 128], BF16)
        Hni = sbt(s5o, "Hni", [128, 16, 128], BF16)
        dec8 = sbt(s5o, "dec8", [128, 16], F32)
        c8 = sbt(s5o, "c8", [128, 16], F32)
        s8 = sbt(s5o, "s8", [128, 16], F32)
        lamre = sbt(s5o, "lamre", [128, 16], F32)
        lamim = sbt(s5o, "lamim", [128, 16], F32)
        logdt = sbt(s5o, "logdt", [128, 16], F32)
        bre = sbt(s5o, "bre", [128, 16, 16], F32)
        bim = sbt(s5o, "bim", [128, 16, 16], F32)
        cre = sbt(s5o, "cre", [128, 16, 16], F32)
        cim = sbt(s5o, "cim", [128, 16, 16], F32)
        dvec = sbt(s5o, "dvec", [128, 32], F32)
        cmask = sbt(s5o, "cmask", [128, 128], F32)
        kv = sbt(s5o, "kv", [128, 32], F32)
        hmask = sbt(s5o, "hmask", [128, 2], F32)
        cload(hmask[:], hmask_d, "hmask")
        for t_, d_, k_ in ((lamre, lamre_d, "lamre"), (lamim, lamim_d, "lamim"), (logdt, logdt_d, "logdt"),
                           (bre, bre_d, "bre"), (bim, bim_d, "bim"), (cre, cre_d, "cre"), (cim, cim_d, "cim"),
                           (dvec, dvec_d, "dvec"), (cmask, cmask_d, "cmask"), (kv, kv_d, "kv")):
            cload(t_[:], d_, k_)
        p0s = ExitStack()
        if True:
            wst = [sbt(p0s, f"wst{i}", [128, 8, 512], F32) for i in range(2)]
            psm = [pst(p0s, f"psm{i}", [128, 512]) for i in range(4)]
            halves1 = ((1, 0), (1, 1), (0, 0), (0, 1), (2, 0), (2, 1), (4, 0), (4, 1), (3, 0), (3, 1), (5, 0), (5, 1))
            brow = [sbt(p0s, f"brow{i}", [1, 512], F32) for i in range(2)]
            for gi in range(2):
                ada_load(halves1[gi][0], halves1[gi][1], wst[gi], f"wst{gi}", f"ld_w{gi}", brow[gi], f"brow{gi}", f"ld_w{4 + gi}")

        def emit_adaln():
            for gi in range(12):
                ada_half(halves1[gi][0], halves1[gi][1], wst[gi % 2], f"wst{gi % 2}", psm[gi % 4], f"psm{gi % 4}", brow[gi % 2], f"brow{gi % 2}")
                if gi + 2 < 12:
                    ada_load(halves1[gi + 2][0], halves1[gi + 2][1], wst[gi % 2], f"wst{gi % 2}", f"ld_w{gi % 2}", brow[gi % 2], f"brow{gi % 2}", f"ld_w{4 + gi % 2}")
                if gi == 3:
                    p.op("dve", lambda e: e.scalar_tensor_tensor(out=geff1[:], in0=modT[:, 1, :], scalar=1.0, in1=n1gT[:], op0=ALU.add, op1=ALU.mult),
                         reads=["modT1_0", "modT1_1", "n1gT"], writes=["geff1"])
            p.op("dve", lambda e: e.scalar_tensor_tensor(out=geff2[:], in0=modT[:, 3, :], scalar=1.0, in1=n2gT[:], op0=ALU.add, op1=ALU.mult),
                 reads=["modT3_0", "modT3_1", "n2gT"], writes=["geff2"])

        cast_step(20)
        with ExitStack() as st:

            def small(name, shape=(128, 16)):
                return sbt(st, "s_" + name, list(shape), F32)

            dt_ = small("dt"); a_ = small("a"); phi = small("phi")
            p.op("act", lambda e: e.activation(out=dt_[:], in_=logdt[:], func=AF.Exp), reads=["logdt"], writes=["dt"])
            p.op("dve", lambda e: e.tensor_tensor(out=a_[:], in0=lamre[:], in1=dt_[:], op=ALU.mult), reads=["lamre", "dt"], writes=["a"])
            p.op("dve", lambda e: e.tensor_tensor(out=phi[:], in0=lamim[:], in1=dt_[:], op=ALU.mult), reads=["lamim", "dt"], writes=["phi"])
            AR = small("AR", (128, 16, 32)); ANG = small("ANG", (128, 16, 32)); RHO = small("RHO", (128, 16, 32))
            SINt = small("SINt", (128, 16, 32)); COSt = small("COSt", (128, 16, 32))
            PRE = small("PRE", (128, 16, 32)); PIM = small("PIM", (128, 16, 32))
            tA = small("tA", (128, 16, 32)); tB = small("tB", (128, 16, 32))
            kvb = kv[:].rearrange("p (o k) -> p o k", o=1).to_broadcast([128, 16, 32])
            p.op("dve", lambda e: e.tensor_tensor(out=AR[:], in0=bc_last(a_[:], 32), in1=kvb, op=ALU.mult), reads=["a", "kv"], writes=["AR"])
            p.op("dve", lambda e: e.tensor_tensor(out=ANG[:], in0=bc_last(phi[:], 32), in1=kvb, op=ALU.mult), reads=["phi", "kv"], writes=["ANG"])
            p.op("act", lambda e: e.activation(out=RHO[:], in_=AR[:], func=AF.Exp), reads=["AR"], writes=["RHO"])

            def range_reduce_sin(src, srck, dst, dstk, shift):
                p.op("dve", lambda e: e.tensor_scalar(out=tA[:], in0=src[:], scalar1=shift, scalar2=None, op0=ALU.add),
                     reads=[srck], writes=["tA"])
                p.op("dve", lambda e: e.tensor_scalar(out=tB[:], in0=tA[:], scalar1=1.0 / (2 * math.pi), scalar2=MAGIC, op0=ALU.mult, op1=ALU.add),
                     reads=["tA"], writes=["tB"])
                p.op("dve", lambda e: e.tensor_scalar(out=tB[:], in0=tB[:], scalar1=MAGIC, scalar2=None, op0=ALU.subtract),
                     reads=["tB"], writes=["tB"])
                p.op("dve", lambda e: e.scalar_tensor_tensor(out=tA[:], in0=tB[:], scalar=-C1, in1=tA[:], op0=ALU.mult, op1=ALU.add),
                     reads=["tB", "tA"], writes=["tA"])
                p.op("dve", lambda e: e.scalar_tensor_tensor(out=tA[:], in0=tB[:], scalar=-C2, in1=tA[:], op0=ALU.mult, op1=ALU.add),
                     reads=["tB", "tA"], writes=["tA"])
                p.op("dve", lambda e: e.tensor_scalar(out=tA[:], in0=tA[:], scalar1=-PI_LO, scalar2=PI_LO, op0=ALU.max, op1=ALU.min),
                     reads=["tA"], writes=["tA"])
                p.op("act", lambda e: e.activation(out=dst[:], in_=tA[:], func=AF.Sin), reads=["tA"], writes=[dstk])

            range_reduce_sin(ANG, "ANG", SINt, "SINt", 0.0)
            range_reduce_sin(ANG, "ANG", COSt, "COSt", math.pi / 2)
            p.op("dve", lambda e: e.tensor_tensor(out=PRE[:], in0=RHO[:], in1=COSt[:], op=ALU.mult), reads=["RHO", "COSt"], writes=["PRE"])
            p.op("dve", lambda e: e.tensor_tensor(out=PIM[:], in0=RHO[:], in1=SINt[:], op=ALU.mult), reads=["RHO", "SINt"], writes=["PIM"])
            ckpt()
            nr = small("nr"); den = small("den"); t1s = small("t1s"); t2s = small("t2s")
            bre_s = small("betare"); bim_s = small("betaim")
            lbre = PRE[:, :, 24]; lbim = PIM[:, :, 24]
            p.op("dve", lambda e: e.tensor_scalar(out=nr[:], in0=lbre, scalar1=-1.0, scalar2=None, op0=ALU.add), reads=["PRE"], writes=["nr"])
            p.op("dve", lambda e: e.tensor_tensor(out=den[:], in0=lamre[:], in1=lamre[:], op=ALU.mult), reads=["lamre"], writes=["den"])
            p.op("dve", lambda e: e.tensor_tensor(out=t1s[:], in0=lamim[:], in1=lamim[:], op=ALU.mult), reads=["lamim"], writes=["t1s"])
            p.op("dve", lambda e: e.tensor_tensor(out=den[:], in0=den[:], in1=t1s[:], op=ALU.add), reads=["den", "t1s"], writes=["den"])
            p.op("dve", lambda e: e.reciprocal(out=den[:], in_=den[:]), reads=["den"], writes=["den"])
            p.op("dve", lambda e: e.tensor_tensor(out=t1s[:], in0=nr[:], in1=lamre[:], op=ALU.mult), reads=["nr", "lamre"], writes=["t1s"])
            p.op("dve", lambda e: e.tensor_tensor(out=t2s[:], in0=lbim, in1=lamim[:], op=ALU.mult), reads=["PIM", "lamim"], writes=["t2s"])
            p.op("dve", lambda e: e.tensor_tensor(out=t1s[:], in0=t1s[:], in1=t2s[:], op=ALU.add), reads=["t1s", "t2s"], writes=["t1s"])
            p.op("dve", lambda e: e.tensor_tensor(out=bre_s[:], in0=t1s[:], in1=den[:], op=ALU.mult), reads=["t1s", "den"], writes=["betare"])
            p.op("dve", lambda e: e.tensor_tensor(out=t1s[:], in0=lbim, in1=lamre[:], op=ALU.mult), reads=["PIM", "lamre"], writes=["t1s"])
            p.op("dve", lambda e: e.tensor_tensor(out=t2s[:], in0=nr[:], in1=lamim[:], op=ALU.mult), reads=["nr", "lamim"], writes=["t2s"])
            p.op("dve", lambda e: e.tensor_tensor(out=t1s[:], in0=t1s[:], in1=t2s[:], op=ALU.subtract), reads=["t1s", "t2s"], writes=["t1s"])
            p.op("dve", lambda e: e.tensor_tensor(out=bim_s[:], in0=t1s[:], in1=den[:], op=ALU.mult), reads=["t1s", "den"], writes=["betaim"])
            big1 = sbt(st, "big1", [128, 16, 8, 16], F32)
            big2 = sbt(st, "big2", [128, 16, 8, 16], F32)

            def cmul(outre, outrek, outim, outimk, are, arek, aim, aimk, bre_, brek, bim_, bimk, shape, neg_im=False, eng="dve"):
                n = 1
                for s_ in shape[1:]:
                    n *= s_
                if len(shape) == 3:
                    t1 = big1[:].rearrange("p a b c -> p (a b c)")[:, 0:n].rearrange("p (a b) -> p a b", b=shape[2])
                    t2 = big2[:].rearrange("p a b c -> p (a b c)")[:, 0:n].rearrange("p (a b) -> p a b", b=shape[2])
                else:
                    t1 = big1[:]
                    t2 = big2[:]
                p.op(eng, lambda e: e.tensor_tensor(out=t1, in0=are, in1=bre_, op=ALU.mult), reads=[arek, brek], writes=["big1"])
                p.op(eng, lambda e: e.tensor_tensor(out=t2, in0=aim, in1=bim_, op=ALU.mult), reads=[aimk, bimk], writes=["big2"])
                p.op(eng, lambda e: e.tensor_tensor(out=outre, in0=t1, in1=t2, op=ALU.subtract), reads=["big1", "big2"], writes=[outrek])
                p.op(eng, lambda e: e.tensor_tensor(out=t1, in0=are, in1=bim_, op=ALU.mult), reads=[arek, bimk], writes=["big1"])
                p.op(eng, lambda e: e.tensor_tensor(out=t2, in0=aim, in1=bre_, op=ALU.mult), reads=[aimk, brek], writes=["big2"])
                if neg_im:
                    p.op(eng, lambda e: e.scalar_tensor_tensor(out=outim, in0=t1, scalar=-1.0, in1=t2, op0=ALU.mult, op1=ALU.subtract),
                         reads=["big1", "big2"], writes=[outimk])
                else:
                    p.op(eng, lambda e: e.tensor_tensor(out=outim, in0=t1, in1=t2, op=ALU.add), reads=["big1", "big2"], writes=[outimk])

            Bre = small("Bre", (128, 16, 16)); Bim = small("Bim", (128, 16, 16))
            cmul(Bre[:], "Bre", Bim[:], "Bim", bc_last(bre_s[:], 16), "betare", bc_last(bim_s[:], 16), "betaim",
                 bre[:], "bre", bim[:], "bim", (128, 16, 16))
            ckpt()
            Xre = sbt(st, "Xre", [128, 16, 8, 16], F32); Xim = sbt(st, "Xim", [128, 16, 8, 16], F32)
            XGre = sbt(st, "XGre", [128, 16, 8, 16], F32); XGim = sbt(st, "XGim", [128, 16, 8, 16], F32)
            Yre = sbt(st, "Yre", [128, 16, 8, 16], F32); nYim = sbt(st, "nYim", [128, 16, 8, 16], F32)
            sh4 = (128, 16, 8, 16)

            def pw(tab, lo):
                return bc_last(tab[:, :, lo:lo + 8], 16)

            def mid(t):
                return t[:].rearrange("p q (o c) -> p q o c", o=1).to_broadcast([128, 16, 8, 16])

            cmul(Xre[:], "Xre", Xim[:], "Xim", pw(PRE, 0), "PRE", pw(PIM, 0), "PIM", mid(Bre), "Bre", mid(Bim), "Bim", sh4)
            cmul(XGre[:], "XGre", XGim[:], "XGim", pw(PRE, 8), "PRE", pw(PIM, 8), "PIM", mid(Bre), "Bre", mid(Bim), "Bim", sh4)
            cmul(Yre[:], "Yre", nYim[:], "nYim", pw(PRE, 16), "PRE", pw(PIM, 16), "PIM", mid(cre), "cre", mid(cim), "cim", sh4, neg_im=True)
            Hre4 = Hre[:].rearrange("p q (j o) -> p q j o", o=16)
            Hni4 = Hni[:].rearrange("p q (j o) -> p q j o", o=16)
            cmul(Hre4, "Hre", Hni4, "Hni", pw(PRE, 24), "PRE", pw(PIM, 24), "PIM", mid(cre), "cre", mid(cim), "cim", sh4, neg_im=True)
            emit_adaln()
            psS = [pst(st, f"psS{i}", [128, 512]) for i in range(4)]
            pctr = 0
            b1f = big1[:].rearrange("p a b c -> p (a b c)")
            Ym = sbt(st, "Ym", [128, 16, 8, 16], F32)
            nYm = sbt(st, "nYm", [128, 16, 8, 16], F32)
            for hf in range(2):
                p.op("dve", lambda e, hf=hf: e.tensor_scalar(out=Ym[:], in0=Yre[:], scalar1=hmask[:, hf:hf + 1], scalar2=None, op0=ALU.mult),
                     reads=["Yre", "hmask"], writes=["Ym"])
                p.op("dve", lambda e, hf=hf: e.tensor_scalar(out=nYm[:], in0=nYim[:], scalar1=hmask[:, hf:hf + 1], scalar2=None, op0=ALU.mult),
                     reads=["nYim", "hmask"], writes=["nYm"])
                for q4 in range(4):
                    ps = psS[pctr % 4]; pk = f"psS{pctr % 4}"; pctr += 1
                    for gg in range(4):
                        q = q4 * 4 + gg
                        p.op("pe", lambda e, ps=ps, gg=gg, q=q: e.matmul(
                            ps[:, gg * 128:(gg + 1) * 128], lhsT=Xre[:, q, :, :].rearrange("p i c -> p (i c)"),
                            rhs=Ym[:, q, :, :].rearrange("p j o -> p (j o)"), start=True, stop=False),
                             reads=["Xre", "Ym"], writes=[pk])
                        p.op("pe", lambda e, ps=ps, gg=gg, q=q: e.matmul(
                            ps[:, gg * 128:(gg + 1) * 128], lhsT=Xim[:, q, :, :].rearrange("p i c -> p (i c)"),
                            rhs=nYm[:, q, :, :].rearrange("p j o -> p (j o)"), start=False, stop=True),
                             reads=["Xim", "nYm"], writes=[pk])
                    for gg in range(4):
                        q = q4 * 4 + gg
                        g = 2 * q + hf
                        p.op("dve", lambda e, ps=ps, gg=gg: e.tensor_tensor(
                            out=b1f[:, gg * 128:(gg + 1) * 128], in0=ps[:, gg * 128:(gg + 1) * 128], in1=cmask[:], op=ALU.mult),
                             reads=[pk, "cmask"], writes=["big1"])
                        p.op("dve", lambda e, g=g, gg=gg: e.scalar_tensor_tensor(
                            out=Mbf[:, g, :], in0=identf[:], scalar=dvec[:, g:g + 1],
                            in1=b1f[:, gg * 128:(gg + 1) * 128], op0=ALU.mult, op1=ALU.add),
                             reads=["identf", "dvec", "big1"], writes=["Mbf"])
            ckpt()
            for (src, srck, dstt, dstk) in ((XGre, "XGre", Gre, "Gre"), (XGim, "XGim", Gim, "Gim")):
                for q8 in range(4):
                    ps = psS[pctr % 4]; pk = f"psS{pctr % 4}"; pctr += 1
                    for qq in range(4):
                        q = q8 * 4 + qq
                        p.op("pe", lambda e, ps=ps, qq=qq, q=q, src=src: e.transpose(
                            ps[:, qq * 128:(qq + 1) * 128], src[:, q, :, :].rearrange("p i c -> p (i c)"), identf[:]),
                             reads=[srck, "identf"], writes=[pk])
                    p.op("act", lambda e, ps=ps, q8=q8, dstt=dstt: e.activation(
                        out=dstt[:, q8 * 8:(q8 + 1) * 8, :].rearrange("p g c -> p (g c)"), in_=ps[:, :], func=AF.Copy),
                         reads=[pk], writes=[dstk])
            ckpt()
            p.op("dve", lambda e: e.tensor_copy(out=dec8[:], in_=RHO[:, :, 31]), reads=["RHO"], writes=["dec8"])
            p.op("dve", lambda e: e.tensor_copy(out=c8[:], in_=COSt[:, :, 31]), reads=["COSt"], writes=["c8"])
            p.op("dve", lambda e: e.tensor_copy(out=s8[:], in_=SINt[:, :, 31]), reads=["SINt"], writes=["s8"])
            phase_end([("PRE", PRE[:].rearrange("p a b -> p (a b)"), [128, 512], F32),
                       ("PIM", PIM[:].rearrange("p a b -> p (a b)"), [128, 512], F32),
                       ("Mbf", Mbf[:].rearrange("p g c -> p (g c)"), [128, G * 128], BF16),
                       ("Gre", Gre[:].rearrange("p g c -> p (g c)"), [128, G * 64], BF16),
                       ("Hre", Hre[:].rearrange("p g c -> p (g c)"), [128, 16 * 128], BF16)])
        p0s.close()

        with ExitStack() as s5:
            Ytm = sbt(s5, "Ytm", [128, NB, 8, 512], BF16)
            with ExitStack() as s5a:
                U = sbt(s5a, "U", [128, NB, G, 8, 16], BF16)
                with ExitStack() as st:
                    ctx = {"name": "p1a", "psb_ctr": [0],
                           "psb": [pst(st, f"p1a_psb{i}", [128, 1024], BF16) for i in range(2)]}
                    psf = [pst(st, f"p1a_psf{i}", [128, 512]) for i in range(6)]
                    xt = [sbt(st, f"p1a_xt{i}", [128, D], F32) for i in range(8)]
                    xn4 = [sbt(st, f"p1a_xn{i}", [128, 4, D], BF16) for i in range(2)]
                    hT = [sbt(st, f"p1a_hT{i}", [128, 8, 1024], BF16) for i in range(1)]
                    wU = sbt(st, "wU", [128, 8, 512], BF16)
                    wUs = sbt(st, "wUs", [128, 4, 512], F32)
                    for hh in range(2):
                        p.dma(lambda e, hh=hh: e.dma_start(out=wUs[:], in_=w_in.rearrange("(kt p) c -> p kt c", p=128)[:, hh * 4:(hh + 1) * 4, 0:512]),
                              "ld_w0", writes=["wUs"])
                        p.op("act", lambda e, hh=hh: e.activation(out=wU[:, hh * 4:(hh + 1) * 4, :].rearrange("p a b -> p (a b)"),
                                                                  in_=wUs[:].rearrange("p a b -> p (a b)"), func=AF.Copy),
                             reads=["wUs"], writes=[f"wU{hh}"])
                    pf = 0
                    normA(make_xloader(xt, "p1a_xt", x_all[0:512, :]), xn4[0], "p1a_xn0")
                    ckpt()
                    for c in range(8):
                        nbk = c // 2
                        r = c % 2
                        hb = hT[0]; hbk = f"p1a_hT0_{c % 2}"
                        normB(ctx, xn4[r], f"p1a_xn{r}", hb, hbk, geff1, "geff1", sh1, "modT0_1", perm_half=(c % 2))
                        if c == 0:
                            ckpt()
                        if c + 1 < 8:
                            r2 = (c + 1) % 2
                            normA(make_xloader(xt, "p1a_xt", x_all[(c + 1) * 512:(c + 2) * 512, :]), xn4[r2], f"p1a_xn{r2}")
                        if c % 2 == 1:
                            hk0 = "p1a_hT0_0"; hk1 = "p1a_hT0_1"
                            for i in range(8):
                                ps = psf[pf % 6]; pk = f"p1a_psf{pf % 6}"; pf += 1
                                for kt in range(8):
                                    p.op("pe", lambda e, ps=ps, hb=hb, kt=kt, i=i: e.matmul(
                                        ps[:, :], lhsT=hb[:, kt, i * 128:(i + 1) * 128], rhs=wU[:, kt, :], start=(kt == 0), stop=(kt == 7)),
                                         reads=[f"{hk0}_{kt}", f"{hk1}_{kt}", "wU0", "wU1"], writes=[pk])
                                if i % 2 == 0:
                                    p.op("act", lambda e, ps=ps, nbk=nbk, i=i: e.activation(
                                        out=U[:, nbk, :, i, :], in_=ps[:, :].rearrange("p (g c) -> p g c", c=16), func=AF.Copy),
                                         reads=[pk], writes=[newkey("U")])
                                else:
                                    p.op("dve", lambda e, ps=ps, nbk=nbk, i=i: e.tensor_copy(
                                        out=U[:, nbk, :, i, :], in_=ps[:, :].rearrange("p (g c) -> p g c", c=16)),
                                         reads=[pk], writes=[newkey("U")])

                    phase_end([("U", U[:].rearrange("p a g i c -> p (a g i c)"), [128, NB * G * 128], BF16)])

                CS = sbt(s5a, "CS", [128, 8, 512], F32)
                SN = sbt(s5a, "SN", [128, 8, 512], F32)
                with ExitStack() as st:
                    Ug = [[sbt(st, f"Ug{r}_{h}", [128, 512], BF16) for h in range(2)] for r in range(2)]
                    psb = [pst(st, f"p2_psb{i}", [128, 1024], BF16) for i in range(2)]
                    psZ = [pst(st, f"p2_psZ{i}", [128, 512]) for i in range(4)]
                    psY = [pst(st, f"p2_psY{i}", [128, 512]) for i in range(2)]
                    tmp = [sbt(st, f"p2_tmp{i}", [128, 512], F32) for i in range(4)]
                    Win = [sbt(st, f"Win{c}", [128, 512], F32) for c in range(2)]
                    Wst = [sbt(st, f"Wst{c}", [128, 512], F32) for c in range(2)]
                    Sp = [[sbt(st, f"Sp{r}_{c}", [128, 512], BF16) for c in range(2)] for r in range(2)]
                    dt1 = sbt(st, "dt1", [128, 8, 256], F32)
                    dt2 = sbt(st, "dt2", [128, 8, 256], F32)
                    cmt = [sbt(st, f"cmt{i}", [128, 8], F32) for i in range(2)]
                    smt = [sbt(st, f"smt{i}", [128, 8], F32) for i in range(2)]
                    sq1 = sbt(st, "sq1", [128, 8], F32)
                    sq2 = sbt(st, "sq2", [128, 8], F32)
                    for r in range(2):
                        for c in range(2):
                            p.op("dve", lambda e, r=r, c=c: e.memset(Sp[r][c][:, 0:1], 0.0), writes=[f"Sp{r}_{c}"])
                    yctr = 0
                    for half in range(2):
                        q0 = half * 8
                        p.op("dve", lambda e, q0=q0: e.tensor_copy(out=cmt[0][:], in_=c8[:, q0:q0 + 8]), reads=["c8"], writes=["cmt0"])
                        p.op("dve", lambda e, q0=q0: e.tensor_copy(out=smt[0][:], in_=s8[:, q0:q0 + 8]), reads=["s8"], writes=["smt0"])
                        p.op("dve", lambda e: e.memset(CS[:, :, 0:1], 1.0), writes=["CS"] + [f"CS{qq}" for qq in range(8)])
                        p.op("dve", lambda e: e.memset(SN[:, :, 0:1], 0.0), writes=["SN"] + [f"SN{qq}" for qq in range(8)])
                        m = 1
                        it = 0
                        while m < 512:
                            c_, s_ = cmt[it % 2], smt[it % 2]
                            ck, sk = f"cmt{it % 2}", f"smt{it % 2}"
                            if m < 32:
                                t1 = dt1[:, :, 0:m]
                                t2 = dt2[:, :, 0:m]
                                cb = bc_last(c_[:], m); sb_ = bc_last(s_[:], m)
                                p.op("dve", lambda e, t1=t1, cb=cb, m=m: e.tensor_tensor(out=t1, in0=CS[:, :, 0:m], in1=cb, op=ALU.mult), reads=["CS", ck], writes=["dt1"])
                                p.op("dve", lambda e, t2=t2, sb_=sb_, m=m: e.tensor_tensor(out=t2, in0=SN[:, :, 0:m], in1=sb_, op=ALU.mult), reads=["SN", sk], writes=["dt2"])
                                p.op("dve", lambda e, t1=t1, t2=t2, m=m: e.tensor_tensor(out=CS[:, :, m:2 * m], in0=t1, in1=t2, op=ALU.subtract), reads=["dt1", "dt2"], writes=["CS"])
                                p.op("dve", lambda e, t1=t1, cb=cb, m=m: e.tensor_tensor(out=t1, in0=SN[:, :, 0:m], in1=cb, op=ALU.mult), reads=["SN", ck], writes=["dt1"])
                                p.op("dve", lambda e, t2=t2, sb_=sb_, m=m: e.tensor_tensor(out=t2, in0=CS[:, :, 0:m], in1=sb_, op=ALU.mult), reads=["CS", sk], writes=["dt2"])
                                p.op("dve", lambda e, t1=t1, t2=t2, m=m: e.tensor_tensor(out=SN[:, :, m:2 * m], in0=t1, in1=t2, op=ALU.add), reads=["dt1", "dt2"], writes=["SN"])
                            else:
                                for qq in range(8):
                                    p.op("act", lambda e, qq=qq, m=m, s_=s_: e.activation(out=dt1[:, qq, 0:m], in_=SN[:, qq, 0:m], func=AF.Identity, scale=s_[:, qq:qq + 1]),
                                         reads=["SN", f"SN{qq}", sk, "dt1"], writes=[f"dt1_{qq}"])
                                    p.op("act", lambda e, qq=qq, m=m, s_=s_: e.activation(out=dt2[:, qq, 0:m], in_=CS[:, qq, 0:m], func=AF.Identity, scale=s_[:, qq:qq + 1]),
                                         reads=["CS", f"CS{qq}", sk, "dt2"], writes=[f"dt2_{qq}"])
                                for qq in range(8):
                                    p.op("dve", lambda e, qq=qq, m=m, c_=c_: e.scalar_tensor_tensor(
                                        out=CS[:, qq, m:2 * m], in0=CS[:, qq, 0:m], scalar=c_[:, qq:qq + 1], in1=dt1[:, qq, 0:m], op0=ALU.mult, op1=ALU.subtract),
                                         reads=["CS", f"CS{qq}", ck, f"dt1_{qq}"], writes=[f"CS{qq}"])
                                    p.op("dve", lambda e, qq=qq, m=m, c_=c_: e.scalar_tensor_tensor(
                                        out=SN[:, qq, m:2 * m], in0=SN[:, qq, 0:m], scalar=c_[:, qq:qq + 1], in1=dt2[:, qq, 0:m], op0=ALU.mult, op1=ALU.add),
                                         reads=["SN", f"SN{qq}", ck, f"dt2_{qq}"], writes=[f"SN{qq}"])
                            if 2 * m < 512:
                                cn, sn_ = cmt[(it + 1) % 2], smt[(it + 1) % 2]
                                cnk, snk = f"cmt{(it + 1) % 2}", f"smt{(it + 1) % 2}"
                                p.op("dve", lambda e, c_=c_: e.tensor_tensor(out=sq1[:], in0=c_[:], in1=c_[:], op=ALU.mult), reads=[ck], writes=["sq1"])
                                p.op("dve", lambda e, s_=s_: e.tensor_tensor(out=sq2[:], in0=s_[:], in1=s_[:], op=ALU.mult), reads=[sk], writes=["sq2"])
                                p.op("dve", lambda e, cn=cn: e.tensor_tensor(out=cn[:], in0=sq1[:], in1=sq2[:], op=ALU.subtract), reads=["sq1", "sq2"], writes=[cnk])
                                p.op("dve", lambda e, sn_=sn_, c_=c_, s_=s_: e.scalar_tensor_tensor(out=sn_[:], in0=c_[:], scalar=2.0, in1=s_[:], op0=ALU.mult, op1=ALU.mult),
                                     reads=[ck, sk], writes=[snk])
                            m *= 2
                            it += 1
                        for ql in range(8):
                            q = q0 + ql
                            r = q % 2
                            for hf in range(2):
                                g = 2 * q + hf
                                pb = psb[hf]; pbk = f"p2_psb{hf}"
                                for nbk in range(4):
                                    p.op("pe", lambda e, pb=pb, nbk=nbk, g=g: e.transpose(
                                        pb[:, nbk * 128:(nbk + 1) * 128], U[:, nbk, g, :, :].rearrange("p i c -> p (i c)"), identb[:]),
                                         reads=["identb"], writes=[pbk])
                                if hf == 0:
                                    p.op("act", lambda e, pb=pb, r=r, hf=hf: e.activation(out=Ug[r][hf][:], in_=pb[:, 0:512], func=AF.Copy),
                                         reads=[pbk], writes=[f"Ug{r}_{hf}"])
                                else:
                                    p.op("dve", lambda e, pb=pb, r=r, hf=hf: e.tensor_copy(out=Ug[r][hf][:], in_=pb[:, 0:512]),
                                         reads=[pbk], writes=[f"Ug{r}_{hf}"])
                            zre = psZ[(2 * q) % 4]; zrek = f"p2_psZ{(2 * q) % 4}"
                            zim = psZ[(2 * q + 1) % 4]; zimk = f"p2_psZ{(2 * q + 1) % 4}"
                            for hf in range(2):
                                g = 2 * q + hf
                                lo, hi = hf * 64, hf * 64 + 64
                                p.op("pe", lambda e, zre=zre, g=g, lo=lo, hi=hi, r=r, hf=hf: e.matmul(
                                    zre[lo:hi, :], lhsT=Gre[:, g, :], rhs=Ug[r][hf][:], start=True, stop=True),
                                     reads=["Gre", f"Ug{r}_{hf}"], writes=[zrek])
                                p.op("pe", lambda e, zim=zim, g=g, lo=lo, hi=hi, r=r, hf=hf: e.matmul(
                                    zim[lo:hi, :], lhsT=Gim[:, g, :], rhs=Ug[r][hf][:], start=True, stop=True),
                                     reads=["Gim", f"Ug{r}_{hf}"], writes=[zimk])
                            p.op("dve", lambda e, zre=zre, ql=ql: e.tensor_tensor(out=tmp[0][:], in0=zre[:, :], in1=CS[:, ql, :], op=ALU.mult), reads=[zrek, "CS", f"CS{ql}"], writes=["p2_tmp0"])
                            p.op("dve", lambda e, zim=zim, ql=ql: e.tensor_tensor(out=tmp[1][:], in0=zim[:, :], in1=SN[:, ql, :], op=ALU.mult), reads=[zimk, "SN", f"SN{ql}"], writes=["p2_tmp1"])
                            p.op("dve", lambda e, zim=zim, ql=ql: e.tensor_tensor(out=tmp[2][:], in0=zim[:, :], in1=CS[:, ql, :], op=ALU.mult), reads=[zimk, "CS", f"CS{ql}"], writes=["p2_tmp2"])
                            p.op("dve", lambda e, zre=zre, ql=ql: e.tensor_tensor(out=tmp[3][:], in0=zre[:, :], in1=SN[:, ql, :], op=ALU.mult), reads=[zrek, "SN", f"SN{ql}"], writes=["p2_tmp3"])
                            p.op("dve", lambda e: e.tensor_tensor(out=Win[0][:], in0=tmp[0][:], in1=tmp[1][:], op=ALU.add), reads=["p2_tmp0", "p2_tmp1"], writes=["Win0"])
                            p.op("dve", lambda e: e.tensor_tensor(out=Win[1][:], in0=tmp[2][:], in1=tmp[3][:], op=ALU.subtract), reads=["p2_tmp2", "p2_tmp3"], writes=["Win1"])
                            for c in range(2):
                                p.op("dve", lambda e, c=c, q=q: e.tensor_tensor_scan(
                                    out=Wst[c][:], data0=dec8[:, q:q + 1].to_broadcast([128, 512]), data1=Win[c][:],
                                    initial=0.0, op0=ALU.mult, op1=ALU.add), reads=["dec8", f"Win{c}"], writes=[f"Wst{c}"])
                            p.op("dve", lambda e, ql=ql: e.tensor_tensor(out=tmp[0][:, 0:511], in0=Wst[0][:, 0:511], in1=CS[:, ql, 0:511], op=ALU.mult), reads=["Wst0", "CS", f"CS{ql}"], writes=["p2_tmp0"])
                            p.op("dve", lambda e, ql=ql: e.tensor_tensor(out=tmp[1][:, 0:511], in0=Wst[1][:, 0:511], in1=SN[:, ql, 0:511], op=ALU.mult), reads=["Wst1", "SN", f"SN{ql}"], writes=["p2_tmp1"])
                            p.op("dve", lambda e, ql=ql: e.tensor_tensor(out=tmp[2][:, 0:511], in0=Wst[1][:, 0:511], in1=CS[:, ql, 0:511], op=ALU.mult), reads=["Wst1", "CS", f"CS{ql}"], writes=["p2_tmp2"])
                            p.op("dve", lambda e, ql=ql: e.tensor_tensor(out=tmp[3][:, 0:511], in0=Wst[0][:, 0:511], in1=SN[:, ql, 0:511], op=ALU.mult), reads=["Wst0", "SN", f"SN{ql}"], writes=["p2_tmp3"])
                            p.op("dve", lambda e, r=r: e.tensor_tensor(out=Sp[r][0][:, 1:512], in0=tmp[0][:, 0:511], in1=tmp[1][:, 0:511], op=ALU.subtract), reads=["p2_tmp0", "p2_tmp1"], writes=[f"Sp{r}_0"])
                            p.op("dve", lambda e, r=r: e.tensor_tensor(out=Sp[r][1][:, 1:512], in0=tmp[2][:, 0:511], in1=tmp[3][:, 0:511], op=ALU.add), reads=["p2_tmp2", "p2_tmp3"], writes=[f"Sp{r}_1"])
                            for nbk in range(4):
                                py = psY[yctr % 2]; pyk = f"p2_psY{yctr % 2}"; yctr += 1
                                for hf in range(2):
                                    g = 2 * q + hf
                                    lo, hi = hf * 64, hf * 64 + 64
                                    p.op("pe", lambda e, py=py, hf=hf, r=r, nbk=nbk, g=g: e.matmul(
                                        py[:, hf * 128:(hf + 1) * 128], lhsT=Ug[r][hf][:, nbk * 128:(nbk + 1) * 128], rhs=Mbf[:, g, :],
                                        start=True, stop=False), reads=[f"Ug{r}_{hf}"], writes=[pyk])
                                    p.op("pe", lambda e, py=py, hf=hf, r=r, nbk=nbk, q=q, lo=lo, hi=hi: e.matmul(
                                        py[:, hf * 128:(hf + 1) * 128], lhsT=Sp[r][0][lo:hi, nbk * 128:(nbk + 1) * 128], rhs=Hre[lo:hi, q, :],
                                        start=False, stop=False), reads=[f"Sp{r}_0"], writes=[pyk])
                                    p.op("pe", lambda e, py=py, hf=hf, r=r, nbk=nbk, q=q, lo=lo, hi=hi: e.matmul(
                                        py[:, hf * 128:(hf + 1) * 128], lhsT=Sp[r][1][lo:hi, nbk * 128:(nbk + 1) * 128], rhs=Hni[lo:hi, q, :],
                                        start=False, stop=True), reads=[f"Sp{r}_1"], writes=[pyk])
                                p.op("act", lambda e, py=py, nbk=nbk, q=q: e.activation(
                                    out=Ytm[:, nbk, :, 32 * q:32 * q + 32].rearrange("p j (h o) -> p j h o", h=2),
                                    in_=py[:, 0:256].rearrange("p (h j o) -> p j h o", h=2, j=8), func=AF.Gelu_apprx_tanh),
                                     reads=[pyk], writes=[newkey("Ytm")])
                    phase_end([("Ytm", Ytm[:].rearrange("p a j c -> p (a j c)"), [128, NB * 8 * 512], BF16),
                               ("Sp", Sp[1][0][:], [128, 512], BF16), ("CS", CS[:].rearrange("p a b -> p (a b)"), [128, 4096], F32)])

            with ExitStack() as st:
                yT = sbt(st, "yT", [128, 4, 2048], BF16)
                sgT = sbt(st, "sgT", [128, 4, 2048], BF16)
                selt = sbt(st, "selt", [128, 4, 64], BF16)
                wglu_t = sbt(st, "wglu_t", [128, 4, 512], BF16)
                bglu_t = sbt(st, "bglu_t", [128, 4], F32)
                sg = [sbt(st, f"sg{i}", [128, 512], BF16) for i in range(2)]
                psZ = [pst(st, f"p2b_ps{i}", [128, 512]) for i in range(4)]
                cload(selt[:], sel_d, "selt")
                cload(bglu_t[:], b_gluT_d, "bglu_t")
                wk = cast_until(["wglu_s"])
                load_w(wglu_t[:], wglu_s.rearrange("(kt p) c -> p kt c", p=128), "wglu_t", wk, "ld_w1")
                zc = 0
                for s in range(4):
                    for ct in range(4):
                        ps = psZ[zc % 4]; pk = f"p2b_ps{zc % 4}"; zc += 1
                        for j in range(8):
                            p.op("pe", lambda e, ps=ps, s=s, ct=ct, j=j: e.matmul(
                                ps[:, j * 64:(j + 1) * 64], lhsT=Ytm[:, s, j, ct * 128:(ct + 1) * 128], rhs=selt[:, s, :],
                                start=True, stop=True), reads=["selt"], writes=[pk])
                        if ct % 2 == 0:
                            p.op("act", lambda e, ps=ps, s=s, ct=ct: e.activation(
                                out=yT[:, ct, s * 512:(s + 1) * 512].rearrange("p (m j) -> p m j", j=8),
                                in_=ps[:, :].rearrange("p (j m) -> p m j", j=8), func=AF.Copy), reads=[pk], writes=[f"yT{s}_{ct}"])
                        else:
                            p.op("dve", lambda e, ps=ps, s=s, ct=ct: e.tensor_copy(
                                out=yT[:, ct, s * 512:(s + 1) * 512].rearrange("p (m j) -> p m j", j=8),
                                in_=ps[:, :].rearrange("p (j m) -> p m j", j=8)), reads=[pk], writes=[f"yT{s}_{ct}"])
                for s in range(4):
                    for ct in range(4):
                        ps = psZ[zc % 4]; pk = f"p2b_ps{zc % 4}"; zc += 1
                        for kt in range(4):
                            p.op("pe", lambda e, ps=ps, s=s, ct=ct, kt=kt: e.matmul(
                                ps[:, :], lhsT=wglu_t[:, kt, ct * 128:(ct + 1) * 128], rhs=yT[:, kt, s * 512:(s + 1) * 512],
                                start=(kt == 0), stop=(kt == 3)), reads=["wglu_t", f"yT{s}_{kt}"], writes=[pk])
                        sgi = zc % 2
                        p.op("act", lambda e, ps=ps, ct=ct, sgi=sgi: e.activation(
                            out=sg[sgi][:], in_=ps[:, :], func=AF.Sigmoid, bias=bglu_t[:, ct:ct + 1]),
                             reads=[pk, "bglu_t"], writes=[f"sg{sgi}"])
                        p.op("dve", lambda e, s=s, ct=ct, sgi=sgi: e.tensor_tensor(
                            out=sgT[:, ct, s * 512:(s + 1) * 512], in0=sg[sgi][:], in1=yT[:, ct, s * 512:(s + 1) * 512], op=ALU.mult),
                             reads=[f"sg{sgi}", f"yT{s}_{ct}"], writes=[f"sgT{s}_{ct}"])
                p.dma(lambda e: e.dma_start(out=sglu_s, in_=sgT[:]), "st_sg",
                      reads=[f"sgT{s}_{ct}" for s in range(4) for ct in range(4)], writes=["sglu_s"])
                phase_end([("sgluT", sgT[:].rearrange("p a b -> p (a b)"), [128, 4 * 2048], BF16)])

        s5o.close()
        with ExitStack() as at:
            KT = sbt(at, "KT", [128, 4, T], BF16)
            V = sbt(at, "V", [128, 32, 512], BF16)
            QT = sbt(at, "QT", [128, 4, 2048], BF16)
            with ExitStack() as st:
                ctx = {"name": "p1b", "psb_ctr": [0],
                       "psb": [pst(st, f"p1b_psb{i}", [128, 1024], BF16) for i in range(2)]}
                psf = [pst(st, f"p1b_psf{i}", [128, 512]) for i in range(6)]
                xt = [sbt(st, f"p1b_xt{i}", [128, D], F32) for i in range(8)]
                xn4 = [sbt(st, f"p1b_xn{i}", [128, 4, D], BF16) for i in range(2)]
                hT = [sbt(st, f"p1b_hT{i}", [128, 8, 512], BF16) for i in range(1)]
                wQKV = sbt(st, "wQKV", [128, 8, 1536], BF16)

                wk = cast_until(["win_qkv"])
                load_w(wQKV[:], win_s.rearrange("(kt p) c -> p kt c", p=128)[:, :, 512:2048], "wQKV", wk, "ld_w2")
                chunks = [("all", c) for c in range(8)] + [("own", s) for s in range(4)]
                pf = 0

                def rows(kind, i):
                    return (x_all if kind == "all" else x_own)[i * 512:(i + 1) * 512, :]

                normA(make_xloader(xt, "p1b_xt", rows(*chunks[0])), xn4[0], "p1b_xn0")
                for ci, (kind, idx) in enumerate(chunks):
                    r = ci % 2
                    hb = hT[0]; hbk = "p1b_hT0"
                    normB(ctx, xn4[r], f"p1b_xn{r}", hb, hbk, geff1, "geff1", sh1, "modT0_1")
                    if ci + 1 < len(chunks):
                        r2 = (ci + 1) % 2
                        normA(make_xloader(xt, "p1b_xt", rows(*chunks[ci + 1])), xn4[r2], f"p1b_xn{r2}")
                    if kind == "all":
                        for ct in range(4):
                            ps = psf[pf % 6]; pk = f"p1b_psf{pf % 6}"; pf += 1
                            for kt in range(8):
                                p.op("pe", lambda e, ps=ps, hb=hb, kt=kt, ct=ct: e.matmul(
                                    ps[:, :], lhsT=wQKV[:, kt, 512 + ct * 128: 512 + (ct + 1) * 128], rhs=hb[:, kt, :],
                                    start=(kt == 0), stop=(kt == 7)), reads=[f"{hbk}_{kt}", "wQKV"], writes=[pk])
                            if ct % 2 == 0:
                                p.op("act", lambda e, ps=ps, ct=ct, idx=idx: e.activation(out=KT[:, ct, idx * 512:(idx + 1) * 512], in_=ps[:, :], func=AF.Copy),
                                     reads=[pk], writes=[newkey("KT")])
                            else:
                                p.op("dve", lambda e, ps=ps, ct=ct, idx=idx: e.tensor_copy(out=KT[:, ct, idx * 512:(idx + 1) * 512], in_=ps[:, :]),
                                     reads=[pk], writes=[newkey("KT")])
                        for tt in range(4):
                            ps = psf[pf % 6]; pk = f"p1b_psf{pf % 6}"; pf += 1
                            for kt in range(8):
                                p.op("pe", lambda e, ps=ps, hb=hb, kt=kt, tt=tt: e.matmul(
                                    ps[:, :], lhsT=hb[:, kt, tt * 128:(tt + 1) * 128], rhs=wQKV[:, kt, 1024:1536],
                                    start=(kt == 0), stop=(kt == 7)), reads=[f"{hbk}_{kt}", "wQKV"], writes=[pk])
                            if tt % 2 == 0:
                                p.op("act", lambda e, ps=ps, tt=tt, idx=idx: e.activation(out=V[:, idx * 4 + tt, :], in_=ps[:, :], func=AF.Copy),
                                     reads=[pk], writes=[newkey("V")])
                            else:
                                p.op("dve", lambda e, ps=ps, tt=tt, idx=idx: e.tensor_copy(out=V[:, idx * 4 + tt, :], in_=ps[:, :]),
                                     reads=[pk], writes=[newkey("V")])
                    else:
                        for ct in range(4):
                            ps = psf[pf % 6]; pk = f"p1b_psf{pf % 6}"; pf += 1
                            for kt in range(8):
                                p.op("pe", lambda e, ps=ps, hb=hb, kt=kt, ct=ct: e.matmul(
                                    ps[:, :], lhsT=wQKV[:, kt, ct * 128:(ct + 1) * 128], rhs=hb[:, kt, :],
                                    start=(kt == 0), stop=(kt == 7)), reads=[f"{hbk}_{kt}", "wQKV"], writes=[pk])
                            p.op("act", lambda e, ps=ps, ct=ct, idx=idx: e.activation(out=QT[:, ct, idx * 512:(idx + 1) * 512], in_=ps[:, :], func=AF.Copy, scale=0.125),
                                 reads=[pk], writes=[newkey("QT")])
                    cast_step(8, paced=True)
                phase_end([("KT", KT[:].rearrange("p a b -> p (a b)"), [128, 4 * T], BF16),
                           ("V", V[:].rearrange("p a b -> p (a b)"), [128, 32 * 512], BF16),
                           ("QT", QT[:].rearrange("p a b -> p (a b)"), [128, 4 * 2048], BF16)])

            with ExitStack() as st:
                atri = sbt(st, "atri", [128, 128], BF16)
                neguincl = sbt(st, "neguincl", [128, 128], BF16)
                neglow = sbt(st, "neglow", [128, 128], BF16)
                cload(atri[:], atri_d, "atri")
                cload(neguincl[:], neguincl_d, "neguincl")
                cload(neglow[:], neglow_d, "neglow")
                attnT = sbt(st, "attnT", [128, 4, 2048], BF16)
                mB = [sbt(st, f"mB{i}", [128, 8, 512], BF16) for i in range(2)]
                psP = [pst(st, f"psP{i}", [128, 512]) for i in range(2)]
                psO = pst(st, "psO", [128, 512])
                psZ = [pst(st, f"a_psZ{i}", [128, 512]) for i in range(4)]
                NE, NL, NP, NW = 3, 4, 2, 3
                eb = [[sbt(st, f"eb{h}_{i}", [128, 512], F32) for i in range(NE)] for h in range(2)]
                Lb = [[sbt(st, f"Lb{h}_{i}", [128, 512], BF16) for i in range(NL)] for h in range(2)]
                ePb = [[sbt(st, f"ePb{h}_{i}", [128, 512], F32) for i in range(NP)] for h in range(2)]
                Wb = [[sbt(st, f"Wb{h}_{i}", [128, 512], BF16) for i in range(NW)] for h in range(2)]
                zc = [0]
                for s in range(4):
                    mr = s % 2
                    p.dma(lambda e, mr=mr, s=s: e.dma_start(out=mB[mr][:], in_=maskB_d[:, s, :, :]), f"ld_mask{mr}", writes=[f"mB{mr}"])
                    KBs = 8 * (s + 1)
                    kbs = list(range(KBs - 1, -1, -1))
                    n = len(kbs)
                    for hp in range(4):
                        heads = (2 * hp, 2 * hp + 1)

                        def st_qk(t):
                            kb = kbs[t]
                            for hh in range(2):
                                lo, hi = hh * 64, hh * 64 + 64
                                zi = zc[0] % 4; zc[0] += 1
                                zb = psZ[zi]; zk = f"a_psZ{zi}"
                                masked = kb >= KBs - 8
                                p.op("pe", lambda e, zb=zb, lo=lo, hi=hi, kb=kb, masked=masked, hp=hp, s=s: e.matmul(
                                    zb[:, :], lhsT=KT[lo:hi, hp, kb * 128:(kb + 1) * 128], rhs=QT[lo:hi, hp, s * 512:(s + 1) * 512],
                                    start=True, stop=(not masked)), reads=[], writes=[zk])
                                if masked:
                                    p.op("pe", lambda e, zb=zb, kb=kb, mr=mr, KBs=KBs: e.matmul(
                                        zb[:, :], lhsT=atri[:], rhs=mB[mr][:, kb - (KBs - 8), :], start=False, stop=True),
                                         reads=["atri", f"mB{mr}"], writes=[zk])
                                ei = t % NE
                                p.op("act", lambda e, zb=zb, hh=hh, ei=ei: e.activation(out=eb[hh][ei][:], in_=zb[:, :], func=AF.Exp),
                                     reads=[zk], writes=[f"eb{hh}_{ei}"])
                            for hh in range(2):
                                ei = t % NE; li = t % NL
                                p.op("act", lambda e, hh=hh, ei=ei, li=li: e.activation(out=Lb[hh][li][:], in_=eb[hh][ei][:], func=AF.Ln, bias=1.0),
                                     reads=[f"eb{hh}_{ei}"], writes=[f"Lb{hh}_{li}"])

                        def st_p(t):
                            for hh in range(2):
                                li = t % NL
                                if t > 0:
                                    lpi = (t - 1) % NL
                                    p.op("pe", lambda e, hh=hh, lpi=lpi: e.matmul(
                                        psP[hh][:, :], lhsT=neglow[:], rhs=Lb[hh][lpi][:], start=False, stop=False, skip_group_check=True),
                                         reads=["neglow", f"Lb{hh}_{lpi}"], writes=[f"psP{hh}"])
                                p.op("pe", lambda e, hh=hh, li=li, t=t: e.matmul(
                                    psP[hh][:, :], lhsT=neguincl[:], rhs=Lb[hh][li][:], start=(t == 0), stop=False, skip_group_check=True),
                                     reads=["neguincl", f"Lb{hh}_{li}"], writes=[f"psP{hh}"])
                            for hh in range(2):
                                pi = t % NP; ei = t % NE; wi = t % NW
                                p.op("act", lambda e, hh=hh, pi=pi: e.activation(out=ePb[hh][pi][:], in_=psP[hh][:, :], func=AF.Exp),
                                     reads=[f"psP{hh}"], writes=[f"ePb{hh}_{pi}"])
                                p.op("dve", lambda e, hh=hh, pi=pi, ei=ei, wi=wi: e.tensor_tensor(
                                    out=Wb[hh][wi][:], in0=eb[hh][ei][:], in1=ePb[hh][pi][:], op=ALU.mult),
                                     reads=[f"eb{hh}_{ei}", f"ePb{hh}_{pi}"], writes=[f"Wb{hh}_{wi}"])

                        def st_pv(t):
                            kb = kbs[t]
                            for hh in range(2):
                                h = heads[hh]
                                lo, hi = hh * 64, hh * 64 + 64
                                wi = t % NW
                                p.op("pe", lambda e, hh=hh, h=h, lo=lo, hi=hi, kb=kb, wi=wi, t=t, n=n: e.matmul(
                                    psO[lo:hi, :], lhsT=V[:, kb, h * 64:(h + 1) * 64], rhs=Wb[hh][wi][:],
                                    start=(t == 0), stop=(t == n - 1)), reads=[f"Wb{hh}_{wi}"], writes=["psO"])

                        for step in range(n + 2):
                            if step < n:
                                st_qk(step)
                            if 1 <= step <= n:
                                st_p(step - 1)
                            if step >= 2:
                                st_pv(step - 2)
                        if hp % 2 == 0:
                            p.op("act", lambda e, s=s, hp=hp: e.activation(out=attnT[:, hp, s * 512:(s + 1) * 512], in_=psO[:, :], func=AF.Copy),
                                 reads=["psO"], writes=[f"attnT{s}_{hp}"])
                        else:
                            p.op("dve", lambda e, s=s, hp=hp: e.tensor_copy(out=attnT[:, hp, s * 512:(s + 1) * 512], in_=psO[:, :]),
                                 reads=["psO"], writes=[f"attnT{s}_{hp}"])
                p.dma(lambda e: e.dma_start(out=attn_s, in_=attnT[:]), "st_at",
                      reads=[f"attnT{s}_{hp}" for s in range(4) for hp in range(4)], writes=["attn_s"])
                phase_end([("attnT", attnT[:].rearrange("p a b -> p (a b)"), [128, 4 * 2048], BF16)])

        with ExitStack() as st:
            ctx = {"name": "p4a", "psb_ctr": [0],
                   "psb": [pst(st, f"p4a_psb{i}", [128, 1024], BF16) for i in range(2)]}
            psf = [pst(st, f"p4a_psf{i}", [128, 512]) for i in range(6)]
            xcA = [sbt(st, f"p4a_xc{i}", [128, 4, D], F32) for i in range(2)]
            xn4 = [sbt(st, f"p4a_xn{i}", [128, 4, D], BF16) for i in range(2)]
            hT = [sbt(st, f"p4a_hT{i}", [128, 8, 512], BF16) for i in range(1)]
            wgt = [sbt(st, f"wgt{i}", [128, 8, 512], BF16) for i in range(3)]
            wa_t = sbt(st, "wa_t", [128, 4, D], BF16)
            wb_t = sbt(st, "wb_t", [128, 4, D], BF16)
            wo_t = sbt(st, "wo_t", [128, 8, D], BF16)
            gT = sbt(st, "gT", [128, 16, 512], BF16)
            mT = sbt(st, "mT", [128, 8, 512], BF16)
            t12 = [sbt(st, f"t12_{i}", [128, 512], F32) for i in range(4)]
            sa = [sbt(st, f"sa{i}", [128, 2, 4, 512], BF16) for i in range(2)]
            wk = cast_until(["win_qkv", "win_s", "wa_s", "wb_s", "wo_s"])
            load_w(wa_t[:], wa_s.rearrange("(kt p) c -> p kt c", p=128), "wa_t", wk, "ld_w4")
            load_w(wb_t[:], wb_s.rearrange("(kt p) c -> p kt c", p=128), "wb_t", wk, "ld_w5")
            load_w(wo_t[:], wo_s.rearrange("(kt p) c -> p kt c", p=128), "wo_t", wk, "ld_w0")
            win_v = win_s.rearrange("(kt p) c -> p kt c", p=128)
            pf = 0
            tc_ = 0
            wq = 0

            def xc_loader(r, s):
                p.dma(lambda e, r=r, s=s: e.dma_start(out=xcA[r][:], in_=x_own[s * 512:(s + 1) * 512, :].rearrange("(tt p) d -> p tt d", p=128)),
                      f"ld_x{r}", writes=[f"p4a_xc{r}"])
                return lambda tt, r=r: (xcA[r][:, tt, :], f"p4a_xc{r}")

            normA(xc_loader(0, 0), xn4[0], "p4a_xn0")
            for s in range(4):
                r = s % 2
                hb = hT[0]; hbk = "p4a_hT0"
                sat = sa[r]
                p.dma(lambda e, sat=sat, s=s: e.dma_start(out=sat[:, 0, :, :], in_=sglu_s[:, :, s * 512:(s + 1) * 512]), f"ld_sa{r}",
                      reads=["sglu_s"], writes=[f"sa{r}_0"])
                p.dma(lambda e, sat=sat, s=s: e.dma_start(out=sat[:, 1, :, :], in_=attn_s[:, :, s * 512:(s + 1) * 512]), f"ld_sb{r}",
                      reads=["attn_s"], writes=[f"sa{r}_1"])
                sak = [f"sa{r}_0", f"sa{r}_1"]
                normB(ctx, xn4[r], f"p4a_xn{r}", hb, hbk, geff1, "geff1", sh1, "modT0_1")
                if s + 1 < 4:
                    r2 = (s + 1) % 2
                    normA(xc_loader(r2, s + 1), xn4[r2], f"p4a_xn{r2}")
                for g4 in range(4):
                    wi = wq % 3; wq += 1
                    p.dma(lambda e, wi=wi, g4=g4: e.dma_start(out=wgt[wi][:], in_=win_v[:, :, 2048 + g4 * 512: 2048 + (g4 + 1) * 512]),
                          f"ld_wgt{wi}", reads=wk, writes=[f"wgt{wi}"])
                    for gl in range(4):
                        gi = g4 * 4 + gl
                        ps = psf[pf % 6]; pk = f"p4a_psf{pf % 6}"; pf += 1
                        for kt in range(8):
                            p.op("pe", lambda e, ps=ps, hb=hb, kt=kt, gl=gl, wi=wi: e.matmul(
                                ps[:, :], lhsT=wgt[wi][:, kt, gl * 128:(gl + 1) * 128], rhs=hb[:, kt, :], start=(kt == 0), stop=(kt == 7)),
                                 reads=[f"{hbk}_{kt}", f"wgt{wi}"], writes=[pk])
                        p.op("act", lambda e, ps=ps, gi=gi: e.activation(out=gT[:, gi, :], in_=ps[:, :], func=AF.Sigmoid),
                             reads=[pk], writes=[f"gT{gi}"])
                for ct in range(8):
                    psa = psf[pf % 6]; pka = f"p4a_psf{pf % 6}"; pf += 1
                    psb_ = psf[pf % 6]; pkb = f"p4a_psf{pf % 6}"; pf += 1
                    for kt in range(4):
                        p.op("pe", lambda e, psa=psa, kt=kt, ct=ct, sat=sat: e.matmul(
                            psa[:, :], lhsT=wa_t[:, kt, ct * 128:(ct + 1) * 128], rhs=sat[:, 0, kt, :],
                            start=(kt == 0), stop=(kt == 3)), reads=["wa_t"] + sak, writes=[pka])
                    for kt in range(4):
                        p.op("pe", lambda e, psb_=psb_, kt=kt, ct=ct, sat=sat: e.matmul(
                            psb_[:, :], lhsT=wb_t[:, kt, ct * 128:(ct + 1) * 128], rhs=sat[:, 1, kt, :],
                            start=(kt == 0), stop=(kt == 3)), reads=["wb_t"] + sak, writes=[pkb])
                    ta = t12[tc_ % 4]; tak = f"t12_{tc_ % 4}"; tc_ += 1
                    tb = t12[tc_ % 4]; tbk = f"t12_{tc_ % 4}"; tc_ += 1
                    p.op("dve", lambda e, psa=psa, ta=ta, ct=ct: e.tensor_tensor(out=ta[:], in0=psa[:, :], in1=gT[:, ct, :], op=ALU.mult),
                         reads=[pka, f"gT{ct}"], writes=[tak])
                    p.op("dve", lambda e, psb_=psb_, tb=tb, ct=ct: e.tensor_tensor(out=tb[:], in0=psb_[:, :], in1=gT[:, 8 + ct, :], op=ALU.mult),
                         reads=[pkb, f"gT{8 + ct}"], writes=[tbk])
                    p.op("dve", lambda e, ta=ta, tb=tb, ct=ct: e.tensor_tensor(out=mT[:, ct, :], in0=ta[:], in1=tb[:], op=ALU.add),
                         reads=[tak, tbk], writes=[f"mT{ct}"])
                mks = [f"mT{ct}" for ct in range(8)]
                xck = f"p4a_xc{r}"
                for tt in range(4):
                    for h in range(2):
                        ps = psf[pf % 6]; pk = f"p4a_psf{pf % 6}"; pf += 1
                        for kt in range(8):
                            p.op("pe", lambda e, ps=ps, kt=kt, tt=tt, h=h: e.matmul(
                                ps[:, :], lhsT=mT[:, kt, tt * 128:(tt + 1) * 128], rhs=wo_t[:, kt, h * 512:(h + 1) * 512],
                                start=(kt == 0), stop=(kt == 7)), reads=[f"mT{kt}", "wo_t"], writes=[pk])
                        ta = t12[tc_ % 4]; tak = f"t12_{tc_ % 4}"; tc_ += 1
                        p.op("dve", lambda e, ps=ps, ta=ta, h=h: e.tensor_tensor(out=ta[:], in0=ps[:, :], in1=g1bc[:, h * 512:(h + 1) * 512], op=ALU.mult),
                             reads=[pk, "g1bc"], writes=[tak])
                        p.op("dve", lambda e, ta=ta, tt=tt, h=h, r=r: e.tensor_tensor(
                            out=xcA[r][:, tt, h * 512:(h + 1) * 512], in0=ta[:], in1=xcA[r][:, tt, h * 512:(h + 1) * 512], op=ALU.add),
                             reads=[tak, xck], writes=[xck])
                p.dma(lambda e, r=r, s=s: e.dma_start(out=x1_s[s * 512:(s + 1) * 512, :].rearrange("(tt p) d -> p tt d", p=128), in_=xcA[r][:]),
                      f"st_x1{r}", reads=[xck], writes=[f"x1_s{s}"], q="pool")
                cast_step(8)
            cast_step(500)
            phase_end(exclude=())

        with ExitStack() as st:
            ctx = {"name": "p4b", "psb_ctr": [0],
                   "psb": [pst(st, f"p4b_psb{i}", [128, 1024], BF16) for i in range(2)]}
            psf = [pst(st, f"p4b_psf{i}", [128, 512]) for i in range(6)]
            xcB = [sbt(st, f"p4b_xc{i}", [128, 4, D], F32) for i in range(2)]
            xn4 = [sbt(st, f"p4b_xn{i}", [128, 4, D], BF16) for i in range(2)]
            hT = [sbt(st, f"p4b_hT{i}", [128, 8, 512], BF16) for i in range(1)]
            wd_t = sbt(st, "wd_t", [128, NFT, D], BF16)
            wgu = [sbt(st, f"wgu{i}", [128, 2, 8, 128], BF16) for i in range(3)]
            hid = sbt(st, "hid", [128, NFT, 512], BF16)
            sgf = [sbt(st, f"sgf{i}", [128, 512], F32) for i in range(2)]
            tf = [sbt(st, f"tf{i}", [128, 512], F32) for i in range(2)]
            x2 = [sbt(st, f"x2_{i}", [128, D], F32) for i in range(2)]
            ost = [sbt(st, f"ost{i}", [128, D], F32) for i in range(2)]
            gfbc = sbt(st, "gfbc", [128, D], F32)
            cload(gfbc[:], nfg_row.partition_broadcast(128), "gfbc")
            load_w(wd_t[:], wd_s.rearrange("(ft p) c -> p ft c", p=128), "wd_t", cast_done["wd_s"], "ld_w1")
            pf = 0
            wq = 0
            tcn = 0
            xq = 0

            def x1_loader(r, s):
                p.dma(lambda e, r=r, s=s: e.dma_start(out=xcB[r][:], in_=x1_s[s * 512:(s + 1) * 512, :].rearrange("(tt p) d -> p tt d", p=128)),
                      f"ld_x1{r}", reads=[f"x1_s{s}"], writes=[f"p4b_xc{r}"])
                return lambda tt, r=r: (xcB[r][:, tt, :], f"p4b_xc{r}")

            normA(x1_loader(0, 0), xn4[0], "p4b_xn0")
            for s in range(4):
                r = s % 2
                hb = hT[0]; hbk = "p4b_hT0"
                normB(ctx, xn4[r], f"p4b_xn{r}", hb, hbk, geff2, "geff2", sh2, "modT2_1")
                if s + 1 < 4:
                    r2 = (s + 1) % 2
                    normA(x1_loader(r2, s + 1), xn4[r2], f"p4b_xn{r2}")
                for ft in range(NFT):
                    wi = wq % 3; wq += 1
                    wt = wgu[wi]; wtk = f"wgu{wi}"
                    p.dma(lambda e, wt=wt, ft=ft: e.dma_start(out=wt[:, 0, :, :], in_=wg_s[ft, :, :, :]), f"ld_wgu{wi}",
                          reads=cast_done["wg_s"], writes=[wtk + "g"])
                    p.dma(lambda e, wt=wt, ft=ft: e.dma_start(out=wt[:, 1, :, :], in_=wu_s[ft, :, :, :]), f"ld_wgv{wi}",
                          reads=cast_done["wu_s"], writes=[wtk + "u"])
                    psg = psf[pf % 6]; pkg = f"p4b_psf{pf % 6}"; pf += 1
                    psu = psf[pf % 6]; pku = f"p4b_psf{pf % 6}"; pf += 1
                    for kt in range(8):
                        p.op("pe", lambda e, psg=psg, wt=wt, kt=kt, hb=hb: e.matmul(
                            psg[:, :], lhsT=wt[:, 0, kt, :], rhs=hb[:, kt, :], start=(kt == 0), stop=(kt == 7)),
                             reads=[wtk + "g", wtk + "u", f"{hbk}_{kt}"], writes=[pkg])
                    for kt in range(8):
                        p.op("pe", lambda e, psu=psu, wt=wt, kt=kt, hb=hb: e.matmul(
                            psu[:, :], lhsT=wt[:, 1, kt, :], rhs=hb[:, kt, :], start=(kt == 0), stop=(kt == 7)),
                             reads=[wtk + "g", wtk + "u", f"{hbk}_{kt}"], writes=[pku])
                    si = ft % 2
                    p.op("act", lambda e, psg=psg, si=si: e.activation(out=sgf[si][:], in_=psg[:, :], func=AF.Silu),
                         reads=[pkg], writes=[f"sgf{si}"])
                    p.op("dve", lambda e, psu=psu, si=si, ft=ft: e.tensor_tensor(out=hid[:, ft, :], in0=psu[:, :], in1=sgf[si][:], op=ALU.mult),
                         reads=[pku, f"sgf{si}"], writes=[f"hid{ft}"])
                for tt in range(4):
                    xi = xq % 2; xq += 1
                    for h in range(2):
                        ps = psf[pf % 6]; pk = f"p4b_psf{pf % 6}"; pf += 1
                        for ft in range(NFT):
                            p.op("pe", lambda e, ps=ps, ft=ft, tt=tt, h=h: e.matmul(
                                ps[:, :], lhsT=hid[:, ft, tt * 128:(tt + 1) * 128], rhs=wd_t[:, ft, h * 512:(h + 1) * 512],
                                start=(ft == 0), stop=(ft == NFT - 1)), reads=[f"hid{ft}", "wd_t"], writes=[pk])
                        ti = tcn % 2; tcn += 1
                        p.op("dve", lambda e, ps=ps, ti=ti, h=h: e.tensor_tensor(out=tf[ti][:], in0=ps[:, :], in1=g2bc[:, h * 512:(h + 1) * 512], op=ALU.mult),
                             reads=[pk, "g2bc"], writes=[f"tf{ti}"])
                        p.op("dve", lambda e, ti=ti, xi=xi, tt=tt, h=h, r=r: e.tensor_tensor(
                            out=x2[xi][:, h * 512:(h + 1) * 512], in0=tf[ti][:], in1=xcB[r][:, tt, h * 512:(h + 1) * 512], op=ALU.add),
                             reads=[f"tf{ti}", f"p4b_xc{r}"], writes=[f"x2_{xi}_{h}"])
                    ss, ssk = newvec(); sd, sdk = newvec(); rs, rsk = newvec()
                    x2k = [f"x2_{xi}_0", f"x2_{xi}_1"]
                    p.op("act", lambda e, xi=xi, ss=ss: e.activation(out=junk[:], in_=x2[xi][:], func=AF.Square, accum_out=ss),
                         reads=x2k, writes=[ssk, "junk"])
                    p.op("act", lambda e, ss=ss, sd=sd: e.activation(out=sd, in_=ss, func=AF.Sqrt, scale=1.0 / D, bias=epsv[:, 0:1]),
                         reads=[ssk, "epsv"], writes=[sdk])
                    p.op("dve", lambda e, sd=sd, rs=rs: e.reciprocal(out=rs, in_=sd), reads=[sdk], writes=[rsk])
                    p.op("dve", lambda e, xi=xi, rs=rs: e.scalar_tensor_tensor(out=ost[xi][:], in0=x2[xi][:], scalar=rs, in1=gfbc[:], op0=ALU.mult, op1=ALU.mult),
                         reads=x2k + [rsk, "gfbc"], writes=[f"ost{xi}"])
                    row0 = s * 512 + tt * 128
                    p.dma(lambda e, xi=xi, row0=row0: e.dma_start(out=out_d[row0:row0 + 128, :], in_=ost[xi][:]), f"st_out{xi}",
                          reads=[f"ost{xi}"], q="pool")
            p.final_wait("sp", [k for k in p.count if k not in ("pe", "act", "dve", "pool")])
        p.emit(block, sems)
    return nc, list(dbg.keys())


def _bf(a):
    return np.ascontiguousarray(a).astype(NPBF)


def prep_inputs(inp):
    f32 = np.float32
    x = np.asarray(inp["x"], f32)
    c = np.asarray(inp["c"], f32)
    L = 0
    shared = {}
    shared["w_ada"] = np.ascontiguousarray(np.asarray(inp["w_ada"], f32)[L])
    b_ada = np.asarray(inp["b_ada"], f32)[L]
    shared["b_adaT"] = np.ascontiguousarray(b_ada.reshape(48, 128).T)
    shared["b_ada_row"] = np.ascontiguousarray(b_ada.reshape(1, -1))
    shared["n1gT"] = np.ascontiguousarray(np.asarray(inp["norm1_g"], f32)[L].reshape(8, 128).T)
    shared["n2gT"] = np.ascontiguousarray(np.asarray(inp["norm2_g"], f32)[L].reshape(8, 128).T)
    shared["nfg_row"] = np.ascontiguousarray(np.asarray(inp["norm_f_g"], f32).reshape(1, D))
    shared["w_in"] = np.ascontiguousarray(np.asarray(inp["w_in"], f32)[L])

    def pairlay(a):
        return np.ascontiguousarray(a.reshape(16, 2, 64).transpose(1, 2, 0).reshape(128, 16))

    lam_re = np.asarray(inp["lam_re"], f32)[L]
    lam_im = np.asarray(inp["lam_im"], f32)[L]
    log_dt = np.asarray(inp["log_dt"], f32)[L]
    shared["lamre_p"] = pairlay(lam_re)
    shared["lamim_p"] = pairlay(lam_im)
    shared["logdt_p"] = pairlay(np.repeat(log_dt[:, None], 64, axis=1))

    def pairlay3(a):
        return np.ascontiguousarray(a.reshape(16, 2, 64, a.shape[2]).transpose(1, 2, 0, 3).reshape(128, 16, a.shape[2]))

    shared["bre_p"] = pairlay3(np.asarray(inp["b_re"], f32)[L])
    shared["bim_p"] = pairlay3(np.asarray(inp["b_im"], f32)[L])
    shared["cre_p"] = pairlay3(np.asarray(inp["c_re"], f32)[L].transpose(0, 2, 1))
    shared["cim_p"] = pairlay3(np.asarray(inp["c_im"], f32)[L].transpose(0, 2, 1))
    d_skip = np.asarray(inp["d_skip"], f32)[L]
    shared["dvec"] = np.ascontiguousarray(np.tile(d_skip.T, (8, 1)))
    shared["w_glu"] = np.ascontiguousarray(np.asarray(inp["w_glu"], f32)[L])
    shared["b_gluT"] = np.ascontiguousarray(np.asarray(inp["b_glu"], f32)[L].reshape(4, 128).T)
    shared["w_a"] = np.ascontiguousarray(np.asarray(inp["w_a"], f32)[L])
    shared["w_b"] = np.ascontiguousarray(np.asarray(inp["w_b"], f32)[L])
    shared["w_o"] = np.ascontiguousarray(np.asarray(inp["w_o"], f32)[L])
    shared["wg"] = np.ascontiguousarray(np.asarray(inp["w_ffn_gate"], f32)[L])
    shared["wu"] = np.ascontiguousarray(np.asarray(inp["w_ffn_up"], f32)[L])
    shared["wd"] = np.ascontiguousarray(np.asarray(inp["w_ffn_down"], f32)[L])
    shared["identf"] = np.eye(128, dtype=f32)
    jj = np.arange(128)
    shared["atri"] = _bf(np.where(jj[:, None] <= jj[None, :], NEG_BIG, 0.0).astype(f32))
    shared["neguincl"] = _bf(np.where(jj[:, None] >= jj[None, :], -1.0, 0.0).astype(f32))
    shared["neglow"] = _bf(np.where(jj[:, None] < jj[None, :], -1.0, 0.0).astype(f32))
    ii = jj // 16
    shared["cmask"] = (ii[None, :] >= ii[:, None]).astype(f32)
    kvals = np.concatenate([-np.arange(8), 7 - np.arange(8), np.arange(8), 1 + np.arange(8)]).astype(f32)
    shared["kv"] = np.ascontiguousarray(np.tile(kvals[None, :], (128, 1)))
    shared["ones_row"] = np.ones((1, 128), f32)
    hm = np.zeros((128, 2), f32)
    hm[:64, 0] = 1.0
    hm[64:, 1] = 1.0
    shared["hmask"] = hm
    in_maps = []
    for core in range(8):
        b, hf = core // 2, core % 2
        own = OWN[hf]
        m = dict(shared)
        m["x_all"] = np.ascontiguousarray(x[b])
        m["x_own"] = np.ascontiguousarray(np.concatenate([x[b, cc * 512:(cc + 1) * 512] for cc in own], axis=0))
        m["cT"] = np.ascontiguousarray(c[b].reshape(8, 128).T)
        maskB = np.zeros((128, 4, 8, 512), f32)
        sel = np.zeros((128, 4, 64), f32)
        qq = np.arange(512)
        for s in range(4):
            cs = own[s]
            for r in range(8):
                off = 512 * (cs - 2 * s) - 128 * r
                tq = qq + off
                allm = tq < 0
                part = (tq >= 0) & (tq <= 127)
                maskB[0, s, r, allm] = 1.0
                maskB[tq[part], s, r, qq[part]] = 1.0
            offn = (cs - 2 * s) * 64
            sel[np.arange(64) + offn, s, np.arange(64)] = 1.0
        m["maskB"] = _bf(maskB)
        m["sel"] = _bf(sel)
        in_maps.append(m)
    return in_maps


_CACHE = {}


def kernel(**inputs):
    if "nc" not in _CACHE:
        _CACHE["nc"] = build_program()
    nc, dbg_names = _CACHE["nc"]
    in_maps = prep_inputs(inputs)
    ncores = int(os.environ.get("K_NCORES", "8"))
    res = run_bass_kernel_spmd(nc, in_maps[:ncores], core_ids=list(range(ncores)))
    out = np.zeros((4, T, D), np.float32)
    for core in range(ncores):
        b, hf = core // 2, core % 2
        o = np.asarray(res.results[core]["out"], np.float32)
        for s, cc in enumerate(OWN[hf]):
            out[b, cc * 512:(cc + 1) * 512] = o[s * 512:(s + 1) * 512]
    _CACHE["last_results"] = res.results
    return out
```

```python
import math
import os
from contextlib import ExitStack

import numpy as np
import ml_dtypes
import concourse.bass as bass
import concourse.mybir as mybir
from concourse.bass_utils import run_bass_kernel_spmd

F32 = mybir.dt.float32
BF16 = mybir.dt.bfloat16
AF = mybir.ActivationFunctionType
ALU = mybir.AluOpType
NPBF = ml_dtypes.bfloat16

D = 1024
T = 4096
NB = 4
G = 32
FF = 2816
NFT = FF // 128
EPS = 1e-6
MAGIC = 12582912.0
C1 = 6.28125
C2 = 2.0 * math.pi - 6.28125
PI_LO = 3.1415925
NEG_BIG = -30000.0
OWN = {0: [0, 3, 4, 7], 1: [1, 2, 5, 6]}
DEBUG = False


class Prog:
    ENGS = ("pe", "act", "dve", "pool", "sp")

    def __init__(self):
        self.streams = {e: [] for e in self.ENGS}
        self.count = {}
        self.known = {e: {} for e in self.ENGS}
        self.lastw = {}
        self.readers = {}
        self.dma_sems = set()
        self.enabled = True
        self.phase = 0

    def _deps(self, eng, reads, writes):
        deps = {}

        def add(tok):
            if tok is None:
                return
            s, v = tok
            if deps.get(s, 0) < v:
                deps[s] = v

        for k in reads:
            add(self.lastw.get(k))
        for k in writes:
            add(self.lastw.get(k))
            for s, v in self.readers.get(k, {}).items():
                add((s, v))
        out = []
        for s, v in deps.items():
            if s == "pe" and eng == "pe":
                continue
            if self.known[eng].get(s, 0) >= v:
                continue
            self.known[eng][s] = v
            out.append((s, v))
        return out

    def _commit(self, tok, reads, writes):
        s, v = tok
        for k in reads:
            d = self.readers.setdefault(k, {})
            if d.get(s, 0) < v:
                d[s] = v
        for k in writes:
            self.lastw[k] = tok
            self.readers[k] = {}

    def op(self, eng, fn, reads=(), writes=()):
        if not self.enabled:
            return
        waits = self._deps(eng, reads, writes)
        v = self.count.get(eng, 0) + 1
        self.count[eng] = v
        self.streams[eng].append((waits, fn, (eng, 1)))
        self._commit((eng, v), reads, writes)

    def dma(self, fn, sem, reads=(), writes=(), q="sp"):
        if not self.enabled:
            return
        self.dma_sems.add(sem)
        waits = self._deps(q, reads, writes)
        v = self.count.get(sem, 0) + 16
        self.count[sem] = v
        self.streams[q].append((waits, fn, (sem, 16)))
        self._commit((sem, v), reads, writes)

    def barrier(self, exclude=()):
        if not self.enabled:
            return
        toks = [(s, v) for s, v in self.count.items() if s not in exclude and v > 0]
        for e in self.ENGS:
            if e in exclude:
                continue
            waits = []
            for s, v in toks:
                if s == e:
                    continue
                if self.known[e].get(s, 0) >= v:
                    continue
                self.known[e][s] = v
                waits.append((s, v))
            if waits:
                self.streams[e].append((waits, None, None))

    def final_wait(self, eng, semnames):
        waits = [(s, self.count[s]) for s in semnames if self.count.get(s, 0) > 0]
        self.streams[eng].append((waits, None, None))

    def emit(self, block, sems):
        engmap = {"pe": block.tensor, "act": block.scalar, "dve": block.vector,
                  "pool": block.gpsimd, "sp": block.sync}
        for e in self.ENGS:
            stream = self.streams[e]

            def body(engine, stream=stream):
                for waits, fn, inc in stream:
                    for s, v in waits:
                        engine.wait_ge(sems[s], v)
                    if fn is not None:
                        ins = fn(engine)
                        ins.then_inc(sems[inc[0]], inc[1])

            engmap[e](body)


def bc_last(ap, n):
    shp = list(ap.shape)
    if len(shp) == 2:
        return ap.rearrange("p (a o) -> p a o", o=1).to_broadcast([shp[0], shp[1], n])
    return ap.rearrange("p a (b o) -> p a b o", o=1).to_broadcast([shp[0], shp[1], shp[2], n])


def build_program():
    nc = bass.Bass("TRN2", target_bir_lowering=False)
    p = Prog()

    def din(name, shape, dt=F32):
        return nc.dram_tensor(name, list(shape), dt, kind="ExternalInput").ap()

    x_all = din("x_all", [T, D])
    x_own = din("x_own", [2048, D])
    cT_d = din("cT", [128, 8])
    w_ada = din("w_ada", [D, 6 * D])
    b_adaT_d = din("b_adaT", [128, 48])
    b_ada_row = din("b_ada_row", [1, 6 * D])
    n1gT_d = din("n1gT", [128, 8])
    n2gT_d = din("n2gT", [128, 8])
    nfg_row = din("nfg_row", [1, D])
    w_in = din("w_in", [D, 4096])
    lamre_d = din("lamre_p", [128, 16])
    lamim_d = din("lamim_p", [128, 16])
    logdt_d = din("logdt_p", [128, 16])
    bre_d = din("bre_p", [128, 16, 16])
    bim_d = din("bim_p", [128, 16, 16])
    cre_d = din("cre_p", [128, 16, 16])
    cim_d = din("cim_p", [128, 16, 16])
    dvec_d = din("dvec", [128, 32])
    w_glu = din("w_glu", [512, 512])
    b_gluT_d = din("b_gluT", [128, 4])
    w_a = din("w_a", [512, D])
    w_b = din("w_b", [512, D])
    w_o = din("w_o", [D, D])
    wg = din("wg", [D, FF])
    wu = din("wu", [D, FF])
    wd = din("wd", [FF, D])
    identf_d = din("identf", [128, 128])
    atri_d = din("atri", [128, 128], BF16)
    neguincl_d = din("neguincl", [128, 128], BF16)
    neglow_d = din("neglow", [128, 128], BF16)
    cmask_d = din("cmask", [128, 128])
    kv_d = din("kv", [128, 32])
    maskB_d = din("maskB", [128, 4, 8, 512], BF16)
    sel_d = din("sel", [128, 4, 64], BF16)
    ones_row_d = din("ones_row", [1, 128])
    hmask_d = din("hmask", [128, 2])
    out_d = nc.dram_tensor("out", [2048, D], F32, kind="ExternalOutput").ap()
    dbg = {}

    def scratch(name, shape, dt=BF16):
        return nc.dram_tensor(name, list(shape), dt).ap()

    win_s = scratch("win_s", [D, 4096])
    wglu_s = scratch("wglu_s", [512, 512])
    wa_s = scratch("wa_s", [512, D])
    wb_s = scratch("wb_s", [512, D])
    wo_s = scratch("wo_s", [D, D])
    wg_s = scratch("wg_s", [NFT, 128, 8, 128])
    wu_s = scratch("wu_s", [NFT, 128, 8, 128])
    wd_s = scratch("wd_s", [FF, D])
    x1_s = scratch("x1_s", [2048, D], F32)
    sglu_s = scratch("sglu_s", [128, 4, 2048])
    attn_s = scratch("attn_s", [128, 4, 2048])

    semnames = ["pe", "act", "dve", "pool", "st_dbg", "st_sg", "st_at"]
    RING_SEMS = {"ld_cast": 2, "ld_x": 8, "ld_w": 6, "ld_wada": 2, "ld_mask": 2, "ld_wgu": 3, "ld_x1": 2,
                 "st_cast": 8, "st_out": 2, "st_x1": 2, "ldc": 30, "ld_sa": 2, "ld_sb": 2, "ld_wgt": 3, "ld_wgv": 3}
    for k, n in RING_SEMS.items():
        for i in range(n):
            semnames.append(f"{k}{i}")
    CEX = ("pool", "ld_cast0", "ld_cast1") + tuple(f"st_cast{i}" for i in range(8))

    with ExitStack() as top:
        sems = {s: top.enter_context(nc.semaphore(s)) for s in semnames}
        block = top.enter_context(nc.Block())

        def sbt(st, name, shape, dt):
            return st.enter_context(nc.sbuf_tensor("sb_" + name, list(shape), dt))

        def pst(st, name, shape, dt=F32):
            return st.enter_context(nc.psum_tensor("pp_" + name, list(shape), dt))

        uid = [0]

        def newkey(prefix="k"):
            uid[0] += 1
            return f"{prefix}#{uid[0]}"

        def phase_end(dumps=(), exclude=CEX):
            p.barrier(exclude=exclude)
            p.phase += 1
            if p.phase >= int(os.environ.get("K_STOP", "99")):
                p.enabled = False
            if DEBUG and dumps:
                for (name, ap, shape, dt) in dumps:
                    o = nc.dram_tensor("dbg_" + name, list(shape), dt, kind="ExternalOutput").ap()
                    dbg[name] = o
                    p.dma(lambda e, o=o, ap=ap: e.dma_start(out=o, in_=ap), "st_dbg")
                p.barrier(exclude=exclude)

        subc = [0]

        def ckpt():
            subc[0] += 1
            if subc[0] >= int(os.environ.get("K_SUB", "999")):
                p.enabled = False

        cctr = [0]

        def cload(tile, src, key):
            i = cctr[0]
            cctr[0] += 1
            p.dma(lambda e: e.dma_start(out=tile, in_=src), f"ldc{i}", writes=[key])

        identf = sbt(top, "identf", [128, 128], F32)
        identb = sbt(top, "identb", [128, 128], BF16)
        cload(identf[:], identf_d, "identf")
        p.op("dve", lambda e: e.tensor_copy(out=identb[:], in_=identf[:]), reads=["identf"], writes=["identb"])
        modT = sbt(top, "modT", [128, 4, 8], F32)
        geff1 = sbt(top, "geff1", [128, 8], F32)
        geff2 = sbt(top, "geff2", [128, 8], F32)
        g1bc = sbt(top, "g1bc", [128, D], F32)
        g2bc = sbt(top, "g2bc", [128, D], F32)
        vecs = sbt(top, "vecs", [128, 96], F32)
        vctr = [0]

        def newvec():
            i = vctr[0] % 96
            vctr[0] += 1
            return vecs[:, i:i + 1], f"vec{i}"

        epsv = sbt(top, "epsv", [128, 1], F32)
        p.op("dve", lambda e: e.memset(epsv[:], EPS), writes=["epsv"])
        junk = sbt(top, "junk", [128, D], BF16)
        sh1 = modT[:, 0, :]
        sh2 = modT[:, 2, :]

        CW = 1408
        cst_in = [sbt(top, f"cst_in{i}", [128, CW], F32) for i in range(2)]
        cst_out = [sbt(top, f"cst_out{i}", [128, CW], BF16) for i in range(2)]
        cast_jobs = []
        cast_done = {}
        cast_pos = [0]

        def add_cast(src_ap, ncols, dst_fn, done_key):
            cast_jobs.append((src_ap, ncols, dst_fn, done_key))

        for kt in range(8):
            for h in range(4):
                add_cast(w_in[kt * 128:(kt + 1) * 128, h * 1024:(h + 1) * 1024], 1024,
                         (lambda kt=kt, h=h: win_s[kt * 128:(kt + 1) * 128, h * 1024:(h + 1) * 1024]), "win_s")
        for kt in range(4):
            add_cast(w_glu[kt * 128:(kt + 1) * 128, :], 512, (lambda kt=kt: wglu_s[kt * 128:(kt + 1) * 128, :]), "wglu_s")
        for kt in range(4):
            add_cast(w_a[kt * 128:(kt + 1) * 128, :], 1024, (lambda kt=kt: wa_s[kt * 128:(kt + 1) * 128, :]), "wa_s")
        for kt in range(4):
            add_cast(w_b[kt * 128:(kt + 1) * 128, :], 1024, (lambda kt=kt: wb_s[kt * 128:(kt + 1) * 128, :]), "wb_s")
        for kt in range(8):
            add_cast(w_o[kt * 128:(kt + 1) * 128, :], 1024, (lambda kt=kt: wo_s[kt * 128:(kt + 1) * 128, :]), "wo_s")
        for ft in range(NFT):
            add_cast(wd[ft * 128:(ft + 1) * 128, :], 1024, (lambda ft=ft: wd_s[ft * 128:(ft + 1) * 128, :]), "wd_s")
        for (wsrc, wdst, key) in ((wg, wg_s, "wg_s"), (wu, wu_s, "wu_s")):
            for kt in range(8):
                for h in range(2):
                    add_cast(wsrc[kt * 128:(kt + 1) * 128, h * 1408:(h + 1) * 1408], 1408,
                             (lambda kt=kt, h=h, wdst=wdst: wdst[h * 11:(h + 1) * 11, :, kt, :].rearrange("ft p f -> p ft f")),
                             key)

        def cast_step(n):
            for _ in range(n):
                j = cast_pos[0]
                if j >= len(cast_jobs):
                    return
                cast_pos[0] += 1
                src, ncols, dst_fn, key = cast_jobs[j]
                r = j % 8
                dst = dst_fn()
                if len(dst.shape) == 3:
                    srcv = src.rearrange("p (ft f) -> p ft f", f=128)
                else:
                    srcv = src
                deps = [cast_keys[j - 8]] if j >= 8 else []
                p.dma(lambda e, dst=dst, srcv=srcv: e.dma_start(out=dst, in_=srcv, max_dma_last_dim=1024), f"st_cast{r}",
                      reads=deps, writes=[f"{key}#{j}"], q="pool")
                cast_keys.append(f"{key}#{j}")
                cast_done.setdefault(key, []).append(f"{key}#{j}")

        cast_keys = []

        def cast_until(key_names):
            need = [i for i, jb in enumerate(cast_jobs) if jb[3] in key_names]
            if need:
                last = max(need)
                if cast_pos[0] <= last:
                    cast_step(last + 1 - cast_pos[0])
            ks = []
            for k in key_names:
                ks += cast_done.get(k, [])
            return ks

        def load_w(tile, src_ap, key, deps, semname):
            p.dma(lambda e: e.dma_start(out=tile, in_=src_ap), semname, reads=deps, writes=[key])

        cT = sbt(top, "cT", [128, 8], F32)
        condf = sbt(top, "condf", [128, 8], F32)
        cond_rep = sbt(top, "cond_rep", [128, 8, 128], F32)
        b_adaT = sbt(top, "b_adaT", [128, 48], F32)
        n1gT = sbt(top, "n1gT", [128, 8], F32)
        n2gT = sbt(top, "n2gT", [128, 8], F32)
        ones_row = sbt(top, "ones_row", [1, 128], F32)
        cload(cT[:], cT_d, "cT")
        cload(b_adaT[:], b_adaT_d, "b_adaT")
        cload(n1gT[:], n1gT_d, "n1gT")
        cload(n2gT[:], n2gT_d, "n2gT")
        cload(ones_row[:], ones_row_d, "ones_row")
        p.op("act", lambda e: e.activation(out=condf[:], in_=cT[:], func=AF.Silu), reads=["cT"], writes=["condf"])
        p.op("dve", lambda e: e.tensor_copy(out=cond_rep[:], in_=bc_last(condf[:], 128)), reads=["condf"], writes=["cond_rep"])
        w_ada_v = w_ada.rearrange("(kt p) c -> p kt c", p=128)
        slot_of = {0: 0, 1: 1, 3: 2, 4: 3}

        def ada_half(grp, h, wt, wtk, ps, pk, brow, browk):
            if grp in slot_of:
                sl = slot_of[grp]
                for ct in range(4):
                    for kt in range(8):
                        p.op("pe", lambda e, ct=ct, kt=kt: e.matmul(
                            ps[:, 2 * ct:2 * ct + 2], lhsT=wt[:, kt, ct * 128:(ct + 1) * 128],
                            rhs=cond_rep[:, kt, 0:2], start=(kt == 0), stop=(kt == 7)),
                             reads=[wtk, "cond_rep"], writes=[pk])
                for ct in range(4):
                    col = grp * 8 + h * 4 + ct
                    p.op("act", lambda e, ct=ct, col=col: e.activation(
                        out=modT[:, sl, h * 4 + ct:h * 4 + ct + 1], in_=ps[:, 2 * ct:2 * ct + 1], func=AF.Identity, bias=b_adaT[:, col:col + 1]),
                         reads=[pk, "b_adaT"], writes=[f"modT{sl}_{h}" if ct == 3 else newkey("modT")])
            else:
                dst = g1bc if grp == 2 else g2bc
                dk = "g1bc" if grp == 2 else "g2bc"
                for kt in range(8):
                    p.op("pe", lambda e, kt=kt: e.matmul(
                        ps[:, :], lhsT=cond_rep[:, kt, :], rhs=wt[:, kt, :],
                        start=(kt == 0), stop=False), reads=[wtk, "cond_rep"], writes=[pk])
                p.op("pe", lambda e: e.matmul(
                    ps[:, :], lhsT=ones_row[0:1, :], rhs=brow[0:1, :],
                    start=False, stop=True), reads=["ones_row", browk], writes=[pk])
                p.op("act", lambda e: e.activation(out=dst[:, h * 512:(h + 1) * 512], in_=ps[:, :], func=AF.Copy),
                     reads=[pk], writes=[f"{dk}_{h}"])

        def ada_load(grp, h, wt, wtk, sem, brow=None, browk=None, bsem=None):
            p.dma(lambda e: e.dma_start(out=wt[:], in_=w_ada_v[:, :, grp * 1024 + h * 512: grp * 1024 + (h + 1) * 512]),
                  sem, writes=[wtk])
            if grp not in slot_of:
                p.dma(lambda e: e.dma_start(out=brow[0:1, :], in_=b_ada_row[0:1, grp * 1024 + h * 512: grp * 1024 + (h + 1) * 512]),
                      bsem, writes=[browk])

        def normA(xtile, xn4, xnk, extra_reads=(), scale_eng="dve"):
            for tt in range(4):
                xa, xk = xtile(tt)
                ss, ssk = newvec()
                sd, sdk = newvec()
                rs, rsk = newvec()
                p.op("act", lambda e, xa=xa, ss=ss: e.activation(out=junk[:], in_=xa, func=AF.Square, accum_out=ss),
                     reads=[xk] + list(extra_reads), writes=[ssk, "junk"])
                p.op("act", lambda e, ss=ss, sd=sd: e.activation(out=sd, in_=ss, func=AF.Sqrt, scale=1.0 / D, bias=epsv[:, 0:1]),
                     reads=[ssk, "epsv"], writes=[sdk])
                p.op("dve", lambda e, sd=sd, rs=rs: e.reciprocal(out=rs, in_=sd), reads=[sdk], writes=[rsk])
                if scale_eng == "act":
                    p.op("act", lambda e, tt=tt, xa=xa, rs=rs: e.activation(out=xn4[:, tt, :], in_=xa, func=AF.Identity, scale=rs),
                         reads=[xk, rsk], writes=[f"{xnk}_{tt}"])
                else:
                    p.op("dve", lambda e, tt=tt, xa=xa, rs=rs: e.tensor_scalar(out=xn4[:, tt, :], in0=xa, scalar1=rs, scalar2=None, op0=ALU.mult),
                         reads=[xk, rsk], writes=[f"{xnk}_{tt}"])

        def normB(ctx, xn4, xnk, hT, hTk, geff, geffk, sh, shk, col0=0, perm_half=None):
            psb = ctx["psb"]
            for kp in range(4):
                b = ctx["psb_ctr"][0] % len(psb)
                ctx["psb_ctr"][0] += 1
                pb = psb[b]; pbk = f"{ctx['name']}psb{b}"
                for kk in range(2):
                    kt = 2 * kp + kk
                    for tt in range(4):
                        p.op("pe", lambda e, pb=pb, kk=kk, tt=tt, kt=kt: e.transpose(
                            pb[:, kk * 512 + tt * 128: kk * 512 + (tt + 1) * 128], xn4[:, tt, kt * 128:(kt + 1) * 128], identb[:]),
                             reads=[f"{xnk}_{tt}", "identb"], writes=[pbk])
                for kk in range(2):
                    kt = 2 * kp + kk
                    if perm_half is None:
                        oap = hT[:, kt, col0:col0 + 512]
                        iap = pb[:, kk * 512:(kk + 1) * 512]
                    else:
                        oap = hT[:, kt, :].rearrange("p (i n) -> p i n", i=8)[:, :, 64 * perm_half:64 * perm_half + 64]
                        iap = pb[:, kk * 512:(kk + 1) * 512].rearrange("p (n i) -> p i n", i=8)
                    if kp % 2 == 0:
                        p.op("dve", lambda e, oap=oap, iap=iap, kt=kt: e.tensor_scalar(
                            out=oap, in0=iap,
                            scalar1=geff[:, kt:kt + 1], scalar2=sh[:, kt:kt + 1], op0=ALU.mult, op1=ALU.add),
                             reads=[pbk, geffk, shk], writes=[f"{hTk}_{kt}"])
                    else:
                        p.op("act", lambda e, oap=oap, iap=iap, kt=kt: e.activation(
                            out=oap, in_=iap, func=AF.Identity,
                            scale=geff[:, kt:kt + 1], bias=sh[:, kt:kt + 1]),
                             reads=[pbk, geffk, shk], writes=[f"{hTk}_{kt}"])

        xring_ctr = [0]

        def make_xloader(xt_tiles, prefix, rows_ap):
            cache = {}

            def xtile(tt):
                if tt not in cache:
                    i = xring_ctr[0] % len(xt_tiles)
                    xring_ctr[0] += 1
                    key = f"{prefix}{i}"
                    p.dma(lambda e, i=i, tt=tt: e.dma_start(out=xt_tiles[i][:], in_=rows_ap[tt * 128:(tt + 1) * 128, :]),
                          f"ld_x{i}", writes=[key])
                    cache[tt] = (xt_tiles[i][:], key)
                return cache[tt]

            return xtile

        s5o = ExitStack()
        Mbf = sbt(s5o, "Mbf", [128, G, 128], BF16)
        Gre = sbt(s5o, "Gre", [128, G, 64], BF16)
        Gim = sbt(s5o, "Gim", [128, G, 64], BF16)
        Hre = sbt(s5o, "Hre", [128, 16, 128], BF16)
        Hni = sbt(s5o, "Hni", [128, 16, 128], BF16)
        dec8 = sbt(s5o, "dec8", [128, 16], F32)
        c8 = sbt(s5o, "c8", [128, 16], F32)
        s8 = sbt(s5o, "s8", [128, 16], F32)
        lamre = sbt(s5o, "lamre", [128, 16], F32)
        lamim = sbt(s5o, "lamim", [128, 16], F32)
        logdt = sbt(s5o, "logdt", [128, 16], F32)
        bre = sbt(s5o, "bre", [128, 16, 16], F32)
        bim = sbt(s5o, "bim", [128, 16, 16], F32)
        cre = sbt(s5o, "cre", [128, 16, 16], F32)
        cim = sbt(s5o, "cim", [128, 16, 16], F32)
        dvec = sbt(s5o, "dvec", [128, 32], F32)
        cmask = sbt(s5o, "cmask", [128, 128], F32)
        kv = sbt(s5o, "kv", [128, 32], F32)
        hmask = sbt(s5o, "hmask", [128, 2], F32)
        cload(hmask[:], hmask_d, "hmask")
        for t_, d_, k_ in ((lamre, lamre_d, "lamre"), (lamim, lamim_d, "lamim"), (logdt, logdt_d, "logdt"),
                           (bre, bre_d, "bre"), (bim, bim_d, "bim"), (cre, cre_d, "cre"), (cim, cim_d, "cim"),
                           (dvec, dvec_d, "dvec"), (cmask, cmask_d, "cmask"), (kv, kv_d, "kv")):
            cload(t_[:], d_, k_)
        p0s = ExitStack()
        if True:
            wst = [sbt(p0s, f"wst{i}", [128, 8, 512], F32) for i in range(2)]
            psm = [pst(p0s, f"psm{i}", [128, 512]) for i in range(4)]
            halves1 = ((1, 0), (1, 1), (0, 0), (0, 1), (2, 0), (2, 1), (4, 0), (4, 1), (3, 0), (3, 1), (5, 0), (5, 1))
            brow = [sbt(p0s, f"brow{i}", [1, 512], F32) for i in range(2)]
            for gi in range(2):
                ada_load(halves1[gi][0], halves1[gi][1], wst[gi], f"wst{gi}", f"ld_w{gi}", brow[gi], f"brow{gi}", f"ld_w{4 + gi}")

        def emit_adaln():
            for gi in range(12):
                ada_half(halves1[gi][0], halves1[gi][1], wst[gi % 2], f"wst{gi % 2}", psm[gi % 4], f"psm{gi % 4}", brow[gi % 2], f"brow{gi % 2}")
                if gi + 2 < 12:
                    ada_load(halves1[gi + 2][0], halves1[gi + 2][1], wst[gi % 2], f"wst{gi % 2}", f"ld_w{gi % 2}", brow[gi % 2], f"brow{gi % 2}", f"ld_w{4 + gi % 2}")
                if gi == 3:
                    p.op("dve", lambda e: e.scalar_tensor_tensor(out=geff1[:], in0=modT[:, 1, :], scalar=1.0, in1=n1gT[:], op0=ALU.add, op1=ALU.mult),
                         reads=["modT1_0", "modT1_1", "n1gT"], writes=["geff1"])
            p.op("dve", lambda e: e.scalar_tensor_tensor(out=geff2[:], in0=modT[:, 3, :], scalar=1.0, in1=n2gT[:], op0=ALU.add, op1=ALU.mult),
                 reads=["modT3_0", "modT3_1", "n2gT"], writes=["geff2"])

        cast_step(32)
        with ExitStack() as st:

            def small(name, shape=(128, 16)):
                return sbt(st, "s_" + name, list(shape), F32)

            dt_ = small("dt"); a_ = small("a"); phi = small("phi")
            p.op("act", lambda e: e.activation(out=dt_[:], in_=logdt[:], func=AF.Exp), reads=["logdt"], writes=["dt"])
            p.op("dve", lambda e: e.tensor_tensor(out=a_[:], in0=lamre[:], in1=dt_[:], op=ALU.mult), reads=["lamre", "dt"], writes=["a"])
            p.op("dve", lambda e: e.tensor_tensor(out=phi[:], in0=lamim[:], in1=dt_[:], op=ALU.mult), reads=["lamim", "dt"], writes=["phi"])
            AR = small("AR", (128, 16, 32)); ANG = small("ANG", (128, 16, 32)); RHO = small("RHO", (128, 16, 32))
            SINt = small("SINt", (128, 16, 32)); COSt = small("COSt", (128, 16, 32))
            PRE = small("PRE", (128, 16, 32)); PIM = small("PIM", (128, 16, 32))
            tA = small("tA", (128, 16, 32)); tB = small("tB", (128, 16, 32))
            kvb = kv[:].rearrange("p (o k) -> p o k", o=1).to_broadcast([128, 16, 32])
            p.op("dve", lambda e: e.tensor_tensor(out=AR[:], in0=bc_last(a_[:], 32), in1=kvb, op=ALU.mult), reads=["a", "kv"], writes=["AR"])
            p.op("dve", lambda e: e.tensor_tensor(out=ANG[:], in0=bc_last(phi[:], 32), in1=kvb, op=ALU.mult), reads=["phi", "kv"], writes=["ANG"])
            p.op("act", lambda e: e.activation(out=RHO[:], in_=AR[:], func=AF.Exp), reads=["AR"], writes=["RHO"])

            def range_reduce_sin(src, srck, dst, dstk, shift):
                p.op("dve", lambda e: e.tensor_scalar(out=tA[:], in0=src[:], scalar1=shift, scalar2=None, op0=ALU.add),
                     reads=[srck], writes=["tA"])
                p.op("dve", lambda e: e.tensor_scalar(out=tB[:], in0=tA[:], scalar1=1.0 / (2 * math.pi), scalar2=MAGIC, op0=ALU.mult, op1=ALU.add),
                     reads=["tA"], writes=["tB"])
                p.op("dve", lambda e: e.tensor_scalar(out=tB[:], in0=tB[:], scalar1=MAGIC, scalar2=None, op0=ALU.subtract),
                     reads=["tB"], writes=["tB"])
                p.op("dve", lambda e: e.scalar_tensor_tensor(out=tA[:], in0=tB[:], scalar=-C1, in1=tA[:], op0=ALU.mult, op1=ALU.add),
                     reads=["tB", "tA"], writes=["tA"])
                p.op("dve", lambda e: e.scalar_tensor_tensor(out=tA[:], in0=tB[:], scalar=-C2, in1=tA[:], op0=ALU.mult, op1=ALU.add),
                     reads=["tB", "tA"], writes=["tA"])
                p.op("dve", lambda e: e.tensor_scalar(out=tA[:], in0=tA[:], scalar1=-PI_LO, scalar2=PI_LO, op0=ALU.max, op1=ALU.min),
                     reads=["tA"], writes=["tA"])
                p.op("act", lambda e: e.activation(out=dst[:], in_=tA[:], func=AF.Sin), reads=["tA"], writes=[dstk])

            range_reduce_sin(ANG, "ANG", SINt, "SINt", 0.0)
            range_reduce_sin(ANG, "ANG", COSt, "COSt", math.pi / 2)
            p.op("dve", lambda e: e.tensor_tensor(out=PRE[:], in0=RHO[:], in1=COSt[:], op=ALU.mult), reads=["RHO", "COSt"], writes=["PRE"])
            p.op("dve", lambda e: e.tensor_tensor(out=PIM[:], in0=RHO[:], in1=SINt[:], op=ALU.mult), reads=["RHO", "SINt"], writes=["PIM"])
            ckpt()
            nr = small("nr"); den = small("den"); t1s = small("t1s"); t2s = small("t2s")
            bre_s = small("betare"); bim_s = small("betaim")
            lbre = PRE[:, :, 24]; lbim = PIM[:, :, 24]
            p.op("dve", lambda e: e.tensor_scalar(out=nr[:], in0=lbre, scalar1=-1.0, scalar2=None, op0=ALU.add), reads=["PRE"], writes=["nr"])
            p.op("dve", lambda e: e.tensor_tensor(out=den[:], in0=lamre[:], in1=lamre[:], op=ALU.mult), reads=["lamre"], writes=["den"])
            p.op("dve", lambda e: e.tensor_tensor(out=t1s[:], in0=lamim[:], in1=lamim[:], op=ALU.mult), reads=["lamim"], writes=["t1s"])
            p.op("dve", lambda e: e.tensor_tensor(out=den[:], in0=den[:], in1=t1s[:], op=ALU.add), reads=["den", "t1s"], writes=["den"])
            p.op("dve", lambda e: e.reciprocal(out=den[:], in_=den[:]), reads=["den"], writes=["den"])
            p.op("dve", lambda e: e.tensor_tensor(out=t1s[:], in0=nr[:], in1=lamre[:], op=ALU.mult), reads=["nr", "lamre"], writes=["t1s"])
            p.op("dve", lambda e: e.tensor_tensor(out=t2s[:], in0=lbim, in1=lamim[:], op=ALU.mult), reads=["PIM", "lamim"], writes=["t2s"])
            p.op("dve", lambda e: e.tensor_tensor(out=t1s[:], in0=t1s[:], in1=t2s[:], op=ALU.add), reads=["t1s", "t2s"], writes=["t1s"])
            p.op("dve", lambda e: e.tensor_tensor(out=bre_s[:], in0=t1s[:], in1=den[:], op=ALU.mult), reads=["t1s", "den"], writes=["betare"])
            p.op("dve", lambda e: e.tensor_tensor(out=t1s[:], in0=lbim, in1=lamre[:], op=ALU.mult), reads=["PIM", "lamre"], writes=["t1s"])
            p.op("dve", lambda e: e.tensor_tensor(out=t2s[:], in0=nr[:], in1=lamim[:], op=ALU.mult), reads=["nr", "lamim"], writes=["t2s"])
            p.op("dve", lambda e: e.tensor_tensor(out=t1s[:], in0=t1s[:], in1=t2s[:], op=ALU.subtract), reads=["t1s", "t2s"], writes=["t1s"])
            p.op("dve", lambda e: e.tensor_tensor(out=bim_s[:], in0=t1s[:], in1=den[:], op=ALU.mult), reads=["t1s", "den"], writes=["betaim"])
            big1 = sbt(st, "big1", [128, 16, 8, 16], F32)
            big2 = sbt(st, "big2", [128, 16, 8, 16], F32)

            def cmul(outre, outrek, outim, outimk, are, arek, aim, aimk, bre_, brek, bim_, bimk, shape, neg_im=False, eng="dve"):
                n = 1
                for s_ in shape[1:]:
                    n *= s_
                if len(shape) == 3:
                    t1 = big1[:].rearrange("p a b c -> p (a b c)")[:, 0:n].rearrange("p (a b) -> p a b", b=shape[2])
                    t2 = big2[:].rearrange("p a b c -> p (a b c)")[:, 0:n].rearrange("p (a b) -> p a b", b=shape[2])
                else:
                    t1 = big1[:]
                    t2 = big2[:]
                p.op(eng, lambda e: e.tensor_tensor(out=t1, in0=are, in1=bre_, op=ALU.mult), reads=[arek, brek], writes=["big1"])
                p.op(eng, lambda e: e.tensor_tensor(out=t2, in0=aim, in1=bim_, op=ALU.mult), reads=[aimk, bimk], writes=["big2"])
                p.op(eng, lambda e: e.tensor_tensor(out=outre, in0=t1, in1=t2, op=ALU.subtract), reads=["big1", "big2"], writes=[outrek])
                p.op(eng, lambda e: e.tensor_tensor(out=t1, in0=are, in1=bim_, op=ALU.mult), reads=[arek, bimk], writes=["big1"])
                p.op(eng, lambda e: e.tensor_tensor(out=t2, in0=aim, in1=bre_, op=ALU.mult), reads=[aimk, brek], writes=["big2"])
                if neg_im:
                    p.op(eng, lambda e: e.scalar_tensor_tensor(out=outim, in0=t1, scalar=-1.0, in1=t2, op0=ALU.mult, op1=ALU.subtract),
                         reads=["big1", "big2"], writes=[outimk])
                else:
                    p.op(eng, lambda e: e.tensor_tensor(out=outim, in0=t1, in1=t2, op=ALU.add), reads=["big1", "big2"], writes=[outimk])

            Bre = small("Bre", (128, 16, 16)); Bim = small("Bim", (128, 16, 16))
            cmul(Bre[:], "Bre", Bim[:], "Bim", bc_last(bre_s[:], 16), "betare", bc_last(bim_s[:], 16), "betaim",
                 bre[:], "bre", bim[:], "bim", (128, 16, 16))
            ckpt()
            Xre = sbt(st, "Xre", [128, 16, 8, 16], F32); Xim = sbt(st, "Xim", [128, 16, 8, 16], F32)
            XGre = sbt(st, "XGre", [128, 16, 8, 16], F32); XGim = sbt(st, "XGim", [128, 16, 8, 16], F32)
            Yre = sbt(st, "Yre", [128, 16, 8, 16], F32); nYim = sbt(st, "nYim", [128, 16, 8, 16], F32)
            sh4 = (128, 16, 8, 16)

            def pw(tab, lo):
                return bc_last(tab[:, :, lo:lo + 8], 16)

            def mid(t):
                return t[:].rearrange("p q (o c) -> p q o c", o=1).to_broadcast([128, 16, 8, 16])

            cmul(Xre[:], "Xre", Xim[:], "Xim", pw(PRE, 0), "PRE", pw(PIM, 0), "PIM", mid(Bre), "Bre", mid(Bim), "Bim", sh4)
            cmul(XGre[:], "XGre", XGim[:], "XGim", pw(PRE, 8), "PRE", pw(PIM, 8), "PIM", mid(Bre), "Bre", mid(Bim), "Bim", sh4)
            cmul(Yre[:], "Yre", nYim[:], "nYim", pw(PRE, 16), "PRE", pw(PIM, 16), "PIM", mid(cre), "cre", mid(cim), "cim", sh4, neg_im=True)
            Hre4 = Hre[:].rearrange("p q (j o) -> p q j o", o=16)
            Hni4 = Hni[:].rearrange("p q (j o) -> p q j o", o=16)
            cmul(Hre4, "Hre", Hni4, "Hni", pw(PRE, 24), "PRE", pw(PIM, 24), "PIM", mid(cre), "cre", mid(cim), "cim", sh4, neg_im=True)
            emit_adaln()
            psS = [pst(st, f"psS{i}", [128, 512]) for i in range(4)]
            pctr = 0
            b1f = big1[:].rearrange("p a b c -> p (a b c)")
            Ym = sbt(st, "Ym", [128, 16, 8, 16], F32)
            nYm = sbt(st, "nYm", [128, 16, 8, 16], F32)
            for hf in range(2):
                p.op("dve", lambda e, hf=hf: e.tensor_scalar(out=Ym[:], in0=Yre[:], scalar1=hmask[:, hf:hf + 1], scalar2=None, op0=ALU.mult),
                     reads=["Yre", "hmask"], writes=["Ym"])
                p.op("dve", lambda e, hf=hf: e.tensor_scalar(out=nYm[:], in0=nYim[:], scalar1=hmask[:, hf:hf + 1], scalar2=None, op0=ALU.mult),
                     reads=["nYim", "hmask"], writes=["nYm"])
                for q4 in range(4):
                    ps = psS[pctr % 4]; pk = f"psS{pctr % 4}"; pctr += 1
                    for gg in range(4):
                        q = q4 * 4 + gg
                        p.op("pe", lambda e, ps=ps, gg=gg, q=q: e.matmul(
                            ps[:, gg * 128:(gg + 1) * 128], lhsT=Xre[:, q, :, :].rearrange("p i c -> p (i c)"),
                            rhs=Ym[:, q, :, :].rearrange("p j o -> p (j o)"), start=True, stop=False),
                             reads=["Xre", "Ym"], writes=[pk])
                        p.op("pe", lambda e, ps=ps, gg=gg, q=q: e.matmul(
                            ps[:, gg * 128:(gg + 1) * 128], lhsT=Xim[:, q, :, :].rearrange("p i c -> p (i c)"),
                            rhs=nYm[:, q, :, :].rearrange("p j o -> p (j o)"), start=False, stop=True),
                             reads=["Xim", "nYm"], writes=[pk])
                    for gg in range(4):
                        q = q4 * 4 + gg
                        g = 2 * q + hf
                        p.op("dve", lambda e, ps=ps, gg=gg: e.tensor_tensor(
                            out=b1f[:, gg * 128:(gg + 1) * 128], in0=ps[:, gg * 128:(gg + 1) * 128], in1=cmask[:], op=ALU.mult),
                             reads=[pk, "cmask"], writes=["big1"])
                        p.op("dve", lambda e, g=g, gg=gg: e.scalar_tensor_tensor(
                            out=Mbf[:, g, :], in0=identf[:], scalar=dvec[:, g:g + 1],
                            in1=b1f[:, gg * 128:(gg + 1) * 128], op0=ALU.mult, op1=ALU.add),
                             reads=["identf", "dvec", "big1"], writes=["Mbf"])
            ckpt()
            for (src, srck, dstt, dstk) in ((XGre, "XGre", Gre, "Gre"), (XGim, "XGim", Gim, "Gim")):
                for q8 in range(4):
                    ps = psS[pctr % 4]; pk = f"psS{pctr % 4}"; pctr += 1
                    for qq in range(4):
                        q = q8 * 4 + qq
                        p.op("pe", lambda e, ps=ps, qq=qq, q=q, src=src: e.transpose(
                            ps[:, qq * 128:(qq + 1) * 128], src[:, q, :, :].rearrange("p i c -> p (i c)"), identf[:]),
                             reads=[srck, "identf"], writes=[pk])
                    p.op("act", lambda e, ps=ps, q8=q8, dstt=dstt: e.activation(
                        out=dstt[:, q8 * 8:(q8 + 1) * 8, :].rearrange("p g c -> p (g c)"), in_=ps[:, :], func=AF.Copy),
                         reads=[pk], writes=[dstk])
            ckpt()
            p.op("dve", lambda e: e.tensor_copy(out=dec8[:], in_=RHO[:, :, 31]), reads=["RHO"], writes=["dec8"])
            p.op("dve", lambda e: e.tensor_copy(out=c8[:], in_=COSt[:, :, 31]), reads=["COSt"], writes=["c8"])
            p.op("dve", lambda e: e.tensor_copy(out=s8[:], in_=SINt[:, :, 31]), reads=["SINt"], writes=["s8"])
            phase_end([("PRE", PRE[:].rearrange("p a b -> p (a b)"), [128, 512], F32),
                       ("PIM", PIM[:].rearrange("p a b -> p (a b)"), [128, 512], F32),
                       ("Mbf", Mbf[:].rearrange("p g c -> p (g c)"), [128, G * 128], BF16),
                       ("Gre", Gre[:].rearrange("p g c -> p (g c)"), [128, G * 64], BF16),
                       ("Hre", Hre[:].rearrange("p g c -> p (g c)"), [128, 16 * 128], BF16)])
        p0s.close()

        with ExitStack() as s5:
            Ytm = sbt(s5, "Ytm", [128, NB, 8, 512], BF16)
            with ExitStack() as s5a:
                U = sbt(s5a, "U", [128, NB, G, 8, 16], BF16)
                with ExitStack() as st:
                    ctx = {"name": "p1a", "psb_ctr": [0],
                           "psb": [pst(st, f"p1a_psb{i}", [128, 1024], BF16) for i in range(2)]}
                    psf = [pst(st, f"p1a_psf{i}", [128, 512]) for i in range(6)]
                    xt = [sbt(st, f"p1a_xt{i}", [128, D], F32) for i in range(8)]
                    xn4 = [sbt(st, f"p1a_xn{i}", [128, 4, D], BF16) for i in range(2)]
                    hT = [sbt(st, f"p1a_hT{i}", [128, 8, 1024], BF16) for i in range(1)]
                    wU = sbt(st, "wU", [128, 8, 512], BF16)
                    wUs = sbt(st, "wUs", [128, 4, 512], F32)
                    for hh in range(2):
                        p.dma(lambda e, hh=hh: e.dma_start(out=wUs[:], in_=w_in.rearrange("(kt p) c -> p kt c", p=128)[:, hh * 4:(hh + 1) * 4, 0:512]),
                              "ld_w0", writes=["wUs"])
                        p.op("act", lambda e, hh=hh: e.activation(out=wU[:, hh * 4:(hh + 1) * 4, :].rearrange("p a b -> p (a b)"),
                                                                  in_=wUs[:].rearrange("p a b -> p (a b)"), func=AF.Copy),
                             reads=["wUs"], writes=[f"wU{hh}"])
                    pf = 0
                    normA(make_xloader(xt, "p1a_xt", x_all[0:512, :]), xn4[0], "p1a_xn0")
                    ckpt()
                    for c in range(8):
                        nbk = c // 2
                        r = c % 2
                        hb = hT[0]; hbk = f"p1a_hT0_{c % 2}"
                        normB(ctx, xn4[r], f"p1a_xn{r}", hb, hbk, geff1, "geff1", sh1, "modT0_1", perm_half=(c % 2))
                        if c == 0:
                            ckpt()
                        if c + 1 < 8:
                            r2 = (c + 1) % 2
                            normA(make_xloader(xt, "p1a_xt", x_all[(c + 1) * 512:(c + 2) * 512, :]), xn4[r2], f"p1a_xn{r2}")
                        if c % 2 == 1:
                            hk0 = "p1a_hT0_0"; hk1 = "p1a_hT0_1"
                            for i in range(8):
                                ps = psf[pf % 6]; pk = f"p1a_psf{pf % 6}"; pf += 1
                                for kt in range(8):
                                    p.op("pe", lambda e, ps=ps, hb=hb, kt=kt, i=i: e.matmul(
                                        ps[:, :], lhsT=hb[:, kt, i * 128:(i + 1) * 128], rhs=wU[:, kt, :], start=(kt == 0), stop=(kt == 7)),
                                         reads=[f"{hk0}_{kt}", f"{hk1}_{kt}", "wU0", "wU1"], writes=[pk])
                                if i % 2 == 0:
                                    p.op("act", lambda e, ps=ps, nbk=nbk, i=i: e.activation(
                                        out=U[:, nbk, :, i, :], in_=ps[:, :].rearrange("p (g c) -> p g c", c=16), func=AF.Copy),
                                         reads=[pk], writes=[newkey("U")])
                                else:
                                    p.op("dve", lambda e, ps=ps, nbk=nbk, i=i: e.tensor_copy(
                                        out=U[:, nbk, :, i, :], in_=ps[:, :].rearrange("p (g c) -> p g c", c=16)),
                                         reads=[pk], writes=[newkey("U")])
                        if c == 1:
                            ckpt()
                        cast_step(2)
                    phase_end([("U", U[:].rearrange("p a g i c -> p (a g i c)"), [128, NB * G * 128], BF16)])

                CS = sbt(s5a, "CS", [128, 8, 512], F32)
                SN = sbt(s5a, "SN", [128, 8, 512], F32)
                with ExitStack() as st:
                    Ug = [[sbt(st, f"Ug{r}_{h}", [128, 512], BF16) for h in range(2)] for r in range(2)]
                    psb = [pst(st, f"p2_psb{i}", [128, 1024], BF16) for i in range(2)]
                    psZ = [pst(st, f"p2_psZ{i}", [128, 512]) for i in range(4)]
                    psY = [pst(st, f"p2_psY{i}", [128, 512]) for i in range(2)]
                    tmp = [sbt(st, f"p2_tmp{i}", [128, 512], F32) for i in range(4)]
                    Win = [sbt(st, f"Win{c}", [128, 512], F32) for c in range(2)]
                    Wst = [sbt(st, f"Wst{c}", [128, 512], F32) for c in range(2)]
                    Sp = [[sbt(st, f"Sp{r}_{c}", [128, 512], BF16) for c in range(2)] for r in range(2)]
                    dt1 = sbt(st, "dt1", [128, 8, 256], F32)
                    dt2 = sbt(st, "dt2", [128, 8, 256], F32)
                    cmt = [sbt(st, f"cmt{i}", [128, 8], F32) for i in range(2)]
                    smt = [sbt(st, f"smt{i}", [128, 8], F32) for i in range(2)]
                    sq1 = sbt(st, "sq1", [128, 8], F32)
                    sq2 = sbt(st, "sq2", [128, 8], F32)
                    for r in range(2):
                        for c in range(2):
                            p.op("dve", lambda e, r=r, c=c: e.memset(Sp[r][c][:, 0:1], 0.0), writes=[f"Sp{r}_{c}"])
                    yctr = 0
                    for half in range(2):
                        q0 = half * 8
                        p.op("dve", lambda e, q0=q0: e.tensor_copy(out=cmt[0][:], in_=c8[:, q0:q0 + 8]), reads=["c8"], writes=["cmt0"])
                        p.op("dve", lambda e, q0=q0: e.tensor_copy(out=smt[0][:], in_=s8[:, q0:q0 + 8]), reads=["s8"], writes=["smt0"])
                        p.op("dve", lambda e: e.memset(CS[:, :, 0:1], 1.0), writes=["CS"] + [f"CS{qq}" for qq in range(8)])
                        p.op("dve", lambda e: e.memset(SN[:, :, 0:1], 0.0), writes=["SN"] + [f"SN{qq}" for qq in range(8)])
                        m = 1
                        it = 0
                        while m < 512:
                            c_, s_ = cmt[it % 2], smt[it % 2]
                            ck, sk = f"cmt{it % 2}", f"smt{it % 2}"
                            if m < 32:
                                t1 = dt1[:, :, 0:m]
                                t2 = dt2[:, :, 0:m]
                                cb = bc_last(c_[:], m); sb_ = bc_last(s_[:], m)
                                p.op("dve", lambda e, t1=t1, cb=cb, m=m: e.tensor_tensor(out=t1, in0=CS[:, :, 0:m], in1=cb, op=ALU.mult), reads=["CS", ck], writes=["dt1"])
                                p.op("dve", lambda e, t2=t2, sb_=sb_, m=m: e.tensor_tensor(out=t2, in0=SN[:, :, 0:m], in1=sb_, op=ALU.mult), reads=["SN", sk], writes=["dt2"])
                                p.op("dve", lambda e, t1=t1, t2=t2, m=m: e.tensor_tensor(out=CS[:, :, m:2 * m], in0=t1, in1=t2, op=ALU.subtract), reads=["dt1", "dt2"], writes=["CS"])
                                p.op("dve", lambda e, t1=t1, cb=cb, m=m: e.tensor_tensor(out=t1, in0=SN[:, :, 0:m], in1=cb, op=ALU.mult), reads=["SN", ck], writes=["dt1"])
                                p.op("dve", lambda e, t2=t2, sb_=sb_, m=m: e.tensor_tensor(out=t2, in0=CS[:, :, 0:m], in1=sb_, op=ALU.mult), reads=["CS", sk], writes=["dt2"])
                                p.op("dve", lambda e, t1=t1, t2=t2, m=m: e.tensor_tensor(out=SN[:, :, m:2 * m], in0=t1, in1=t2, op=ALU.add), reads=["dt1", "dt2"], writes=["SN"])
                            else:
                                for qq in range(8):
                                    p.op("act", lambda e, qq=qq, m=m, s_=s_: e.activation(out=dt1[:, qq, 0:m], in_=SN[:, qq, 0:m], func=AF.Identity, scale=s_[:, qq:qq + 1]),
                                         reads=["SN", f"SN{qq}", sk, "dt1"], writes=[f"dt1_{qq}"])
                                    p.op("act", lambda e, qq=qq, m=m, s_=s_: e.activation(out=dt2[:, qq, 0:m], in_=CS[:, qq, 0:m], func=AF.Identity, scale=s_[:, qq:qq + 1]),
                                         reads=["CS", f"CS{qq}", sk, "dt2"], writes=[f"dt2_{qq}"])
                                for qq in range(8):
                                    p.op("dve", lambda e, qq=qq, m=m, c_=c_: e.scalar_tensor_tensor(
                                        out=CS[:, qq, m:2 * m], in0=CS[:, qq, 0:m], scalar=c_[:, qq:qq + 1], in1=dt1[:, qq, 0:m], op0=ALU.mult, op1=ALU.subtract),
                                         reads=["CS", f"CS{qq}", ck, f"dt1_{qq}"], writes=[f"CS{qq}"])
                                    p.op("dve", lambda e, qq=qq, m=m, c_=c_: e.scalar_tensor_tensor(
                                        out=SN[:, qq, m:2 * m], in0=SN[:, qq, 0:m], scalar=c_[:, qq:qq + 1], in1=dt2[:, qq, 0:m], op0=ALU.mult, op1=ALU.add),
                                         reads=["SN", f"SN{qq}", ck, f"dt2_{qq}"], writes=[f"SN{qq}"])
                            if 2 * m < 512:
                                cn, sn_ = cmt[(it + 1) % 2], smt[(it + 1) % 2]
                                cnk, snk = f"cmt{(it + 1) % 2}", f"smt{(it + 1) % 2}"
                                p.op("dve", lambda e, c_=c_: e.tensor_tensor(out=sq1[:], in0=c_[:], in1=c_[:], op=ALU.mult), reads=[ck], writes=["sq1"])
                                p.op("dve", lambda e, s_=s_: e.tensor_tensor(out=sq2[:], in0=s_[:], in1=s_[:], op=ALU.mult), reads=[sk], writes=["sq2"])
                                p.op("dve", lambda e, cn=cn: e.tensor_tensor(out=cn[:], in0=sq1[:], in1=sq2[:], op=ALU.subtract), reads=["sq1", "sq2"], writes=[cnk])
                                p.op("dve", lambda e, sn_=sn_, c_=c_, s_=s_: e.scalar_tensor_tensor(out=sn_[:], in0=c_[:], scalar=2.0, in1=s_[:], op0=ALU.mult, op1=ALU.mult),
                                     reads=[ck, sk], writes=[snk])
                            m *= 2
                            it += 1
                        for ql in range(8):
                            q = q0 + ql
                            r = q % 2
                            for hf in range(2):
                                g = 2 * q + hf
                                pb = psb[hf]; pbk = f"p2_psb{hf}"
                                for nbk in range(4):
                                    p.op("pe", lambda e, pb=pb, nbk=nbk, g=g: e.transpose(
                                        pb[:, nbk * 128:(nbk + 1) * 128], U[:, nbk, g, :, :].rearrange("p i c -> p (i c)"), identb[:]),
                                         reads=["identb"], writes=[pbk])
                                if hf == 0:
                                    p.op("act", lambda e, pb=pb, r=r, hf=hf: e.activation(out=Ug[r][hf][:], in_=pb[:, 0:512], func=AF.Copy),
                                         reads=[pbk], writes=[f"Ug{r}_{hf}"])
                                else:
                                    p.op("dve", lambda e, pb=pb, r=r, hf=hf: e.tensor_copy(out=Ug[r][hf][:], in_=pb[:, 0:512]),
                                         reads=[pbk], writes=[f"Ug{r}_{hf}"])
                            zre = psZ[(2 * q) % 4]; zrek = f"p2_psZ{(2 * q) % 4}"
                            zim = psZ[(2 * q + 1) % 4]; zimk = f"p2_psZ{(2 * q + 1) % 4}"
                            for hf in range(2):
                                g = 2 * q + hf
                                lo, hi = hf * 64, hf * 64 + 64
                                p.op("pe", lambda e, zre=zre, g=g, lo=lo, hi=hi, r=r, hf=hf: e.matmul(
                                    zre[lo:hi, :], lhsT=Gre[:, g, :], rhs=Ug[r][hf][:], start=True, stop=True),
                                     reads=["Gre", f"Ug{r}_{hf}"], writes=[zrek])
                                p.op("pe", lambda e, zim=zim, g=g, lo=lo, hi=hi, r=r, hf=hf: e.matmul(
                                    zim[lo:hi, :], lhsT=Gim[:, g, :], rhs=Ug[r][hf][:], start=True, stop=True),
                                     reads=["Gim", f"Ug{r}_{hf}"], writes=[zimk])
                            p.op("dve", lambda e, zre=zre, ql=ql: e.tensor_tensor(out=tmp[0][:], in0=zre[:, :], in1=CS[:, ql, :], op=ALU.mult), reads=[zrek, "CS", f"CS{ql}"], writes=["p2_tmp0"])
                            p.op("dve", lambda e, zim=zim, ql=ql: e.tensor_tensor(out=tmp[1][:], in0=zim[:, :], in1=SN[:, ql, :], op=ALU.mult), reads=[zimk, "SN", f"SN{ql}"], writes=["p2_tmp1"])
                            p.op("dve", lambda e, zim=zim, ql=ql: e.tensor_tensor(out=tmp[2][:], in0=zim[:, :], in1=CS[:, ql, :], op=ALU.mult), reads=[zimk, "CS", f"CS{ql}"], writes=["p2_tmp2"])
                            p.op("dve", lambda e, zre=zre, ql=ql: e.tensor_tensor(out=tmp[3][:], in0=zre[:, :], in1=SN[:, ql, :], op=ALU.mult), reads=[zrek, "SN", f"SN{ql}"], writes=["p2_tmp3"])
                            p.op("dve", lambda e: e.tensor_tensor(out=Win[0][:], in0=tmp[0][:], in1=tmp[1][:], op=ALU.add), reads=["p2_tmp0", "p2_tmp1"], writes=["Win0"])
                            p.op("dve", lambda e: e.tensor_tensor(out=Win[1][:], in0=tmp[2][:], in1=tmp[3][:], op=ALU.subtract), reads=["p2_tmp2", "p2_tmp3"], writes=["Win1"])
                            for c in range(2):
                                p.op("dve", lambda e, c=c, q=q: e.tensor_tensor_scan(
                                    out=Wst[c][:], data0=dec8[:, q:q + 1].to_broadcast([128, 512]), data1=Win[c][:],
                                    initial=0.0, op0=ALU.mult, op1=ALU.add), reads=["dec8", f"Win{c}"], writes=[f"Wst{c}"])
                            p.op("dve", lambda e, ql=ql: e.tensor_tensor(out=tmp[0][:, 0:511], in0=Wst[0][:, 0:511], in1=CS[:, ql, 0:511], op=ALU.mult), reads=["Wst0", "CS", f"CS{ql}"], writes=["p2_tmp0"])
                            p.op("dve", lambda e, ql=ql: e.tensor_tensor(out=tmp[1][:, 0:511], in0=Wst[1][:, 0:511], in1=SN[:, ql, 0:511], op=ALU.mult), reads=["Wst1", "SN", f"SN{ql}"], writes=["p2_tmp1"])
                            p.op("dve", lambda e, ql=ql: e.tensor_tensor(out=tmp[2][:, 0:511], in0=Wst[1][:, 0:511], in1=CS[:, ql, 0:511], op=ALU.mult), reads=["Wst1", "CS", f"CS{ql}"], writes=["p2_tmp2"])
                            p.op("dve", lambda e, ql=ql: e.tensor_tensor(out=tmp[3][:, 0:511], in0=Wst[0][:, 0:511], in1=SN[:, ql, 0:511], op=ALU.mult), reads=["Wst0", "SN", f"SN{ql}"], writes=["p2_tmp3"])
                            p.op("dve", lambda e, r=r: e.tensor_tensor(out=Sp[r][0][:, 1:512], in0=tmp[0][:, 0:511], in1=tmp[1][:, 0:511], op=ALU.subtract), reads=["p2_tmp0", "p2_tmp1"], writes=[f"Sp{r}_0"])
                            p.op("dve", lambda e, r=r: e.tensor_tensor(out=Sp[r][1][:, 1:512], in0=tmp[2][:, 0:511], in1=tmp[3][:, 0:511], op=ALU.add), reads=["p2_tmp2", "p2_tmp3"], writes=[f"Sp{r}_1"])
                            for nbk in range(4):
                                py = psY[yctr % 2]; pyk = f"p2_psY{yctr % 2}"; yctr += 1
                                for hf in range(2):
                                    g = 2 * q + hf
                                    lo, hi = hf * 64, hf * 64 + 64
                                    p.op("pe", lambda e, py=py, hf=hf, r=r, nbk=nbk, g=g: e.matmul(
                                        py[:, hf * 128:(hf + 1) * 128], lhsT=Ug[r][hf][:, nbk * 128:(nbk + 1) * 128], rhs=Mbf[:, g, :],
                                        start=True, stop=False), reads=[f"Ug{r}_{hf}"], writes=[pyk])
                                    p.op("pe", lambda e, py=py, hf=hf, r=r, nbk=nbk, q=q, lo=lo, hi=hi: e.matmul(
                                        py[:, hf * 128:(hf + 1) * 128], lhsT=Sp[r][0][lo:hi, nbk * 128:(nbk + 1) * 128], rhs=Hre[lo:hi, q, :],
                                        start=False, stop=False), reads=[f"Sp{r}_0"], writes=[pyk])
                                    p.op("pe", lambda e, py=py, hf=hf, r=r, nbk=nbk, q=q, lo=lo, hi=hi: e.matmul(
                                        py[:, hf * 128:(hf + 1) * 128], lhsT=Sp[r][1][lo:hi, nbk * 128:(nbk + 1) * 128], rhs=Hni[lo:hi, q, :],
                                        start=False, stop=True), reads=[f"Sp{r}_1"], writes=[pyk])
                                p.op("act", lambda e, py=py, nbk=nbk, q=q: e.activation(
                                    out=Ytm[:, nbk, :, 32 * q:32 * q + 32].rearrange("p j (h o) -> p j h o", h=2),
                                    in_=py[:, 0:256].rearrange("p (h j o) -> p j h o", h=2, j=8), func=AF.Gelu_apprx_tanh),
                                     reads=[pyk], writes=[newkey("Ytm")])
                            cast_step(1)
                    phase_end([("Ytm", Ytm[:].rearrange("p a j c -> p (a j c)"), [128, NB * 8 * 512], BF16),
                               ("Sp", Sp[1][0][:], [128, 512], BF16), ("CS", CS[:].rearrange("p a b -> p (a b)"), [128, 4096], F32)])

            with ExitStack() as st:
                yT = sbt(st, "yT", [128, 4, 2048], BF16)
                sgT = sbt(st, "sgT", [128, 4, 2048], BF16)
                selt = sbt(st, "selt", [128, 4, 64], BF16)
                wglu_t = sbt(st, "wglu_t", [128, 4, 512], BF16)
                bglu_t = sbt(st, "bglu_t", [128, 4], F32)
                sg = [sbt(st, f"sg{i}", [128, 512], BF16) for i in range(2)]
                psZ = [pst(st, f"p2b_ps{i}", [128, 512]) for i in range(4)]
                cload(selt[:], sel_d, "selt")
                cload(bglu_t[:], b_gluT_d, "bglu_t")
                wk = cast_until(["wglu_s"])
                load_w(wglu_t[:], wglu_s.rearrange("(kt p) c -> p kt c", p=128), "wglu_t", wk, "ld_w1")
                zc = 0
                for s in range(4):
                    for ct in range(4):
                        ps = psZ[zc % 4]; pk = f"p2b_ps{zc % 4}"; zc += 1
                        for j in range(8):
                            p.op("pe", lambda e, ps=ps, s=s, ct=ct, j=j: e.matmul(
                                ps[:, j * 64:(j + 1) * 64], lhsT=Ytm[:, s, j, ct * 128:(ct + 1) * 128], rhs=selt[:, s, :],
                                start=True, stop=True), reads=["selt"], writes=[pk])
                        if ct % 2 == 0:
                            p.op("act", lambda e, ps=ps, s=s, ct=ct: e.activation(
                                out=yT[:, ct, s * 512:(s + 1) * 512].rearrange("p (m j) -> p m j", j=8),
                                in_=ps[:, :].rearrange("p (j m) -> p m j", j=8), func=AF.Copy), reads=[pk], writes=[f"yT{s}_{ct}"])
                        else:
                            p.op("dve", lambda e, ps=ps, s=s, ct=ct: e.tensor_copy(
                                out=yT[:, ct, s * 512:(s + 1) * 512].rearrange("p (m j) -> p m j", j=8),
                                in_=ps[:, :].rearrange("p (j m) -> p m j", j=8)), reads=[pk], writes=[f"yT{s}_{ct}"])
                for s in range(4):
                    for ct in range(4):
                        ps = psZ[zc % 4]; pk = f"p2b_ps{zc % 4}"; zc += 1
                        for kt in range(4):
                            p.op("pe", lambda e, ps=ps, s=s, ct=ct, kt=kt: e.matmul(
                                ps[:, :], lhsT=wglu_t[:, kt, ct * 128:(ct + 1) * 128], rhs=yT[:, kt, s * 512:(s + 1) * 512],
                                start=(kt == 0), stop=(kt == 3)), reads=["wglu_t", f"yT{s}_{kt}"], writes=[pk])
                        sgi = zc % 2
                        p.op("act", lambda e, ps=ps, ct=ct, sgi=sgi: e.activation(
                            out=sg[sgi][:], in_=ps[:, :], func=AF.Sigmoid, bias=bglu_t[:, ct:ct + 1]),
                             reads=[pk, "bglu_t"], writes=[f"sg{sgi}"])
                        p.op("dve", lambda e, s=s, ct=ct, sgi=sgi: e.tensor_tensor(
                            out=sgT[:, ct, s * 512:(s + 1) * 512], in0=sg[sgi][:], in1=yT[:, ct, s * 512:(s + 1) * 512], op=ALU.mult),
                             reads=[f"sg{sgi}", f"yT{s}_{ct}"], writes=[f"sgT{s}_{ct}"])
                p.dma(lambda e: e.dma_start(out=sglu_s, in_=sgT[:]), "st_sg",
                      reads=[f"sgT{s}_{ct}" for s in range(4) for ct in range(4)], writes=["sglu_s"])
                phase_end([("sgluT", sgT[:].rearrange("p a b -> p (a b)"), [128, 4 * 2048], BF16)])

        s5o.close()
        with ExitStack() as at:
            KT = sbt(at, "KT", [128, 4, T], BF16)
            V = sbt(at, "V", [128, 32, 512], BF16)
            QT = sbt(at, "QT", [128, 4, 2048], BF16)
            with ExitStack() as st:
                ctx = {"name": "p1b", "psb_ctr": [0],
                       "psb": [pst(st, f"p1b_psb{i}", [128, 1024], BF16) for i in range(2)]}
                psf = [pst(st, f"p1b_psf{i}", [128, 512]) for i in range(6)]
                xt = [sbt(st, f"p1b_xt{i}", [128, D], F32) for i in range(8)]
                xn4 = [sbt(st, f"p1b_xn{i}", [128, 4, D], BF16) for i in range(2)]
                hT = [sbt(st, f"p1b_hT{i}", [128, 8, 512], BF16) for i in range(1)]
                wQKV = sbt(st, "wQKV", [128, 8, 1536], BF16)

                wk = cast_until(["win_s"])
                load_w(wQKV[:], win_s.rearrange("(kt p) c -> p kt c", p=128)[:, :, 512:2048], "wQKV", wk, "ld_w2")
                chunks = [("all", c) for c in range(8)] + [("own", s) for s in range(4)]
                pf = 0

                def rows(kind, i):
                    return (x_all if kind == "all" else x_own)[i * 512:(i + 1) * 512, :]

                normA(make_xloader(xt, "p1b_xt", rows(*chunks[0])), xn4[0], "p1b_xn0")
                for ci, (kind, idx) in enumerate(chunks):
                    r = ci % 2
                    hb = hT[0]; hbk = "p1b_hT0"
                    normB(ctx, xn4[r], f"p1b_xn{r}", hb, hbk, geff1, "geff1", sh1, "modT0_1")
                    if ci + 1 < len(chunks):
                        r2 = (ci + 1) % 2
                        normA(make_xloader(xt, "p1b_xt", rows(*chunks[ci + 1])), xn4[r2], f"p1b_xn{r2}")
                    if kind == "all":
                        for ct in range(4):
                            ps = psf[pf % 6]; pk = f"p1b_psf{pf % 6}"; pf += 1
                            for kt in range(8):
                                p.op("pe", lambda e, ps=ps, hb=hb, kt=kt, ct=ct: e.matmul(
                                    ps[:, :], lhsT=wQKV[:, kt, 512 + ct * 128: 512 + (ct + 1) * 128], rhs=hb[:, kt, :],
                                    start=(kt == 0), stop=(kt == 7)), reads=[f"{hbk}_{kt}", "wQKV"], writes=[pk])
                            if ct % 2 == 0:
                                p.op("act", lambda e, ps=ps, ct=ct, idx=idx: e.activation(out=KT[:, ct, idx * 512:(idx + 1) * 512], in_=ps[:, :], func=AF.Copy),
                                     reads=[pk], writes=[newkey("KT")])
                            else:
                                p.op("dve", lambda e, ps=ps, ct=ct, idx=idx: e.tensor_copy(out=KT[:, ct, idx * 512:(idx + 1) * 512], in_=ps[:, :]),
                                     reads=[pk], writes=[newkey("KT")])
                        for tt in range(4):
                            ps = psf[pf % 6]; pk = f"p1b_psf{pf % 6}"; pf += 1
                            for kt in range(8):
                                p.op("pe", lambda e, ps=ps, hb=hb, kt=kt, tt=tt: e.matmul(
                                    ps[:, :], lhsT=hb[:, kt, tt * 128:(tt + 1) * 128], rhs=wQKV[:, kt, 1024:1536],
                                    start=(kt == 0), stop=(kt == 7)), reads=[f"{hbk}_{kt}", "wQKV"], writes=[pk])
                            if tt % 2 == 0:
                                p.op("act", lambda e, ps=ps, tt=tt, idx=idx: e.activation(out=V[:, idx * 4 + tt, :], in_=ps[:, :], func=AF.Copy),
                                     reads=[pk], writes=[newkey("V")])
                            else:
                                p.op("dve", lambda e, ps=ps, tt=tt, idx=idx: e.tensor_copy(out=V[:, idx * 4 + tt, :], in_=ps[:, :]),
                                     reads=[pk], writes=[newkey("V")])
                    else:
                        for ct in range(4):
                            ps = psf[pf % 6]; pk = f"p1b_psf{pf % 6}"; pf += 1
                            for kt in range(8):
                                p.op("pe", lambda e, ps=ps, hb=hb, kt=kt, ct=ct: e.matmul(
                                    ps[:, :], lhsT=wQKV[:, kt, ct * 128:(ct + 1) * 128], rhs=hb[:, kt, :],
                                    start=(kt == 0), stop=(kt == 7)), reads=[f"{hbk}_{kt}", "wQKV"], writes=[pk])
                            p.op("act", lambda e, ps=ps, ct=ct, idx=idx: e.activation(out=QT[:, ct, idx * 512:(idx + 1) * 512], in_=ps[:, :], func=AF.Copy, scale=0.125),
                                 reads=[pk], writes=[newkey("QT")])
                    cast_step(2)
                phase_end([("KT", KT[:].rearrange("p a b -> p (a b)"), [128, 4 * T], BF16),
                           ("V", V[:].rearrange("p a b -> p (a b)"), [128, 32 * 512], BF16),
                           ("QT", QT[:].rearrange("p a b -> p (a b)"), [128, 4 * 2048], BF16)])

            with ExitStack() as st:
                atri = sbt(st, "atri", [128, 128], BF16)
                neguincl = sbt(st, "neguincl", [128, 128], BF16)
                neglow = sbt(st, "neglow", [128, 128], BF16)
                cload(atri[:], atri_d, "atri")
                cload(neguincl[:], neguincl_d, "neguincl")
                cload(neglow[:], neglow_d, "neglow")
                attnT = sbt(st, "attnT", [128, 4, 2048], BF16)
                mB = [sbt(st, f"mB{i}", [128, 8, 512], BF16) for i in range(2)]
                psP = [pst(st, f"psP{i}", [128, 512]) for i in range(2)]
                psO = pst(st, "psO", [128, 512])
                psZ = [pst(st, f"a_psZ{i}", [128, 512]) for i in range(4)]
                NE, NL, NP, NW = 3, 4, 2, 3
                eb = [[sbt(st, f"eb{h}_{i}", [128, 512], F32) for i in range(NE)] for h in range(2)]
                Lb = [[sbt(st, f"Lb{h}_{i}", [128, 512], BF16) for i in range(NL)] for h in range(2)]
                ePb = [[sbt(st, f"ePb{h}_{i}", [128, 512], F32) for i in range(NP)] for h in range(2)]
                Wb = [[sbt(st, f"Wb{h}_{i}", [128, 512], BF16) for i in range(NW)] for h in range(2)]
                zc = [0]
                for s in range(4):
                    mr = s % 2
                    p.dma(lambda e, mr=mr, s=s: e.dma_start(out=mB[mr][:], in_=maskB_d[:, s, :, :]), f"ld_mask{mr}", writes=[f"mB{mr}"])
                    KBs = 8 * (s + 1)
                    kbs = list(range(KBs - 1, -1, -1))
                    n = len(kbs)
                    for hp in range(4):
                        heads = (2 * hp, 2 * hp + 1)

                        def st_qk(t):
                            kb = kbs[t]
                            for hh in range(2):
                                lo, hi = hh * 64, hh * 64 + 64
                                zi = zc[0] % 4; zc[0] += 1
                                zb = psZ[zi]; zk = f"a_psZ{zi}"
                                masked = kb >= KBs - 8
                                p.op("pe", lambda e, zb=zb, lo=lo, hi=hi, kb=kb, masked=masked, hp=hp, s=s: e.matmul(
                                    zb[:, :], lhsT=KT[lo:hi, hp, kb * 128:(kb + 1) * 128], rhs=QT[lo:hi, hp, s * 512:(s + 1) * 512],
                                    start=True, stop=(not masked)), reads=[], writes=[zk])
                                if masked:
                                    p.op("pe", lambda e, zb=zb, kb=kb, mr=mr, KBs=KBs: e.matmul(
                                        zb[:, :], lhsT=atri[:], rhs=mB[mr][:, kb - (KBs - 8), :], start=False, stop=True),
                                         reads=["atri", f"mB{mr}"], writes=[zk])
                                ei = t % NE
                                p.op("act", lambda e, zb=zb, hh=hh, ei=ei: e.activation(out=eb[hh][ei][:], in_=zb[:, :], func=AF.Exp),
                                     reads=[zk], writes=[f"eb{hh}_{ei}"])
                            for hh in range(2):
                                ei = t % NE; li = t % NL
                                p.op("act", lambda e, hh=hh, ei=ei, li=li: e.activation(out=Lb[hh][li][:], in_=eb[hh][ei][:], func=AF.Ln, bias=1.0),
                                     reads=[f"eb{hh}_{ei}"], writes=[f"Lb{hh}_{li}"])

                        def st_p(t):
                            for hh in range(2):
                                li = t % NL
                                if t > 0:
                                    lpi = (t - 1) % NL
                                    p.op("pe", lambda e, hh=hh, lpi=lpi: e.matmul(
                                        psP[hh][:, :], lhsT=neglow[:], rhs=Lb[hh][lpi][:], start=False, stop=False, skip_group_check=True),
                                         reads=["neglow", f"Lb{hh}_{lpi}"], writes=[f"psP{hh}"])
                                p.op("pe", lambda e, hh=hh, li=li, t=t: e.matmul(
                                    psP[hh][:, :], lhsT=neguincl[:], rhs=Lb[hh][li][:], start=(t == 0), stop=False, skip_group_check=True),
                                     reads=["neguincl", f"Lb{hh}_{li}"], writes=[f"psP{hh}"])
                            for hh in range(2):
                                pi = t % NP; ei = t % NE; wi = t % NW
                                p.op("act", lambda e, hh=hh, pi=pi: e.activation(out=ePb[hh][pi][:], in_=psP[hh][:, :], func=AF.Exp),
                                     reads=[f"psP{hh}"], writes=[f"ePb{hh}_{pi}"])
                                p.op("dve", lambda e, hh=hh, pi=pi, ei=ei, wi=wi: e.tensor_tensor(
                                    out=Wb[hh][wi][:], in0=eb[hh][ei][:], in1=ePb[hh][pi][:], op=ALU.mult),
                                     reads=[f"eb{hh}_{ei}", f"ePb{hh}_{pi}"], writes=[f"Wb{hh}_{wi}"])

                        def st_pv(t):
                            kb = kbs[t]
                            for hh in range(2):
                                h = heads[hh]
                                lo, hi = hh * 64, hh * 64 + 64
                                wi = t % NW
                                p.op("pe", lambda e, hh=hh, h=h, lo=lo, hi=hi, kb=kb, wi=wi, t=t, n=n: e.matmul(
                                    psO[lo:hi, :], lhsT=V[:, kb, h * 64:(h + 1) * 64], rhs=Wb[hh][wi][:],
                                    start=(t == 0), stop=(t == n - 1)), reads=[f"Wb{hh}_{wi}"], writes=["psO"])

                        for step in range(n + 2):
                            if step < n:
                                st_qk(step)
                            if 1 <= step <= n:
                                st_p(step - 1)
                            if step >= 2:
                                st_pv(step - 2)
                        if hp % 2 == 0:
                            p.op("act", lambda e, s=s, hp=hp: e.activation(out=attnT[:, hp, s * 512:(s + 1) * 512], in_=psO[:, :], func=AF.Copy),
                                 reads=["psO"], writes=[f"attnT{s}_{hp}"])
                        else:
                            p.op("dve", lambda e, s=s, hp=hp: e.tensor_copy(out=attnT[:, hp, s * 512:(s + 1) * 512], in_=psO[:, :]),
                                 reads=["psO"], writes=[f"attnT{s}_{hp}"])
                        cast_step(2)
                p.dma(lambda e: e.dma_start(out=attn_s, in_=attnT[:]), "st_at",
                      reads=[f"attnT{s}_{hp}" for s in range(4) for hp in range(4)], writes=["attn_s"])
                phase_end([("attnT", attnT[:].rearrange("p a b -> p (a b)"), [128, 4 * 2048], BF16)])

        with ExitStack() as st:
            ctx = {"name": "p4a", "psb_ctr": [0],
                   "psb": [pst(st, f"p4a_psb{i}", [128, 1024], BF16) for i in range(2)]}
            psf = [pst(st, f"p4a_psf{i}", [128, 512]) for i in range(6)]
            xcA = [sbt(st, f"p4a_xc{i}", [128, 4, D], F32) for i in range(2)]
            xn4 = [sbt(st, f"p4a_xn{i}", [128, 4, D], BF16) for i in range(2)]
            hT = [sbt(st, f"p4a_hT{i}", [128, 8, 512], BF16) for i in range(1)]
            wgt = [sbt(st, f"wgt{i}", [128, 8, 512], BF16) for i in range(3)]
            wa_t = sbt(st, "wa_t", [128, 4, D], BF16)
            wb_t = sbt(st, "wb_t", [128, 4, D], BF16)
            wo_t = sbt(st, "wo_t", [128, 8, D], BF16)
            gT = sbt(st, "gT", [128, 16, 512], BF16)
            mT = sbt(st, "mT", [128, 8, 512], BF16)
            t12 = [sbt(st, f"t12_{i}", [128, 512], F32) for i in range(4)]
            sa = [sbt(st, f"sa{i}", [128, 2, 4, 512], BF16) for i in range(2)]
            wk = cast_until(["win_s", "wa_s", "wb_s", "wo_s"])
            load_w(wa_t[:], wa_s.rearrange("(kt p) c -> p kt c", p=128), "wa_t", wk, "ld_w4")
            load_w(wb_t[:], wb_s.rearrange("(kt p) c -> p kt c", p=128), "wb_t", wk, "ld_w5")
            load_w(wo_t[:], wo_s.rearrange("(kt p) c -> p kt c", p=128), "wo_t", wk, "ld_w0")
            win_v = win_s.rearrange("(kt p) c -> p kt c", p=128)
            pf = 0
            tc_ = 0
            wq = 0

            def xc_loader(r, s):
                p.dma(lambda e, r=r, s=s: e.dma_start(out=xcA[r][:], in_=x_own[s * 512:(s + 1) * 512, :].rearrange("(tt p) d -> p tt d", p=128)),
                      f"ld_x{r}", writes=[f"p4a_xc{r}"])
                return lambda tt, r=r: (xcA[r][:, tt, :], f"p4a_xc{r}")

            normA(xc_loader(0, 0), xn4[0], "p4a_xn0")
            for s in range(4):
                r = s % 2
                hb = hT[0]; hbk = "p4a_hT0"
                sat = sa[r]
                p.dma(lambda e, sat=sat, s=s: e.dma_start(out=sat[:, 0, :, :], in_=sglu_s[:, :, s * 512:(s + 1) * 512]), f"ld_sa{r}",
                      reads=["sglu_s"], writes=[f"sa{r}_0"])
                p.dma(lambda e, sat=sat, s=s: e.dma_start(out=sat[:, 1, :, :], in_=attn_s[:, :, s * 512:(s + 1) * 512]), f"ld_sb{r}",
                      reads=["attn_s"], writes=[f"sa{r}_1"])
                sak = [f"sa{r}_0", f"sa{r}_1"]
                normB(ctx, xn4[r], f"p4a_xn{r}", hb, hbk, geff1, "geff1", sh1, "modT0_1")
                if s + 1 < 4:
                    r2 = (s + 1) % 2
                    normA(xc_loader(r2, s + 1), xn4[r2], f"p4a_xn{r2}")
                for g4 in range(4):
                    wi = wq % 3; wq += 1
                    p.dma(lambda e, wi=wi, g4=g4: e.dma_start(out=wgt[wi][:], in_=win_v[:, :, 2048 + g4 * 512: 2048 + (g4 + 1) * 512]),
                          f"ld_wgt{wi}", reads=wk, writes=[f"wgt{wi}"])
                    for gl in range(4):
                        gi = g4 * 4 + gl
                        ps = psf[pf % 6]; pk = f"p4a_psf{pf % 6}"; pf += 1
                        for kt in range(8):
                            p.op("pe", lambda e, ps=ps, hb=hb, kt=kt, gl=gl, wi=wi: e.matmul(
                                ps[:, :], lhsT=wgt[wi][:, kt, gl * 128:(gl + 1) * 128], rhs=hb[:, kt, :], start=(kt == 0), stop=(kt == 7)),
                                 reads=[f"{hbk}_{kt}", f"wgt{wi}"], writes=[pk])
                        p.op("act", lambda e, ps=ps, gi=gi: e.activation(out=gT[:, gi, :], in_=ps[:, :], func=AF.Sigmoid),
                             reads=[pk], writes=[f"gT{gi}"])
                for ct in range(8):
                    psa = psf[pf % 6]; pka = f"p4a_psf{pf % 6}"; pf += 1
                    psb_ = psf[pf % 6]; pkb = f"p4a_psf{pf % 6}"; pf += 1
                    for kt in range(4):
                        p.op("pe", lambda e, psa=psa, kt=kt, ct=ct, sat=sat: e.matmul(
                            psa[:, :], lhsT=wa_t[:, kt, ct * 128:(ct + 1) * 128], rhs=sat[:, 0, kt, :],
                            start=(kt == 0), stop=(kt == 3)), reads=["wa_t"] + sak, writes=[pka])
                    for kt in range(4):
                        p.op("pe", lambda e, psb_=psb_, kt=kt, ct=ct, sat=sat: e.matmul(
                            psb_[:, :], lhsT=wb_t[:, kt, ct * 128:(ct + 1) * 128], rhs=sat[:, 1, kt, :],
                            start=(kt == 0), stop=(kt == 3)), reads=["wb_t"] + sak, writes=[pkb])
                    ta = t12[tc_ % 4]; tak = f"t12_{tc_ % 4}"; tc_ += 1
                    tb = t12[tc_ % 4]; tbk = f"t12_{tc_ % 4}"; tc_ += 1
                    p.op("dve", lambda e, psa=psa, ta=ta, ct=ct: e.tensor_tensor(out=ta[:], in0=psa[:, :], in1=gT[:, ct, :], op=ALU.mult),
                         reads=[pka, f"gT{ct}"], writes=[tak])
                    p.op("dve", lambda e, psb_=psb_, tb=tb, ct=ct: e.tensor_tensor(out=tb[:], in0=psb_[:, :], in1=gT[:, 8 + ct, :], op=ALU.mult),
                         reads=[pkb, f"gT{8 + ct}"], writes=[tbk])
                    p.op("dve", lambda e, ta=ta, tb=tb, ct=ct: e.tensor_tensor(out=mT[:, ct, :], in0=ta[:], in1=tb[:], op=ALU.add),
                         reads=[tak, tbk], writes=[f"mT{ct}"])
                mks = [f"mT{ct}" for ct in range(8)]
                xck = f"p4a_xc{r}"
                for tt in range(4):
                    for h in range(2):
                        ps = psf[pf % 6]; pk = f"p4a_psf{pf % 6}"; pf += 1
                        for kt in range(8):
                            p.op("pe", lambda e, ps=ps, kt=kt, tt=tt, h=h: e.matmul(
                                ps[:, :], lhsT=mT[:, kt, tt * 128:(tt + 1) * 128], rhs=wo_t[:, kt, h * 512:(h + 1) * 512],
                                start=(kt == 0), stop=(kt == 7)), reads=[f"mT{kt}", "wo_t"], writes=[pk])
                        ta = t12[tc_ % 4]; tak = f"t12_{tc_ % 4}"; tc_ += 1
                        p.op("dve", lambda e, ps=ps, ta=ta, h=h: e.tensor_tensor(out=ta[:], in0=ps[:, :], in1=g1bc[:, h * 512:(h + 1) * 512], op=ALU.mult),
                             reads=[pk, "g1bc"], writes=[tak])
                        p.op("dve", lambda e, ta=ta, tt=tt, h=h, r=r: e.tensor_tensor(
                            out=xcA[r][:, tt, h * 512:(h + 1) * 512], in0=ta[:], in1=xcA[r][:, tt, h * 512:(h + 1) * 512], op=ALU.add),
                             reads=[tak, xck], writes=[xck])
                p.dma(lambda e, r=r, s=s: e.dma_start(out=x1_s[s * 512:(s + 1) * 512, :].rearrange("(tt p) d -> p tt d", p=128), in_=xcA[r][:]),
                      f"st_x1{r}", reads=[xck], writes=[f"x1_s{s}"], q="pool")
                cast_step(8)
            cast_step(500)
            phase_end(exclude=())

        with ExitStack() as st:
            ctx = {"name": "p4b", "psb_ctr": [0],
                   "psb": [pst(st, f"p4b_psb{i}", [128, 1024], BF16) for i in range(2)]}
            psf = [pst(st, f"p4b_psf{i}", [128, 512]) for i in range(6)]
            xcB = [sbt(st, f"p4b_xc{i}", [128, 4, D], F32) for i in range(2)]
            xn4 = [sbt(st, f"p4b_xn{i}", [128, 4, D], BF16) for i in range(2)]
            hT = [sbt(st, f"p4b_hT{i}", [128, 8, 512], BF16) for i in range(1)]
            wd_t = sbt(st, "wd_t", [128, NFT, D], BF16)
            wgu = [sbt(st, f"wgu{i}", [128, 2, 8, 128], BF16) for i in range(3)]
            hid = sbt(st, "hid", [128, NFT, 512], BF16)
            sgf = [sbt(st, f"sgf{i}", [128, 512], F32) for i in range(2)]
            tf = [sbt(st, f"tf{i}", [128, 512], F32) for i in range(2)]
            x2 = [sbt(st, f"x2_{i}", [128, D], F32) for i in range(2)]
            ost = [sbt(st, f"ost{i}", [128, D], F32) for i in range(2)]
            gfbc = sbt(st, "gfbc", [128, D], F32)
            cload(gfbc[:], nfg_row.partition_broadcast(128), "gfbc")
            load_w(wd_t[:], wd_s.rearrange("(ft p) c -> p ft c", p=128), "wd_t", cast_done["wd_s"], "ld_w1")
            pf = 0
            wq = 0
            tcn = 0
            xq = 0

            def x1_loader(r, s):
                p.dma(lambda e, r=r, s=s: e.dma_start(out=xcB[r][:], in_=x1_s[s * 512:(s + 1) * 512, :].rearrange("(tt p) d -> p tt d", p=128)),
                      f"ld_x1{r}", reads=[f"x1_s{s}"], writes=[f"p4b_xc{r}"])
                return lambda tt, r=r: (xcB[r][:, tt, :], f"p4b_xc{r}")

            normA(x1_loader(0, 0), xn4[0], "p4b_xn0")
            for s in range(4):
                r = s % 2
                hb = hT[0]; hbk = "p4b_hT0"
                normB(ctx, xn4[r], f"p4b_xn{r}", hb, hbk, geff2, "geff2", sh2, "modT2_1")
                if s + 1 < 4:
                    r2 = (s + 1) % 2
                    normA(x1_loader(r2, s + 1), xn4[r2], f"p4b_xn{r2}")
                for ft in range(NFT):
                    wi = wq % 3; wq += 1
                    wt = wgu[wi]; wtk = f"wgu{wi}"
                    p.dma(lambda e, wt=wt, ft=ft: e.dma_start(out=wt[:, 0, :, :], in_=wg_s[ft, :, :, :]), f"ld_wgu{wi}",
                          reads=cast_done["wg_s"], writes=[wtk + "g"])
                    p.dma(lambda e, wt=wt, ft=ft: e.dma_start(out=wt[:, 1, :, :], in_=wu_s[ft, :, :, :]), f"ld_wgv{wi}",
                          reads=cast_done["wu_s"], writes=[wtk + "u"])
                    psg = psf[pf % 6]; pkg = f"p4b_psf{pf % 6}"; pf += 1
                    psu = psf[pf % 6]; pku = f"p4b_psf{pf % 6}"; pf += 1
                    for kt in range(8):
                        p.op("pe", lambda e, psg=psg, wt=wt, kt=kt, hb=hb: e.matmul(
                            psg[:, :], lhsT=wt[:, 0, kt, :], rhs=hb[:, kt, :], start=(kt == 0), stop=(kt == 7)),
                             reads=[wtk + "g", wtk + "u", f"{hbk}_{kt}"], writes=[pkg])
                    for kt in range(8):
                        p.op("pe", lambda e, psu=psu, wt=wt, kt=kt, hb=hb: e.matmul(
                            psu[:, :], lhsT=wt[:, 1, kt, :], rhs=hb[:, kt, :], start=(kt == 0), stop=(kt == 7)),
                             reads=[wtk + "g", wtk + "u", f"{hbk}_{kt}"], writes=[pku])
                    si = ft % 2
                    p.op("act", lambda e, psg=psg, si=si: e.activation(out=sgf[si][:], in_=psg[:, :], func=AF.Silu),
                         reads=[pkg], writes=[f"sgf{si}"])
                    p.op("dve", lambda e, psu=psu, si=si, ft=ft: e.tensor_tensor(out=hid[:, ft, :], in0=psu[:, :], in1=sgf[si][:], op=ALU.mult),
                         reads=[pku, f"sgf{si}"], writes=[f"hid{ft}"])
                for tt in range(4):
                    xi = xq % 2; xq += 1
                    for h in range(2):
                        ps = psf[pf % 6]; pk = f"p4b_psf{pf % 6}"; pf += 1
                        for ft in range(NFT):
                            p.op("pe", lambda e, ps=ps, ft=ft, tt=tt, h=h: e.matmul(
                                ps[:, :], lhsT=hid[:, ft, tt * 128:(tt + 1) * 128], rhs=wd_t[:, ft, h * 512:(h + 1) * 512],
                                start=(ft == 0), stop=(ft == NFT - 1)), reads=[f"hid{ft}", "wd_t"], writes=[pk])
                        ti = tcn % 2; tcn += 1
                        p.op("dve", lambda e, ps=ps, ti=ti, h=h: e.tensor_tensor(out=tf[ti][:], in0=ps[:, :], in1=g2bc[:, h * 512:(h + 1) * 512], op=ALU.mult),
                             reads=[pk, "g2bc"], writes=[f"tf{ti}"])
                        p.op("dve", lambda e, ti=ti, xi=xi, tt=tt, h=h, r=r: e.tensor_tensor(
                            out=x2[xi][:, h * 512:(h + 1) * 512], in0=tf[ti][:], in1=xcB[r][:, tt, h * 512:(h + 1) * 512], op=ALU.add),
                             reads=[f"tf{ti}", f"p4b_xc{r}"], writes=[f"x2_{xi}_{h}"])
                    ss, ssk = newvec(); sd, sdk = newvec(); rs, rsk = newvec()
                    x2k = [f"x2_{xi}_0", f"x2_{xi}_1"]
                    p.op("act", lambda e, xi=xi, ss=ss: e.activation(out=junk[:], in_=x2[xi][:], func=AF.Square, accum_out=ss),
                         reads=x2k, writes=[ssk, "junk"])
                    p.op("act", lambda e, ss=ss, sd=sd: e.activation(out=sd, in_=ss, func=AF.Sqrt, scale=1.0 / D, bias=epsv[:, 0:1]),
                         reads=[ssk, "epsv"], writes=[sdk])
                    p.op("dve", lambda e, sd=sd, rs=rs: e.reciprocal(out=rs, in_=sd), reads=[sdk], writes=[rsk])
                    p.op("dve", lambda e, xi=xi, rs=rs: e.scalar_tensor_tensor(out=ost[xi][:], in0=x2[xi][:], scalar=rs, in1=gfbc[:], op0=ALU.mult, op1=ALU.mult),
                         reads=x2k + [rsk, "gfbc"], writes=[f"ost{xi}"])
                    row0 = s * 512 + tt * 128
                    p.dma(lambda e, xi=xi, row0=row0: e.dma_start(out=out_d[row0:row0 + 128, :], in_=ost[xi][:]), f"st_out{xi}",
                          reads=[f"ost{xi}"], q="pool")
            p.final_wait("sp", [k for k in p.count if k not in ("pe", "act", "dve", "pool")])
        p.emit(block, sems)
    return nc, list(dbg.keys())


def _bf(a):
    return np.ascontiguousarray(a).astype(NPBF)


def prep_inputs(inp):
    f32 = np.float32
    x = np.asarray(inp["x"], f32)
    c = np.asarray(inp["c"], f32)
    L = 0
    shared = {}
    shared["w_ada"] = np.ascontiguousarray(np.asarray(inp["w_ada"], f32)[L])
    b_ada = np.asarray(inp["b_ada"], f32)[L]
    shared["b_adaT"] = np.ascontiguousarray(b_ada.reshape(48, 128).T)
    shared["b_ada_row"] = np.ascontiguousarray(b_ada.reshape(1, -1))
    shared["n1gT"] = np.ascontiguousarray(np.asarray(inp["norm1_g"], f32)[L].reshape(8, 128).T)
    shared["n2gT"] = np.ascontiguousarray(np.asarray(inp["norm2_g"], f32)[L].reshape(8, 128).T)
    shared["nfg_row"] = np.ascontiguousarray(np.asarray(inp["norm_f_g"], f32).reshape(1, D))
    shared["w_in"] = np.ascontiguousarray(np.asarray(inp["w_in"], f32)[L])

    def pairlay(a):
        return np.ascontiguousarray(a.reshape(16, 2, 64).transpose(1, 2, 0).reshape(128, 16))

    lam_re = np.asarray(inp["lam_re"], f32)[L]
    lam_im = np.asarray(inp["lam_im"], f32)[L]
    log_dt = np.asarray(inp["log_dt"], f32)[L]
    shared["lamre_p"] = pairlay(lam_re)
    shared["lamim_p"] = pairlay(lam_im)
    shared["logdt_p"] = pairlay(np.repeat(log_dt[:, None], 64, axis=1))

    def pairlay3(a):
        return np.ascontiguousarray(a.reshape(16, 2, 64, a.shape[2]).transpose(1, 2, 0, 3).reshape(128, 16, a.shape[2]))

    shared["bre_p"] = pairlay3(np.asarray(inp["b_re"], f32)[L])
    shared["bim_p"] = pairlay3(np.asarray(inp["b_im"], f32)[L])
    shared["cre_p"] = pairlay3(np.asarray(inp["c_re"], f32)[L].transpose(0, 2, 1))
    shared["cim_p"] = pairlay3(np.asarray(inp["c_im"], f32)[L].transpose(0, 2, 1))
    d_skip = np.asarray(inp["d_skip"], f32)[L]
    shared["dvec"] = np.ascontiguousarray(np.tile(d_skip.T, (8, 1)))
    shared["w_glu"] = np.ascontiguousarray(np.asarray(inp["w_glu"], f32)[L])
    shared["b_gluT"] = np.ascontiguousarray(np.asarray(inp["b_glu"], f32)[L].reshape(4, 128).T)
    shared["w_a"] = np.ascontiguousarray(np.asarray(inp["w_a"], f32)[L])
    shared["w_b"] = np.ascontiguousarray(np.asarray(inp["w_b"], f32)[L])
    shared["w_o"] = np.ascontiguousarray(np.asarray(inp["w_o"], f32)[L])
    shared["wg"] = np.ascontiguousarray(np.asarray(inp["w_ffn_gate"], f32)[L])
    shared["wu"] = np.ascontiguousarray(np.asarray(inp["w_ffn_up"], f32)[L])
    shared["wd"] = np.ascontiguousarray(np.asarray(inp["w_ffn_down"], f32)[L])
    shared["identf"] = np.eye(128, dtype=f32)
    jj = np.arange(128)
    shared["atri"] = _bf(np.where(jj[:, None] <= jj[None, :], NEG_BIG, 0.0).astype(f32))
    shared["neguincl"] = _bf(np.where(jj[:, None] >= jj[None, :], -1.0, 0.0).astype(f32))
    shared["neglow"] = _bf(np.where(jj[:, None] < jj[None, :], -1.0, 0.0).astype(f32))
    ii = jj // 16
    shared["cmask"] = (ii[None, :] >= ii[:, None]).astype(f32)
    kvals = np.concatenate([-np.arange(8), 7 - np.arange(8), np.arange(8), 1 + np.arange(8)]).astype(f32)
    shared["kv"] = np.ascontiguousarray(np.tile(kvals[None, :], (128, 1)))
    shared["ones_row"] = np.ones((1, 128), f32)
    hm = np.zeros((128, 2), f32)
    hm[:64, 0] = 1.0
    hm[64:, 1] = 1.0
    shared["hmask"] = hm
    in_maps = []
    for core in range(8):
        b, hf = core // 2, core % 2
        own = OWN[hf]
        m = dict(shared)
        m["x_all"] = np.ascontiguousarray(x[b])
        m["x_own"] = np.ascontiguousarray(np.concatenate([x[b, cc * 512:(cc + 1) * 512] for cc in own], axis=0))
        m["cT"] = np.ascontiguousarray(c[b].reshape(8, 128).T)
        maskB = np.zeros((128, 4, 8, 512), f32)
        sel = np.zeros((128, 4, 64), f32)
        qq = np.arange(512)
        for s in range(4):
            cs = own[s]
            for r in range(8):
                off = 512 * (cs - 2 * s) - 128 * r
                tq = qq + off
                allm = tq < 0
                part = (tq >= 0) & (tq <= 127)
                maskB[0, s, r, allm] = 1.0
                maskB[tq[part], s, r, qq[part]] = 1.0
            offn = (cs - 2 * s) * 64
            sel[np.arange(64) + offn, s, np.arange(64)] = 1.0
        m["maskB"] = _bf(maskB)
        m["sel"] = _bf(sel)
        in_maps.append(m)
    return in_maps


_CACHE = {}


def kernel(**inputs):
    if "nc" not in _CACHE:
        _CACHE["nc"] = build_program()
    nc, dbg_names = _CACHE["nc"]
    in_maps = prep_inputs(inputs)
    ncores = int(os.environ.get("K_NCORES", "8"))
    res = run_bass_kernel_spmd(nc, in_maps[:ncores], core_ids=list(range(ncores)))
    out = np.zeros((4, T, D), np.float32)
    for core in range(ncores):
        b, hf = core // 2, core % 2
        o = np.asarray(res.results[core]["out"], np.float32)
        for s, cc in enumerate(OWN[hf]):
            out[b, cc * 512:(cc + 1) * 512] = o[s * 512:(s + 1) * 512]
    _CACHE["last_results"] = res.results
    return out
```

```python
import math
import os
from contextlib import ExitStack

import numpy as np
import ml_dtypes
import concourse.bass as bass
import concourse.mybir as mybir
from concourse.bass_utils import run_bass_kernel_spmd

F32 = mybir.dt.float32
BF16 = mybir.dt.bfloat16
AF = mybir.ActivationFunctionType
ALU = mybir.AluOpType
NPBF = ml_dtypes.bfloat16

D = 1024
T = 4096
NB = 4
G = 32
FF = 2816
NFT = FF // 128
EPS = 1e-6
MAGIC = 12582912.0
C1 = 6.28125
C2 = 2.0 * math.pi - 6.28125
PI_LO = 3.1415925
NEG_BIG = -30000.0
OWN = {0: [0, 3, 4, 7], 1: [1, 2, 5, 6]}
DEBUG = False


class Prog:
    ENGS = ("pe", "act", "dve", "pool", "sp")

    def __init__(self):
        self.streams = {e: [] for e in self.ENGS}
        self.count = {}
        self.known = {e: {} for e in self.ENGS}
        self.lastw = {}
        self.readers = {}
        self.dma_sems = set()
        self.enabled = True
        self.phase = 0

    def _deps(self, eng, reads, writes):
        deps = {}

        def add(tok):
            if tok is None:
                return
            s, v = tok
            if deps.get(s, 0) < v:
                deps[s] = v

        for k in reads:
            add(self.lastw.get(k))
        for k in writes:
            add(self.lastw.get(k))
            for s, v in self.readers.get(k, {}).items():
                add((s, v))
        out = []
        for s, v in deps.items():
            if s == "pe" and eng == "pe":
                continue
            if self.known[eng].get(s, 0) >= v:
                continue
            self.known[eng][s] = v
            out.append((s, v))
        return out

    def _commit(self, tok, reads, writes):
        s, v = tok
        for k in reads:
            d = self.readers.setdefault(k, {})
            if d.get(s, 0) < v:
                d[s] = v
        for k in writes:
            self.lastw[k] = tok
            self.readers[k] = {}

    def op(self, eng, fn, reads=(), writes=()):
        if not self.enabled:
            return
        waits = self._deps(eng, reads, writes)
        v = self.count.get(eng, 0) + 1
        self.count[eng] = v
        self.streams[eng].append((waits, fn, (eng, 1)))
        self._commit((eng, v), reads, writes)

    def dma(self, fn, sem, reads=(), writes=(), q="sp"):
        if not self.enabled:
            return
        self.dma_sems.add(sem)
        waits = self._deps(q, reads, writes)
        v = self.count.get(sem, 0) + 16
        self.count[sem] = v
        self.streams[q].append((waits, fn, (sem, 16)))
        self._commit((sem, v), reads, writes)

    def barrier(self, exclude=()):
        if not self.enabled:
            return
        toks = [(s, v) for s, v in self.count.items() if s not in exclude and v > 0]
        for e in self.ENGS:
            if e in exclude:
                continue
            waits = []
            for s, v in toks:
                if s == e:
                    continue
                if self.known[e].get(s, 0) >= v:
                    continue
                self.known[e][s] = v
                waits.append((s, v))
            if waits:
                self.streams[e].append((waits, None, None))

    def final_wait(self, eng, semnames):
        waits = [(s, self.count[s]) for s in semnames if self.count.get(s, 0) > 0]
        self.streams[eng].append((waits, None, None))

    def emit(self, block, sems):
        engmap = {"pe": block.tensor, "act": block.scalar, "dve": block.vector,
                  "pool": block.gpsimd, "sp": block.sync}
        for e in self.ENGS:
            stream = self.streams[e]

            def body(engine, stream=stream):
                for waits, fn, inc in stream:
                    for s, v in waits:
                        engine.wait_ge(sems[s], v)
                    if fn is not None:
                        ins = fn(engine)
                        ins.then_inc(sems[inc[0]], inc[1])

            engmap[e](body)


def bc_last(ap, n):
    shp = list(ap.shape)
    if len(shp) == 2:
        return ap.rearrange("p (a o) -> p a o", o=1).to_broadcast([shp[0], shp[1], n])
    return ap.rearrange("p a (b o) -> p a b o", o=1).to_broadcast([shp[0], shp[1], shp[2], n])


def build_program():
    nc = bass.Bass("TRN2", target_bir_lowering=False)
    p = Prog()

    def din(name, shape, dt=F32):
        return nc.dram_tensor(name, list(shape), dt, kind="ExternalInput").ap()

    x_all = din("x_all", [T, D])
    x_own = din("x_own", [2048, D])
    cT_d = din("cT", [128, 8])
    w_ada = din("w_ada", [D, 6 * D])
    b_adaT_d = din("b_adaT", [128, 48])
    b_ada_row = din("b_ada_row", [1, 6 * D])
    n1gT_d = din("n1gT", [128, 8])
    n2gT_d = din("n2gT", [128, 8])
    nfg_row = din("nfg_row", [1, D])
    w_in = din("w_in", [D, 4096])
    lamre_d = din("lamre_p", [128, 16])
    lamim_d = din("lamim_p", [128, 16])
    logdt_d = din("logdt_p", [128, 16])
    bre_d = din("bre_p", [128, 16, 16])
    bim_d = din("bim_p", [128, 16, 16])
    cre_d = din("cre_p", [128, 16, 16])
    cim_d = din("cim_p", [128, 16, 16])
    dvec_d = din("dvec", [128, 32])
    w_glu = din("w_glu", [512, 512])
    b_gluT_d = din("b_gluT", [128, 4])
    w_a = din("w_a", [512, D])
    w_b = din("w_b", [512, D])
    w_o = din("w_o", [D, D])
    wg = din("wg", [D, FF])
    wu = din("wu", [D, FF])
    wd = din("wd", [FF, D])
    identf_d = din("identf", [128, 128])
    atri_d = din("atri", [128, 128], BF16)
    neguincl_d = din("neguincl", [128, 128], BF16)
    neglow_d = din("neglow", [128, 128], BF16)
    cmask_d = din("cmask", [128, 128])
    kv_d = din("kv", [128, 32])
    maskB_d = din("maskB", [128, 4, 8, 512], BF16)
    sel_d = din("sel", [128, 4, 64], BF16)
    ones_row_d = din("ones_row", [1, 128])
    hmask_d = din("hmask", [128, 2])
    out_d = nc.dram_tensor("out", [2048, D], F32, kind="ExternalOutput").ap()
    dbg = {}

    def scratch(name, shape, dt=BF16):
        return nc.dram_tensor(name, list(shape), dt).ap()

    win_s = scratch("win_s", [D, 4096])
    wglu_s = scratch("wglu_s", [512, 512])
    wa_s = scratch("wa_s", [512, D])
    wb_s = scratch("wb_s", [512, D])
    wo_s = scratch("wo_s", [D, D])
    wg_s = scratch("wg_s", [NFT, 128, 8, 128])
    wu_s = scratch("wu_s", [NFT, 128, 8, 128])
    wd_s = scratch("wd_s", [FF, D])
    x1_s = scratch("x1_s", [2048, D], F32)
    sglu_s = scratch("sglu_s", [128, 4, 2048])
    attn_s = scratch("attn_s", [128, 4, 2048])

    semnames = ["pe", "act", "dve", "pool", "st_dbg", "st_sg", "st_at"]
    RING_SEMS = {"ld_cast": 2, "ld_x": 8, "ld_w": 6, "ld_wada": 2, "ld_mask": 2, "ld_wgu": 3, "ld_x1": 2,
                 "st_cast": 8, "st_out": 2, "st_x1": 2, "ldc": 30, "ld_sa": 2, "ld_ada": 2, "ld_sb": 2, "ld_wgt": 3, "ld_wgv": 3}
    for k, n in RING_SEMS.items():
        for i in range(n):
            semnames.append(f"{k}{i}")
    CEX = ("pool", "ld_cast0", "ld_cast1") + tuple(f"st_cast{i}" for i in range(8))

    with ExitStack() as top:
        sems = {s: top.enter_context(nc.semaphore(s)) for s in semnames}
        block = top.enter_context(nc.Block())

        def sbt(st, name, shape, dt):
            return st.enter_context(nc.sbuf_tensor("sb_" + name, list(shape), dt))

        def pst(st, name, shape, dt=F32):
            return st.enter_context(nc.psum_tensor("pp_" + name, list(shape), dt))

        uid = [0]

        def newkey(prefix="k"):
            uid[0] += 1
            return f"{prefix}#{uid[0]}"

        def phase_end(dumps=(), exclude=CEX):
            p.barrier(exclude=exclude)
            p.phase += 1
            if p.phase >= int(os.environ.get("K_STOP", "99")):
                p.enabled = False
            if DEBUG and dumps:
                for (name, ap, shape, dt) in dumps:
                    o = nc.dram_tensor("dbg_" + name, list(shape), dt, kind="ExternalOutput").ap()
                    dbg[name] = o
                    p.dma(lambda e, o=o, ap=ap: e.dma_start(out=o, in_=ap), "st_dbg")
                p.barrier(exclude=exclude)

        subc = [0]

        def ckpt():
            subc[0] += 1
            if subc[0] >= int(os.environ.get("K_SUB", "999")):
                p.enabled = False

        cctr = [0]

        def cload(tile, src, key):
            i = cctr[0]
            cctr[0] += 1
            p.dma(lambda e: e.dma_start(out=tile, in_=src), f"ldc{i}", writes=[key])

        identf = sbt(top, "identf", [128, 128], F32)
        identb = sbt(top, "identb", [128, 128], BF16)
        cload(identf[:], identf_d, "identf")
        p.op("dve", lambda e: e.tensor_copy(out=identb[:], in_=identf[:]), reads=["identf"], writes=["identb"])
        modT = sbt(top, "modT", [128, 4, 8], F32)
        geff1 = sbt(top, "geff1", [128, 8], F32)
        geff2 = sbt(top, "geff2", [128, 8], F32)
        g1bc = sbt(top, "g1bc", [128, D], F32)
        g2bc = sbt(top, "g2bc", [128, D], F32)
        vecs = sbt(top, "vecs", [128, 96], F32)
        vctr = [0]

        def newvec():
            i = vctr[0] % 96
            vctr[0] += 1
            return vecs[:, i:i + 1], f"vec{i}"

        epsv = sbt(top, "epsv", [128, 1], F32)
        p.op("dve", lambda e: e.memset(epsv[:], EPS), writes=["epsv"])
        junk = sbt(top, "junk", [128, D], BF16)
        sh1 = modT[:, 0, :]
        sh2 = modT[:, 2, :]

        CW = 1408
        cst_in = [sbt(top, f"cst_in{i}", [128, CW], F32) for i in range(2)]
        cst_out = [sbt(top, f"cst_out{i}", [128, CW], BF16) for i in range(2)]
        cast_jobs = []
        cast_done = {}
        cast_pos = [0]

        def add_cast(src_ap, ncols, dst_fn, done_key):
            cast_jobs.append((src_ap, ncols, dst_fn, done_key))

        for kt in range(8):
            for h in range(4):
                add_cast(w_in[kt * 128:(kt + 1) * 128, h * 1024:(h + 1) * 1024], 1024,
                         (lambda kt=kt, h=h: win_s[kt * 128:(kt + 1) * 128, h * 1024:(h + 1) * 1024]), "win_s")
        for kt in range(4):
            add_cast(w_glu[kt * 128:(kt + 1) * 128, :], 512, (lambda kt=kt: wglu_s[kt * 128:(kt + 1) * 128, :]), "wglu_s")
        for kt in range(4):
            add_cast(w_a[kt * 128:(kt + 1) * 128, :], 1024, (lambda kt=kt: wa_s[kt * 128:(kt + 1) * 128, :]), "wa_s")
        for kt in range(4):
            add_cast(w_b[kt * 128:(kt + 1) * 128, :], 1024, (lambda kt=kt: wb_s[kt * 128:(kt + 1) * 128, :]), "wb_s")
        for kt in range(8):
            add_cast(w_o[kt * 128:(kt + 1) * 128, :], 1024, (lambda kt=kt: wo_s[kt * 128:(kt + 1) * 128, :]), "wo_s")
        for ft in range(NFT):
            add_cast(wd[ft * 128:(ft + 1) * 128, :], 1024, (lambda ft=ft: wd_s[ft * 128:(ft + 1) * 128, :]), "wd_s")
        for (wsrc, wdst, key) in ((wg, wg_s, "wg_s"), (wu, wu_s, "wu_s")):
            for kt in range(8):
                for h in range(2):
                    add_cast(wsrc[kt * 128:(kt + 1) * 128, h * 1408:(h + 1) * 1408], 1408,
                             (lambda kt=kt, h=h, wdst=wdst: wdst[h * 11:(h + 1) * 11, :, kt, :].rearrange("ft p f -> p ft f")),
                             key)

        def cast_step(n):
            for _ in range(n):
                j = cast_pos[0]
                if j >= len(cast_jobs):
                    return
                cast_pos[0] += 1
                src, ncols, dst_fn, key = cast_jobs[j]
                r = j % 8
                dst = dst_fn()
                if len(dst.shape) == 3:
                    srcv = src.rearrange("p (ft f) -> p ft f", f=128)
                else:
                    srcv = src
                deps = [cast_keys[j - 8]] if j >= 8 else []
                p.dma(lambda e, dst=dst, srcv=srcv: e.dma_start(out=dst, in_=srcv, max_dma_last_dim=1024), f"st_cast{r}",
                      reads=deps, writes=[f"{key}#{j}"], q="pool")
                cast_keys.append(f"{key}#{j}")
                cast_done.setdefault(key, []).append(f"{key}#{j}")

        cast_keys = []

        def cast_until(key_names):
            need = [i for i, jb in enumerate(cast_jobs) if jb[3] in key_names]
            if need:
                last = max(need)
                if cast_pos[0] <= last:
                    cast_step(last + 1 - cast_pos[0])
            ks = []
            for k in key_names:
                ks += cast_done.get(k, [])
            return ks

        def load_w(tile, src_ap, key, deps, semname):
            p.dma(lambda e: e.dma_start(out=tile, in_=src_ap), semname, reads=deps, writes=[key])

        cT = sbt(top, "cT", [128, 8], F32)
        condf = sbt(top, "condf", [128, 8], F32)
        cond_rep = sbt(top, "cond_rep", [128, 8, 128], BF16)
        b_adaT = sbt(top, "b_adaT", [128, 48], F32)
        n1gT = sbt(top, "n1gT", [128, 8], F32)
        n2gT = sbt(top, "n2gT", [128, 8], F32)
        ones_row = sbt(top, "ones_row", [1, 128], F32)
        cload(cT[:], cT_d, "cT")
        cload(b_adaT[:], b_adaT_d, "b_adaT")
        cload(n1gT[:], n1gT_d, "n1gT")
        cload(n2gT[:], n2gT_d, "n2gT")
        cload(ones_row[:], ones_row_d, "ones_row")
        p.op("act", lambda e: e.activation(out=condf[:], in_=cT[:], func=AF.Silu), reads=["cT"], writes=["condf"])
        p.op("dve", lambda e: e.tensor_copy(out=cond_rep[:], in_=bc_last(condf[:], 128)), reads=["condf"], writes=["cond_rep"])
        w_ada_v = w_ada.rearrange("(kt p) c -> p kt c", p=128)
        slot_of = {0: 0, 1: 1, 3: 2, 4: 3}

        def ada_half(grp, h, wt, wtk, ps, pk, brow, browk):
            if grp in slot_of:
                sl = slot_of[grp]
                for ct in range(4):
                    for kt in range(8):
                        p.op("pe", lambda e, ct=ct, kt=kt: e.matmul(
                            ps[:, 2 * ct:2 * ct + 2], lhsT=wt[:, kt, ct * 128:(ct + 1) * 128],
                            rhs=cond_rep[:, kt, 0:2], start=(kt == 0), stop=(kt == 7)),
                             reads=[wtk, "cond_rep"], writes=[pk])
                for ct in range(4):
                    col = grp * 8 + h * 4 + ct
                    p.op("act", lambda e, ct=ct, col=col: e.activation(
                        out=modT[:, sl, h * 4 + ct:h * 4 + ct + 1], in_=ps[:, 2 * ct:2 * ct + 1], func=AF.Identity, bias=b_adaT[:, col:col + 1]),
                         reads=[pk, "b_adaT"], writes=[f"modT{sl}_{h}" if ct == 3 else newkey("modT")])
            else:
                dst = g1bc if grp == 2 else g2bc
                dk = "g1bc" if grp == 2 else "g2bc"
                for kt in range(8):
                    p.op("pe", lambda e, kt=kt: e.matmul(
                        ps[:, :], lhsT=cond_rep[:, kt, :], rhs=wt[:, kt, :],
                        start=(kt == 0), stop=False), reads=[wtk, "cond_rep"], writes=[pk])
                p.op("pe", lambda e: e.matmul(
                    ps[:, :], lhsT=ones_row[0:1, :], rhs=brow[0:1, :],
                    start=False, stop=True), reads=["ones_row", browk], writes=[pk])
                p.op("act", lambda e: e.activation(out=dst[:, h * 512:(h + 1) * 512], in_=ps[:, :], func=AF.Copy),
                     reads=[pk], writes=[f"{dk}_{h}"])

        def ada_load(grp, h, wt, wtk, sem, brow=None, browk=None, bsem=None):
            p.dma(lambda e: e.dma_start(out=wt[:], in_=w_ada_v[:, :, grp * 1024 + h * 512: grp * 1024 + (h + 1) * 512]),
                  sem, writes=[wtk], q="pool")
            if grp not in slot_of:
                p.dma(lambda e: e.dma_start(out=brow[0:1, :], in_=b_ada_row[0:1, grp * 1024 + h * 512: grp * 1024 + (h + 1) * 512]),
                      bsem, writes=[browk])

        def normA(xtile, xn4, xnk, extra_reads=(), scale_eng="dve"):
            for tt in range(4):
                xa, xk = xtile(tt)
                ss, ssk = newvec()
                sd, sdk = newvec()
                rs, rsk = newvec()
                p.op("act", lambda e, xa=xa, ss=ss: e.activation(out=junk[:], in_=xa, func=AF.Square, accum_out=ss),
                     reads=[xk] + list(extra_reads), writes=[ssk, "junk"])
                p.op("act", lambda e, ss=ss, sd=sd: e.activation(out=sd, in_=ss, func=AF.Sqrt, scale=1.0 / D, bias=epsv[:, 0:1]),
                     reads=[ssk, "epsv"], writes=[sdk])
                p.op("dve", lambda e, sd=sd, rs=rs: e.reciprocal(out=rs, in_=sd), reads=[sdk], writes=[rsk])
                if scale_eng == "act":
                    p.op("act", lambda e, tt=tt, xa=xa, rs=rs: e.activation(out=xn4[:, tt, :], in_=xa, func=AF.Identity, scale=rs),
                         reads=[xk, rsk], writes=[f"{xnk}_{tt}"])
                else:
                    p.op("dve", lambda e, tt=tt, xa=xa, rs=rs: e.tensor_scalar(out=xn4[:, tt, :], in0=xa, scalar1=rs, scalar2=None, op0=ALU.mult),
                         reads=[xk, rsk], writes=[f"{xnk}_{tt}"])

        def normB(ctx, xn4, xnk, hT, hTk, geff, geffk, sh, shk, col0=0, perm_half=None):
            psb = ctx["psb"]
            for kp in range(4):
                b = ctx["psb_ctr"][0] % len(psb)
                ctx["psb_ctr"][0] += 1
                pb = psb[b]; pbk = f"{ctx['name']}psb{b}"
                for kk in range(2):
                    kt = 2 * kp + kk
                    for tt in range(4):
                        p.op("pe", lambda e, pb=pb, kk=kk, tt=tt, kt=kt: e.transpose(
                            pb[:, kk * 512 + tt * 128: kk * 512 + (tt + 1) * 128], xn4[:, tt, kt * 128:(kt + 1) * 128], identb[:]),
                             reads=[f"{xnk}_{tt}", "identb"], writes=[pbk])
                for kk in range(2):
                    kt = 2 * kp + kk
                    if perm_half is None:
                        oap = hT[:, kt, col0:col0 + 512]
                        iap = pb[:, kk * 512:(kk + 1) * 512]
                    else:
                        oap = hT[:, kt, :].rearrange("p (i n) -> p i n", i=8)[:, :, 64 * perm_half:64 * perm_half + 64]
                        iap = pb[:, kk * 512:(kk + 1) * 512].rearrange("p (n i) -> p i n", i=8)
                    if kp % 2 == 0:
                        p.op("dve", lambda e, oap=oap, iap=iap, kt=kt: e.tensor_scalar(
                            out=oap, in0=iap,
                            scalar1=geff[:, kt:kt + 1], scalar2=sh[:, kt:kt + 1], op0=ALU.mult, op1=ALU.add),
                             reads=[pbk, geffk, shk], writes=[f"{hTk}_{kt}"])
                    else:
                        p.op("act", lambda e, oap=oap, iap=iap, kt=kt: e.activation(
                            out=oap, in_=iap, func=AF.Identity,
                            scale=geff[:, kt:kt + 1], bias=sh[:, kt:kt + 1]),
                             reads=[pbk, geffk, shk], writes=[f"{hTk}_{kt}"])

        xring_ctr = [0]

        def make_xloader(xt_tiles, prefix, rows_ap):
            cache = {}

            def xtile(tt):
                if tt not in cache:
                    i = xring_ctr[0] % len(xt_tiles)
                    xring_ctr[0] += 1
                    key = f"{prefix}{i}"
                    p.dma(lambda e, i=i, tt=tt: e.dma_start(out=xt_tiles[i][:], in_=rows_ap[tt * 128:(tt + 1) * 128, :]),
                          f"ld_x{i}", writes=[key])
                    cache[tt] = (xt_tiles[i][:], key)
                return cache[tt]

            return xtile

        s5o = ExitStack()
        Mbf = sbt(s5o, "Mbf", [128, G, 128], BF16)
        Gre = sbt(s5o, "Gre", [128, G, 64], BF16)
        Gim = sbt(s5o, "Gim", [128, G, 64], BF16)
        Hre = sbt(s5o, "Hre", [128, 16, 128], BF16)
        Hni = sbt(s5o, "Hni", [128, 16, 128], BF16)
        dec8 = sbt(s5o, "dec8", [128, 16], F32)
        c8 = sbt(s5o, "c8", [128, 16], F32)
        s8 = sbt(s5o, "s8", [128, 16], F32)
        lamre = sbt(s5o, "lamre", [128, 16], F32)
        lamim = sbt(s5o, "lamim", [128, 16], F32)
        logdt = sbt(s5o, "logdt", [128, 16], F32)
        bre = sbt(s5o, "bre", [128, 16, 16], F32)
        bim = sbt(s5o, "bim", [128, 16, 16], F32)
        cre = sbt(s5o, "cre", [128, 16, 16], F32)
        cim = sbt(s5o, "cim", [128, 16, 16], F32)
        dvec = sbt(s5o, "dvec", [128, 32], F32)
        cmask = sbt(s5o, "cmask", [128, 128], F32)
        kv = sbt(s5o, "kv", [128, 32], F32)
        hmask = sbt(s5o, "hmask", [128, 2], F32)
        cload(hmask[:], hmask_d, "hmask")
        for t_, d_, k_ in ((lamre, lamre_d, "lamre"), (lamim, lamim_d, "lamim"), (logdt, logdt_d, "logdt"),
                           (bre, bre_d, "bre"), (bim, bim_d, "bim"), (cre, cre_d, "cre"), (cim, cim_d, "cim"),
                           (dvec, dvec_d, "dvec"), (cmask, cmask_d, "cmask"), (kv, kv_d, "kv")):
            cload(t_[:], d_, k_)
        p0s = ExitStack()
        if True:
            wst = [sbt(p0s, f"wst{i}", [128, 8, 512], BF16) for i in range(2)]
            psm = [pst(p0s, f"psm{i}", [128, 512]) for i in range(4)]
            halves1 = ((1, 0), (1, 1), (0, 0), (0, 1), (2, 0), (2, 1), (4, 0), (4, 1), (3, 0), (3, 1), (5, 0), (5, 1))
            brow = [sbt(p0s, f"brow{i}", [1, 512], F32) for i in range(2)]
            for gi in range(2):
                ada_load(halves1[gi][0], halves1[gi][1], wst[gi], f"wst{gi}", f"ld_ada{gi}", brow[gi], f"brow{gi}", f"ld_w{4 + gi}")

        def emit_adaln():
            for gi in range(12):
                ada_half(halves1[gi][0], halves1[gi][1], wst[gi % 2], f"wst{gi % 2}", psm[gi % 4], f"psm{gi % 4}", brow[gi % 2], f"brow{gi % 2}")
                if gi + 2 < 12:
                    ada_load(halves1[gi + 2][0], halves1[gi + 2][1], wst[gi % 2], f"wst{gi % 2}", f"ld_ada{gi % 2}", brow[gi % 2], f"brow{gi % 2}", f"ld_w{4 + gi % 2}")
                if gi == 3:
                    p.op("dve", lambda e: e.scalar_tensor_tensor(out=geff1[:], in0=modT[:, 1, :], scalar=1.0, in1=n1gT[:], op0=ALU.add, op1=ALU.mult),
                         reads=["modT1_0", "modT1_1", "n1gT"], writes=["geff1"])
            p.op("dve", lambda e: e.scalar_tensor_tensor(out=geff2[:], in0=modT[:, 3, :], scalar=1.0, in1=n2gT[:], op0=ALU.add, op1=ALU.mult),
                 reads=["modT3_0", "modT3_1", "n2gT"], writes=["geff2"])

        cast_step(32)
        with ExitStack() as st:

            def small(name, shape=(128, 16)):
                return sbt(st, "s_" + name, list(shape), F32)

            dt_ = small("dt"); a_ = small("a"); phi = small("phi")
            p.op("act", lambda e: e.activation(out=dt_[:], in_=logdt[:], func=AF.Exp), reads=["logdt"], writes=["dt"])
            p.op("dve", lambda e: e.tensor_tensor(out=a_[:], in0=lamre[:], in1=dt_[:], op=ALU.mult), reads=["lamre", "dt"], writes=["a"])
            p.op("dve", lambda e: e.tensor_tensor(out=phi[:], in0=lamim[:], in1=dt_[:], op=ALU.mult), reads=["lamim", "dt"], writes=["phi"])
            AR = small("AR", (128, 16, 32)); ANG = small("ANG", (128, 16, 32)); RHO = small("RHO", (128, 16, 32))
            SINt = small("SINt", (128, 16, 32)); COSt = small("COSt", (128, 16, 32))
            PRE = small("PRE", (128, 16, 32)); PIM = small("PIM", (128, 16, 32))
            tA = small("tA", (128, 16, 32)); tB = small("tB", (128, 16, 32))
            kvb = kv[:].rearrange("p (o k) -> p o k", o=1).to_broadcast([128, 16, 32])
            p.op("dve", lambda e: e.tensor_tensor(out=AR[:], in0=bc_last(a_[:], 32), in1=kvb, op=ALU.mult), reads=["a", "kv"], writes=["AR"])
            p.op("dve", lambda e: e.tensor_tensor(out=ANG[:], in0=bc_last(phi[:], 32), in1=kvb, op=ALU.mult), reads=["phi", "kv"], writes=["ANG"])
            p.op("act", lambda e: e.activation(out=RHO[:], in_=AR[:], func=AF.Exp), reads=["AR"], writes=["RHO"])

            def range_reduce_sin(src, srck, dst, dstk, shift):
                p.op("dve", lambda e: e.tensor_scalar(out=tA[:], in0=src[:], scalar1=shift, scalar2=None, op0=ALU.add),
                     reads=[srck], writes=["tA"])
                p.op("dve", lambda e: e.tensor_scalar(out=tB[:], in0=tA[:], scalar1=1.0 / (2 * math.pi), scalar2=MAGIC, op0=ALU.mult, op1=ALU.add),
                     reads=["tA"], writes=["tB"])
                p.op("dve", lambda e: e.tensor_scalar(out=tB[:], in0=tB[:], scalar1=MAGIC, scalar2=None, op0=ALU.subtract),
                     reads=["tB"], writes=["tB"])
                p.op("dve", lambda e: e.scalar_tensor_tensor(out=tA[:], in0=tB[:], scalar=-C1, in1=tA[:], op0=ALU.mult, op1=ALU.add),
                     reads=["tB", "tA"], writes=["tA"])
                p.op("dve", lambda e: e.scalar_tensor_tensor(out=tA[:], in0=tB[:], scalar=-C2, in1=tA[:], op0=ALU.mult, op1=ALU.add),
                     reads=["tB", "tA"], writes=["tA"])
                p.op("dve", lambda e: e.tensor_scalar(out=tA[:], in0=tA[:], scalar1=-PI_LO, scalar2=PI_LO, op0=ALU.max, op1=ALU.min),
                     reads=["tA"], writes=["tA"])
                p.op("act", lambda e: e.activation(out=dst[:], in_=tA[:], func=AF.Sin), reads=["tA"], writes=[dstk])

            range_reduce_sin(ANG, "ANG", SINt, "SINt", 0.0)
            range_reduce_sin(ANG, "ANG", COSt, "COSt", math.pi / 2)
            p.op("dve", lambda e: e.tensor_tensor(out=PRE[:], in0=RHO[:], in1=COSt[:], op=ALU.mult), reads=["RHO", "COSt"], writes=["PRE"])
            p.op("dve", lambda e: e.tensor_tensor(out=PIM[:], in0=RHO[:], in1=SINt[:], op=ALU.mult), reads=["RHO", "SINt"], writes=["PIM"])
            ckpt()
            nr = small("nr"); den = small("den"); t1s = small("t1s"); t2s = small("t2s")
            bre_s = small("betare"); bim_s = small("betaim")
            lbre = PRE[:, :, 24]; lbim = PIM[:, :, 24]
            p.op("dve", lambda e: e.tensor_scalar(out=nr[:], in0=lbre, scalar1=-1.0, scalar2=None, op0=ALU.add), reads=["PRE"], writes=["nr"])
            p.op("dve", lambda e: e.tensor_tensor(out=den[:], in0=lamre[:], in1=lamre[:], op=ALU.mult), reads=["lamre"], writes=["den"])
            p.op("dve", lambda e: e.tensor_tensor(out=t1s[:], in0=lamim[:], in1=lamim[:], op=ALU.mult), reads=["lamim"], writes=["t1s"])
            p.op("dve", lambda e: e.tensor_tensor(out=den[:], in0=den[:], in1=t1s[:], op=ALU.add), reads=["den", "t1s"], writes=["den"])
            p.op("dve", lambda e: e.reciprocal(out=den[:], in_=den[:]), reads=["den"], writes=["den"])
            p.op("dve", lambda e: e.tensor_tensor(out=t1s[:], in0=nr[:], in1=lamre[:], op=ALU.mult), reads=["nr", "lamre"], writes=["t1s"])
            p.op("dve", lambda e: e.tensor_tensor(out=t2s[:], in0=lbim, in1=lamim[:], op=ALU.mult), reads=["PIM", "lamim"], writes=["t2s"])
            p.op("dve", lambda e: e.tensor_tensor(out=t1s[:], in0=t1s[:], in1=t2s[:], op=ALU.add), reads=["t1s", "t2s"], writes=["t1s"])
            p.op("dve", lambda e: e.tensor_tensor(out=bre_s[:], in0=t1s[:], in1=den[:], op=ALU.mult), reads=["t1s", "den"], writes=["betare"])
            p.op("dve", lambda e: e.tensor_tensor(out=t1s[:], in0=lbim, in1=lamre[:], op=ALU.mult), reads=["PIM", "lamre"], writes=["t1s"])
            p.op("dve", lambda e: e.tensor_tensor(out=t2s[:], in0=nr[:], in1=lamim[:], op=ALU.mult), reads=["nr", "lamim"], writes=["t2s"])
            p.op("dve", lambda e: e.tensor_tensor(out=t1s[:], in0=t1s[:], in1=t2s[:], op=ALU.subtract), reads=["t1s", "t2s"], writes=["t1s"])
            p.op("dve", lambda e: e.tensor_tensor(out=bim_s[:], in0=t1s[:], in1=den[:], op=ALU.mult), reads=["t1s", "den"], writes=["betaim"])
            big1 = sbt(st, "big1", [128, 16, 8, 16], F32)
            big2 = sbt(st, "big2", [128, 16, 8, 16], F32)

            def cmul(outre, outrek, outim, outimk, are, arek, aim, aimk, bre_, brek, bim_, bimk, shape, neg_im=False, eng="dve"):
                n = 1
                for s_ in shape[1:]:
                    n *= s_
                if len(shape) == 3:
                    t1 = big1[:].rearrange("p a b c -> p (a b c)")[:, 0:n].rearrange("p (a b) -> p a b", b=shape[2])
                    t2 = big2[:].rearrange("p a b c -> p (a b c)")[:, 0:n].rearrange("p (a b) -> p a b", b=shape[2])
                else:
                    t1 = big1[:]
                    t2 = big2[:]
                p.op(eng, lambda e: e.tensor_tensor(out=t1, in0=are, in1=bre_, op=ALU.mult), reads=[arek, brek], writes=["big1"])
                p.op(eng, lambda e: e.tensor_tensor(out=t2, in0=aim, in1=bim_, op=ALU.mult), reads=[aimk, bimk], writes=["big2"])
                p.op(eng, lambda e: e.tensor_tensor(out=outre, in0=t1, in1=t2, op=ALU.subtract), reads=["big1", "big2"], writes=[outrek])
                p.op(eng, lambda e: e.tensor_tensor(out=t1, in0=are, in1=bim_, op=ALU.mult), reads=[arek, bimk], writes=["big1"])
                p.op(eng, lambda e: e.tensor_tensor(out=t2, in0=aim, in1=bre_, op=ALU.mult), reads=[aimk, brek], writes=["big2"])
                if neg_im:
                    p.op(eng, lambda e: e.scalar_tensor_tensor(out=outim, in0=t1, scalar=-1.0, in1=t2, op0=ALU.mult, op1=ALU.subtract),
                         reads=["big1", "big2"], writes=[outimk])
                else:
                    p.op(eng, lambda e: e.tensor_tensor(out=outim, in0=t1, in1=t2, op=ALU.add), reads=["big1", "big2"], writes=[outimk])

            Bre = small("Bre", (128, 16, 16)); Bim = small("Bim", (128, 16, 16))
            cmul(Bre[:], "Bre", Bim[:], "Bim", bc_last(bre_s[:], 16), "betare", bc_last(bim_s[:], 16), "betaim",
                 bre[:], "bre", bim[:], "bim", (128, 16, 16))
            ckpt()
            Xre = sbt(st, "Xre", [128, 16, 8, 16], F32); Xim = sbt(st, "Xim", [128, 16, 8, 16], F32)
            XGre = sbt(st, "XGre", [128, 16, 8, 16], F32); XGim = sbt(st, "XGim", [128, 16, 8, 16], F32)
            Yre = sbt(st, "Yre", [128, 16, 8, 16], F32); nYim = sbt(st, "nYim", [128, 16, 8, 16], F32)
            sh4 = (128, 16, 8, 16)

            def pw(tab, lo):
                return bc_last(tab[:, :, lo:lo + 8], 16)

            def mid(t):
                return t[:].rearrange("p q (o c) -> p q o c", o=1).to_broadcast([128, 16, 8, 16])

            cmul(Xre[:], "Xre", Xim[:], "Xim", pw(PRE, 0), "PRE", pw(PIM, 0), "PIM", mid(Bre), "Bre", mid(Bim), "Bim", sh4)
            cmul(XGre[:], "XGre", XGim[:], "XGim", pw(PRE, 8), "PRE", pw(PIM, 8), "PIM", mid(Bre), "Bre", mid(Bim), "Bim", sh4)
            cmul(Yre[:], "Yre", nYim[:], "nYim", pw(PRE, 16), "PRE", pw(PIM, 16), "PIM", mid(cre), "cre", mid(cim), "cim", sh4, neg_im=True)
            Hre4 = Hre[:].rearrange("p q (j o) -> p q j o", o=16)
            Hni4 = Hni[:].rearrange("p q (j o) -> p q j o", o=16)
            cmul(Hre4, "Hre", Hni4, "Hni", pw(PRE, 24), "PRE", pw(PIM, 24), "PIM", mid(cre), "cre", mid(cim), "cim", sh4, neg_im=True)
            emit_adaln()
            psS = [pst(st, f"psS{i}", [128, 512]) for i in range(4)]
            pctr = 0
            b1f = big1[:].rearrange("p a b c -> p (a b c)")
            Ym = sbt(st, "Ym", [128, 16, 8, 16], F32)
            nYm = sbt(st, "nYm", [128, 16, 8, 16], F32)
            for hf in range(2):
                p.op("dve", lambda e, hf=hf: e.tensor_scalar(out=Ym[:], in0=Yre[:], scalar1=hmask[:, hf:hf + 1], scalar2=None, op0=ALU.mult),
                     reads=["Yre", "hmask"], writes=["Ym"])
                p.op("dve", lambda e, hf=hf: e.tensor_scalar(out=nYm[:], in0=nYim[:], scalar1=hmask[:, hf:hf + 1], scalar2=None, op0=ALU.mult),
                     reads=["nYim", "hmask"], writes=["nYm"])
                for q4 in range(4):
                    ps = psS[pctr % 4]; pk = f"psS{pctr % 4}"; pctr += 1
                    for gg in range(4):
                        q = q4 * 4 + gg
                        p.op("pe", lambda e, ps=ps, gg=gg, q=q: e.matmul(
                            ps[:, gg * 128:(gg + 1) * 128], lhsT=Xre[:, q, :, :].rearrange("p i c -> p (i c)"),
                            rhs=Ym[:, q, :, :].rearrange("p j o -> p (j o)"), start=True, stop=False),
                             reads=["Xre", "Ym"], writes=[pk])
                        p.op("pe", lambda e, ps=ps, gg=gg, q=q: e.matmul(
                            ps[:, gg * 128:(gg + 1) * 128], lhsT=Xim[:, q, :, :].rearrange("p i c -> p (i c)"),
                            rhs=nYm[:, q, :, :].rearrange("p j o -> p (j o)"), start=False, stop=True),
                             reads=["Xim", "nYm"], writes=[pk])
                    for gg in range(4):
                        q = q4 * 4 + gg
                        g = 2 * q + hf
                        p.op("dve", lambda e, ps=ps, gg=gg: e.tensor_tensor(
                            out=b1f[:, gg * 128:(gg + 1) * 128], in0=ps[:, gg * 128:(gg + 1) * 128], in1=cmask[:], op=ALU.mult),
                             reads=[pk, "cmask"], writes=["big1"])
                        p.op("dve", lambda e, g=g, gg=gg: e.scalar_tensor_tensor(
                            out=Mbf[:, g, :], in0=identf[:], scalar=dvec[:, g:g + 1],
                            in1=b1f[:, gg * 128:(gg + 1) * 128], op0=ALU.mult, op1=ALU.add),
                             reads=["identf", "dvec", "big1"], writes=["Mbf"])
            ckpt()
            for (src, srck, dstt, dstk) in ((XGre, "XGre", Gre, "Gre"), (XGim, "XGim", Gim, "Gim")):
                for q8 in range(4):
                    ps = psS[pctr % 4]; pk = f"psS{pctr % 4}"; pctr += 1
                    for qq in range(4):
                        q = q8 * 4 + qq
                        p.op("pe", lambda e, ps=ps, qq=qq, q=q, src=src: e.transpose(
                            ps[:, qq * 128:(qq + 1) * 128], src[:, q, :, :].rearrange("p i c -> p (i c)"), identf[:]),
                             reads=[srck, "identf"], writes=[pk])
                    p.op("act", lambda e, ps=ps, q8=q8, dstt=dstt: e.activation(
                        out=dstt[:, q8 * 8:(q8 + 1) * 8, :].rearrange("p g c -> p (g c)"), in_=ps[:, :], func=AF.Copy),
                         reads=[pk], writes=[dstk])
            ckpt()
            p.op("dve", lambda e: e.tensor_copy(out=dec8[:], in_=RHO[:, :, 31]), reads=["RHO"], writes=["dec8"])
            p.op("dve", lambda e: e.tensor_copy(out=c8[:], in_=COSt[:, :, 31]), reads=["COSt"], writes=["c8"])
            p.op("dve", lambda e: e.tensor_copy(out=s8[:], in_=SINt[:, :, 31]), reads=["SINt"], writes=["s8"])
            phase_end([("PRE", PRE[:].rearrange("p a b -> p (a b)"), [128, 512], F32),
                       ("PIM", PIM[:].rearrange("p a b -> p (a b)"), [128, 512], F32),
                       ("Mbf", Mbf[:].rearrange("p g c -> p (g c)"), [128, G * 128], BF16),
                       ("Gre", Gre[:].rearrange("p g c -> p (g c)"), [128, G * 64], BF16),
                       ("Hre", Hre[:].rearrange("p g c -> p (g c)"), [128, 16 * 128], BF16)])
        p0s.close()

        with ExitStack() as s5:
            Ytm = sbt(s5, "Ytm", [128, NB, 8, 512], BF16)
            with ExitStack() as s5a:
                U = sbt(s5a, "U", [128, NB, G, 8, 16], BF16)
                with ExitStack() as st:
                    ctx = {"name": "p1a", "psb_ctr": [0],
                           "psb": [pst(st, f"p1a_psb{i}", [128, 1024], BF16) for i in range(2)]}
                    psf = [pst(st, f"p1a_psf{i}", [128, 512]) for i in range(6)]
                    xt = [sbt(st, f"p1a_xt{i}", [128, D], F32) for i in range(8)]
                    xn4 = [sbt(st, f"p1a_xn{i}", [128, 4, D], BF16) for i in range(2)]
                    hT = [sbt(st, f"p1a_hT{i}", [128, 8, 1024], BF16) for i in range(1)]
                    wU = sbt(st, "wU", [128, 8, 512], BF16)
                    wUs = sbt(st, "wUs", [128, 4, 512], F32)
                    for hh in range(2):
                        p.dma(lambda e, hh=hh: e.dma_start(out=wUs[:], in_=w_in.rearrange("(kt p) c -> p kt c", p=128)[:, hh * 4:(hh + 1) * 4, 0:512]),
                              "ld_w0", writes=["wUs"])
                        p.op("act", lambda e, hh=hh: e.activation(out=wU[:, hh * 4:(hh + 1) * 4, :].rearrange("p a b -> p (a b)"),
                                                                  in_=wUs[:].rearrange("p a b -> p (a b)"), func=AF.Copy),
                             reads=["wUs"], writes=[f"wU{hh}"])
                    pf = 0
                    normA(make_xloader(xt, "p1a_xt", x_all[0:512, :]), xn4[0], "p1a_xn0")
                    ckpt()
                    for c in range(8):
                        nbk = c // 2
                        r = c % 2
                        hb = hT[0]; hbk = f"p1a_hT0_{c % 2}"
                        normB(ctx, xn4[r], f"p1a_xn{r}", hb, hbk, geff1, "geff1", sh1, "modT0_1", perm_half=(c % 2))
                        if c == 0:
                            ckpt()
                        if c + 1 < 8:
                            r2 = (c + 1) % 2
                            normA(make_xloader(xt, "p1a_xt", x_all[(c + 1) * 512:(c + 2) * 512, :]), xn4[r2], f"p1a_xn{r2}")
                        if c % 2 == 1:
                            hk0 = "p1a_hT0_0"; hk1 = "p1a_hT0_1"
                            for i in range(8):
                                ps = psf[pf % 6]; pk = f"p1a_psf{pf % 6}"; pf += 1
                                for kt in range(8):
                                    p.op("pe", lambda e, ps=ps, hb=hb, kt=kt, i=i: e.matmul(
                                        ps[:, :], lhsT=hb[:, kt, i * 128:(i + 1) * 128], rhs=wU[:, kt, :], start=(kt == 0), stop=(kt == 7)),
                                         reads=[f"{hk0}_{kt}", f"{hk1}_{kt}", "wU0", "wU1"], writes=[pk])
                                if i % 2 == 0:
                                    p.op("act", lambda e, ps=ps, nbk=nbk, i=i: e.activation(
                                        out=U[:, nbk, :, i, :], in_=ps[:, :].rearrange("p (g c) -> p g c", c=16), func=AF.Copy),
                                         reads=[pk], writes=[newkey("U")])
                                else:
                                    p.op("dve", lambda e, ps=ps, nbk=nbk, i=i: e.tensor_copy(
                                        out=U[:, nbk, :, i, :], in_=ps[:, :].rearrange("p (g c) -> p g c", c=16)),
                                         reads=[pk], writes=[newkey("U")])
                        if c == 1:
                            ckpt()
                        cast_step(2)
                    phase_end([("U", U[:].rearrange("p a g i c -> p (a g i c)"), [128, NB * G * 128], BF16)])

                CS = sbt(s5a, "CS", [128, 8, 512], F32)
                SN = sbt(s5a, "SN", [128, 8, 512], F32)
                with ExitStack() as st:
                    Ug = [[sbt(st, f"Ug{r}_{h}", [128, 512], BF16) for h in range(2)] for r in range(2)]
                    psb = [pst(st, f"p2_psb{i}", [128, 1024], BF16) for i in range(2)]
                    psZ = [pst(st, f"p2_psZ{i}", [128, 512]) for i in range(4)]
                    psY = [pst(st, f"p2_psY{i}", [128, 512]) for i in range(2)]
                    tmp = [sbt(st, f"p2_tmp{i}", [128, 512], F32) for i in range(4)]
                    Win = [sbt(st, f"Win{c}", [128, 512], F32) for c in range(2)]
                    Wst = [sbt(st, f"Wst{c}", [128, 512], F32) for c in range(2)]
                    Sp = [[sbt(st, f"Sp{r}_{c}", [128, 512], BF16) for c in range(2)] for r in range(2)]
                    dt1 = sbt(st, "dt1", [128, 8, 256], F32)
                    dt2 = sbt(st, "dt2", [128, 8, 256], F32)
                    cmt = [sbt(st, f"cmt{i}", [128, 8], F32) for i in range(2)]
                    smt = [sbt(st, f"smt{i}", [128, 8], F32) for i in range(2)]
                    sq1 = sbt(st, "sq1", [128, 8], F32)
                    sq2 = sbt(st, "sq2", [128, 8], F32)
                    for r in range(2):
                        for c in range(2):
                            p.op("dve", lambda e, r=r, c=c: e.memset(Sp[r][c][:, 0:1], 0.0), writes=[f"Sp{r}_{c}"])
                    yctr = 0
                    for half in range(2):
                        q0 = half * 8
                        p.op("dve", lambda e, q0=q0: e.tensor_copy(out=cmt[0][:], in_=c8[:, q0:q0 + 8]), reads=["c8"], writes=["cmt0"])
                        p.op("dve", lambda e, q0=q0: e.tensor_copy(out=smt[0][:], in_=s8[:, q0:q0 + 8]), reads=["s8"], writes=["smt0"])
                        p.op("dve", lambda e: e.memset(CS[:, :, 0:1], 1.0), writes=["CS"] + [f"CS{qq}" for qq in range(8)])
                        p.op("dve", lambda e: e.memset(SN[:, :, 0:1], 0.0), writes=["SN"] + [f"SN{qq}" for qq in range(8)])
                        m = 1
                        it = 0
                        while m < 512:
                            c_, s_ = cmt[it % 2], smt[it % 2]
                            ck, sk = f"cmt{it % 2}", f"smt{it % 2}"
                            if m < 32:
                                t1 = dt1[:, :, 0:m]
                                t2 = dt2[:, :, 0:m]
                                cb = bc_last(c_[:], m); sb_ = bc_last(s_[:], m)
                                p.op("dve", lambda e, t1=t1, cb=cb, m=m: e.tensor_tensor(out=t1, in0=CS[:, :, 0:m], in1=cb, op=ALU.mult), reads=["CS", ck], writes=["dt1"])
                                p.op("dve", lambda e, t2=t2, sb_=sb_, m=m: e.tensor_tensor(out=t2, in0=SN[:, :, 0:m], in1=sb_, op=ALU.mult), reads=["SN", sk], writes=["dt2"])
                                p.op("dve", lambda e, t1=t1, t2=t2, m=m: e.tensor_tensor(out=CS[:, :, m:2 * m], in0=t1, in1=t2, op=ALU.subtract), reads=["dt1", "dt2"], writes=["CS"])
                                p.op("dve", lambda e, t1=t1, cb=cb, m=m: e.tensor_tensor(out=t1, in0=SN[:, :, 0:m], in1=cb, op=ALU.mult), reads=["SN", ck], writes=["dt1"])
                                p.op("dve", lambda e, t2=t2, sb_=sb_, m=m: e.tensor_tensor(out=t2, in0=CS[:, :, 0:m], in1=sb_, op=ALU.mult), reads=["CS", sk], writes=["dt2"])
                                p.op("dve", lambda e, t1=t1, t2=t2, m=m: e.tensor_tensor(out=SN[:, :, m:2 * m], in0=t1, in1=t2, op=ALU.add), reads=["dt1", "dt2"], writes=["SN"])
                            else:
                                for qq in range(8):
                                    p.op("act", lambda e, qq=qq, m=m, s_=s_: e.activation(out=dt1[:, qq, 0:m], in_=SN[:, qq, 0:m], func=AF.Identity, scale=s_[:, qq:qq + 1]),
                                         reads=["SN", f"SN{qq}", sk, "dt1"], writes=[f"dt1_{qq}"])
                                    p.op("act", lambda e, qq=qq, m=m, s_=s_: e.activation(out=dt2[:, qq, 0:m], in_=CS[:, qq, 0:m], func=AF.Identity, scale=s_[:, qq:qq + 1]),
                                         reads=["CS", f"CS{qq}", sk, "dt2"], writes=[f"dt2_{qq}"])
                                for qq in range(8):
                                    p.op("dve", lambda e, qq=qq, m=m, c_=c_: e.scalar_tensor_tensor(
                                        out=CS[:, qq, m:2 * m], in0=CS[:, qq, 0:m], scalar=c_[:, qq:qq + 1], in1=dt1[:, qq, 0:m], op0=ALU.mult, op1=ALU.subtract),
                                         reads=["CS", f"CS{qq}", ck, f"dt1_{qq}"], writes=[f"CS{qq}"])
                                    p.op("dve", lambda e, qq=qq, m=m, c_=c_: e.scalar_tensor_tensor(
                                        out=SN[:, qq, m:2 * m], in0=SN[:, qq, 0:m], scalar=c_[:, qq:qq + 1], in1=dt2[:, qq, 0:m], op0=ALU.mult, op1=ALU.add),
                                         reads=["SN", f"SN{qq}", ck, f"dt2_{qq}"], writes=[f"SN{qq}"])
                            if 2 * m < 512:
                                cn, sn_ = cmt[(it + 1) % 2], smt[(it + 1) % 2]
                                cnk, snk = f"cmt{(it + 1) % 2}", f"smt{(it + 1) % 2}"
                                p.op("dve", lambda e, c_=c_: e.tensor_tensor(out=sq1[:], in0=c_[:], in1=c_[:], op=ALU.mult), reads=[ck], writes=["sq1"])
                                p.op("dve", lambda e, s_=s_: e.tensor_tensor(out=sq2[:], in0=s_[:], in1=s_[:], op=ALU.mult), reads=[sk], writes=["sq2"])
                                p.op("dve", lambda e, cn=cn: e.tensor_tensor(out=cn[:], in0=sq1[:], in1=sq2[:], op=ALU.subtract), reads=["sq1", "sq2"], writes=[cnk])
                                p.op("dve", lambda e, sn_=sn_, c_=c_, s_=s_: e.scalar_tensor_tensor(out=sn_[:], in0=c_[:], scalar=2.0, in1=s_[:], op0=ALU.mult, op1=ALU.mult),
                                     reads=[ck, sk], writes=[snk])
                            m *= 2
                            it += 1
                        for ql in range(8):
                            q = q0 + ql
                            r = q % 2
                            for hf in range(2):
                                g = 2 * q + hf
                                pb = psb[hf]; pbk = f"p2_psb{hf}"
                                for nbk in range(4):
                                    p.op("pe", lambda e, pb=pb, nbk=nbk, g=g: e.transpose(
                                        pb[:, nbk * 128:(nbk + 1) * 128], U[:, nbk, g, :, :].rearrange("p i c -> p (i c)"), identb[:]),
                                         reads=["identb"], writes=[pbk])
                                if hf == 0:
                                    p.op("act", lambda e, pb=pb, r=r, hf=hf: e.activation(out=Ug[r][hf][:], in_=pb[:, 0:512], func=AF.Copy),
                                         reads=[pbk], writes=[f"Ug{r}_{hf}"])
                                else:
                                    p.op("dve", lambda e, pb=pb, r=r, hf=hf: e.tensor_copy(out=Ug[r][hf][:], in_=pb[:, 0:512]),
                                         reads=[pbk], writes=[f"Ug{r}_{hf}"])
                            zre = psZ[(2 * q) % 4]; zrek = f"p2_psZ{(2 * q) % 4}"
                            zim = psZ[(2 * q + 1) % 4]; zimk = f"p2_psZ{(2 * q + 1) % 4}"
                            for hf in range(2):
                                g = 2 * q + hf
                                lo, hi = hf * 64, hf * 64 + 64
                                p.op("pe", lambda e, zre=zre, g=g, lo=lo, hi=hi, r=r, hf=hf: e.matmul(
                                    zre[lo:hi, :], lhsT=Gre[:, g, :], rhs=Ug[r][hf][:], start=True, stop=True),
                                     reads=["Gre", f"Ug{r}_{hf}"], writes=[zrek])
                                p.op("pe", lambda e, zim=zim, g=g, lo=lo, hi=hi, r=r, hf=hf: e.matmul(
                                    zim[lo:hi, :], lhsT=Gim[:, g, :], rhs=Ug[r][hf][:], start=True, stop=True),
                                     reads=["Gim", f"Ug{r}_{hf}"], writes=[zimk])
                            p.op("dve", lambda e, zre=zre, ql=ql: e.tensor_tensor(out=tmp[0][:], in0=zre[:, :], in1=CS[:, ql, :], op=ALU.mult), reads=[zrek, "CS", f"CS{ql}"], writes=["p2_tmp0"])
                            p.op("dve", lambda e, zim=zim, ql=ql: e.tensor_tensor(out=tmp[1][:], in0=zim[:, :], in1=SN[:, ql, :], op=ALU.mult), reads=[zimk, "SN", f"SN{ql}"], writes=["p2_tmp1"])
                            p.op("dve", lambda e, zim=zim, ql=ql: e.tensor_tensor(out=tmp[2][:], in0=zim[:, :], in1=CS[:, ql, :], op=ALU.mult), reads=[zimk, "CS", f"CS{ql}"], writes=["p2_tmp2"])
                            p.op("dve", lambda e, zre=zre, ql=ql: e.tensor_tensor(out=tmp[3][:], in0=zre[:, :], in1=SN[:, ql, :], op=ALU.mult), reads=[zrek, "SN", f"SN{ql}"], writes=["p2_tmp3"])
                            p.op("dve", lambda e: e.tensor_tensor(out=Win[0][:], in0=tmp[0][:], in1=tmp[1][:], op=ALU.add), reads=["p2_tmp0", "p2_tmp1"], writes=["Win0"])
                            p.op("dve", lambda e: e.tensor_tensor(out=Win[1][:], in0=tmp[2][:], in1=tmp[3][:], op=ALU.subtract), reads=["p2_tmp2", "p2_tmp3"], writes=["Win1"])
                            for c in range(2):
                                p.op("dve", lambda e, c=c, q=q: e.tensor_tensor_scan(
                                    out=Wst[c][:], data0=dec8[:, q:q + 1].to_broadcast([128, 512]), data1=Win[c][:],
                                    initial=0.0, op0=ALU.mult, op1=ALU.add), reads=["dec8", f"Win{c}"], writes=[f"Wst{c}"])
                            p.op("dve", lambda e, ql=ql: e.tensor_tensor(out=tmp[0][:, 0:511], in0=Wst[0][:, 0:511], in1=CS[:, ql, 0:511], op=ALU.mult), reads=["Wst0", "CS", f"CS{ql}"], writes=["p2_tmp0"])
                            p.op("dve", lambda e, ql=ql: e.tensor_tensor(out=tmp[1][:, 0:511], in0=Wst[1][:, 0:511], in1=SN[:, ql, 0:511], op=ALU.mult), reads=["Wst1", "SN", f"SN{ql}"], writes=["p2_tmp1"])
                            p.op("dve", lambda e, ql=ql: e.tensor_tensor(out=tmp[2][:, 0:511], in0=Wst[1][:, 0:511], in1=CS[:, ql, 0:511], op=ALU.mult), reads=["Wst1", "CS", f"CS{ql}"], writes=["p2_tmp2"])
                            p.op("dve", lambda e, ql=ql: e.tensor_tensor(out=tmp[3][:, 0:511], in0=Wst[0][:, 0:511], in1=SN[:, ql, 0:511], op=ALU.mult), reads=["Wst0", "SN", f"SN{ql}"], writes=["p2_tmp3"])
                            p.op("dve", lambda e, r=r: e.tensor_tensor(out=Sp[r][0][:, 1:512], in0=tmp[0][:, 0:511], in1=tmp[1][:, 0:511], op=ALU.subtract), reads=["p2_tmp0", "p2_tmp1"], writes=[f"Sp{r}_0"])
                            p.op("dve", lambda e, r=r: e.tensor_tensor(out=Sp[r][1][:, 1:512], in0=tmp[2][:, 0:511], in1=tmp[3][:, 0:511], op=ALU.add), reads=["p2_tmp2", "p2_tmp3"], writes=[f"Sp{r}_1"])
                            for nbk in range(4):
                                py = psY[yctr % 2]; pyk = f"p2_psY{yctr % 2}"; yctr += 1
                                for hf in range(2):
                                    g = 2 * q + hf
                                    lo, hi = hf * 64, hf * 64 + 64
                                    p.op("pe", lambda e, py=py, hf=hf, r=r, nbk=nbk, g=g: e.matmul(
                                        py[:, hf * 128:(hf + 1) * 128], lhsT=Ug[r][hf][:, nbk * 128:(nbk + 1) * 128], rhs=Mbf[:, g, :],
                                        start=True, stop=False), reads=[f"Ug{r}_{hf}"], writes=[pyk])
                                    p.op("pe", lambda e, py=py, hf=hf, r=r, nbk=nbk, q=q, lo=lo, hi=hi: e.matmul(
                                        py[:, hf * 128:(hf + 1) * 128], lhsT=Sp[r][0][lo:hi, nbk * 128:(nbk + 1) * 128], rhs=Hre[lo:hi, q, :],
                                        start=False, stop=False), reads=[f"Sp{r}_0"], writes=[pyk])
                                    p.op("pe", lambda e, py=py, hf=hf, r=r, nbk=nbk, q=q, lo=lo, hi=hi: e.matmul(
                                        py[:, hf * 128:(hf + 1) * 128], lhsT=Sp[r][1][lo:hi, nbk * 128:(nbk + 1) * 128], rhs=Hni[lo:hi, q, :],
                                        start=False, stop=True), reads=[f"Sp{r}_1"], writes=[pyk])
                                p.op("act", lambda e, py=py, nbk=nbk, q=q: e.activation(
                                    out=Ytm[:, nbk, :, 32 * q:32 * q + 32].rearrange("p j (h o) -> p j h o", h=2),
                                    in_=py[:, 0:256].rearrange("p (h j o) -> p j h o", h=2, j=8), func=AF.Gelu_apprx_tanh),
                                     reads=[pyk], writes=[newkey("Ytm")])
                            cast_step(1)
                    phase_end([("Ytm", Ytm[:].rearrange("p a j c -> p (a j c)"), [128, NB * 8 * 512], BF16),
                               ("Sp", Sp[1][0][:], [128, 512], BF16), ("CS", CS[:].rearrange("p a b -> p (a b)"), [128, 4096], F32)])

            with ExitStack() as st:
                yT = sbt(st, "yT", [128, 4, 2048], BF16)
                sgT = sbt(st, "sgT", [128, 4, 2048], BF16)
                selt = sbt(st, "selt", [128, 4, 64], BF16)
                wglu_t = sbt(st, "wglu_t", [128, 4, 512], BF16)
                bglu_t = sbt(st, "bglu_t", [128, 4], F32)
                sg = [sbt(st, f"sg{i}", [128, 512], BF16) for i in range(2)]
                psZ = [pst(st, f"p2b_ps{i}", [128, 512]) for i in range(4)]
                cload(selt[:], sel_d, "selt")
                cload(bglu_t[:], b_gluT_d, "bglu_t")
                wk = cast_until(["wglu_s"])
                load_w(wglu_t[:], wglu_s.rearrange("(kt p) c -> p kt c", p=128), "wglu_t", wk, "ld_w1")
                zc = 0
                for s in range(4):
                    for ct in range(4):
                        ps = psZ[zc % 4]; pk = f"p2b_ps{zc % 4}"; zc += 1
                        for j in range(8):
                            p.op("pe", lambda e, ps=ps, s=s, ct=ct, j=j: e.matmul(
                                ps[:, j * 64:(j + 1) * 64], lhsT=Ytm[:, s, j, ct * 128:(ct + 1) * 128], rhs=selt[:, s, :],
                                start=True, stop=True), reads=["selt"], writes=[pk])
                        if ct % 2 == 0:
                            p.op("act", lambda e, ps=ps, s=s, ct=ct: e.activation(
                                out=yT[:, ct, s * 512:(s + 1) * 512].rearrange("p (m j) -> p m j", j=8),
                                in_=ps[:, :].rearrange("p (j m) -> p m j", j=8), func=AF.Copy), reads=[pk], writes=[f"yT{s}_{ct}"])
                        else:
                            p.op("dve", lambda e, ps=ps, s=s, ct=ct: e.tensor_copy(
                                out=yT[:, ct, s * 512:(s + 1) * 512].rearrange("p (m j) -> p m j", j=8),
                                in_=ps[:, :].rearrange("p (j m) -> p m j", j=8)), reads=[pk], writes=[f"yT{s}_{ct}"])
                for s in range(4):
                    for ct in range(4):
                        ps = psZ[zc % 4]; pk = f"p2b_ps{zc % 4}"; zc += 1
                        for kt in range(4):
                            p.op("pe", lambda e, ps=ps, s=s, ct=ct, kt=kt: e.matmul(
                                ps[:, :], lhsT=wglu_t[:, kt, ct * 128:(ct + 1) * 128], rhs=yT[:, kt, s * 512:(s + 1) * 512],
                                start=(kt == 0), stop=(kt == 3)), reads=["wglu_t", f"yT{s}_{kt}"], writes=[pk])
                        sgi = zc % 2
                        p.op("act", lambda e, ps=ps, ct=ct, sgi=sgi: e.activation(
                            out=sg[sgi][:], in_=ps[:, :], func=AF.Sigmoid, bias=bglu_t[:, ct:ct + 1]),
                             reads=[pk, "bglu_t"], writes=[f"sg{sgi}"])
                        p.op("dve", lambda e, s=s, ct=ct, sgi=sgi: e.tensor_tensor(
                            out=sgT[:, ct, s * 512:(s + 1) * 512], in0=sg[sgi][:], in1=yT[:, ct, s * 512:(s + 1) * 512], op=ALU.mult),
                             reads=[f"sg{sgi}", f"yT{s}_{ct}"], writes=[f"sgT{s}_{ct}"])
                p.dma(lambda e: e.dma_start(out=sglu_s, in_=sgT[:]), "st_sg",
                      reads=[f"sgT{s}_{ct}" for s in range(4) for ct in range(4)], writes=["sglu_s"])
                phase_end([("sgluT", sgT[:].rearrange("p a b -> p (a b)"), [128, 4 * 2048], BF16)])

        s5o.close()
        with ExitStack() as at:
            KT = sbt(at, "KT", [128, 4, T], BF16)
            V = sbt(at, "V", [128, 32, 512], BF16)
            QT = sbt(at, "QT", [128, 4, 2048], BF16)
            with ExitStack() as st:
                ctx = {"name": "p1b", "psb_ctr": [0],
                       "psb": [pst(st, f"p1b_psb{i}", [128, 1024], BF16) for i in range(2)]}
                psf = [pst(st, f"p1b_psf{i}", [128, 512]) for i in range(6)]
                xt = [sbt(st, f"p1b_xt{i}", [128, D], F32) for i in range(8)]
                xn4 = [sbt(st, f"p1b_xn{i}", [128, 4, D], BF16) for i in range(2)]
                hT = [sbt(st, f"p1b_hT{i}", [128, 8, 512], BF16) for i in range(1)]
                wQKV = sbt(st, "wQKV", [128, 8, 1536], BF16)

                wk = cast_until(["win_s"])
                load_w(wQKV[:], win_s.rearrange("(kt p) c -> p kt c", p=128)[:, :, 512:2048], "wQKV", wk, "ld_w2")
                chunks = [("all", c) for c in range(8)] + [("own", s) for s in range(4)]
                pf = 0

                def rows(kind, i):
                    return (x_all if kind == "all" else x_own)[i * 512:(i + 1) * 512, :]

                normA(make_xloader(xt, "p1b_xt", rows(*chunks[0])), xn4[0], "p1b_xn0")
                for ci, (kind, idx) in enumerate(chunks):
                    r = ci % 2
                    hb = hT[0]; hbk = "p1b_hT0"
                    normB(ctx, xn4[r], f"p1b_xn{r}", hb, hbk, geff1, "geff1", sh1, "modT0_1")
                    if ci + 1 < len(chunks):
                        r2 = (ci + 1) % 2
                        normA(make_xloader(xt, "p1b_xt", rows(*chunks[ci + 1])), xn4[r2], f"p1b_xn{r2}")
                    if kind == "all":
                        for ct in range(4):
                            ps = psf[pf % 6]; pk = f"p1b_psf{pf % 6}"; pf += 1
                            for kt in range(8):
                                p.op("pe", lambda e, ps=ps, hb=hb, kt=kt, ct=ct: e.matmul(
                                    ps[:, :], lhsT=wQKV[:, kt, 512 + ct * 128: 512 + (ct + 1) * 128], rhs=hb[:, kt, :],
                                    start=(kt == 0), stop=(kt == 7)), reads=[f"{hbk}_{kt}", "wQKV"], writes=[pk])
                            if ct % 2 == 0:
                                p.op("act", lambda e, ps=ps, ct=ct, idx=idx: e.activation(out=KT[:, ct, idx * 512:(idx + 1) * 512], in_=ps[:, :], func=AF.Copy),
                                     reads=[pk], writes=[newkey("KT")])
                            else:
                                p.op("dve", lambda e, ps=ps, ct=ct, idx=idx: e.tensor_copy(out=KT[:, ct, idx * 512:(idx + 1) * 512], in_=ps[:, :]),
                                     reads=[pk], writes=[newkey("KT")])
                        for tt in range(4):
                            ps = psf[pf % 6]; pk = f"p1b_psf{pf % 6}"; pf += 1
                            for kt in range(8):
                                p.op("pe", lambda e, ps=ps, hb=hb, kt=kt, tt=tt: e.matmul(
                                    ps[:, :], lhsT=hb[:, kt, tt * 128:(tt + 1) * 128], rhs=wQKV[:, kt, 1024:1536],
                                    start=(kt == 0), stop=(kt == 7)), reads=[f"{hbk}_{kt}", "wQKV"], writes=[pk])
                            if tt % 2 == 0:
                                p.op("act", lambda e, ps=ps, tt=tt, idx=idx: e.activation(out=V[:, idx * 4 + tt, :], in_=ps[:, :], func=AF.Copy),
                                     reads=[pk], writes=[newkey("V")])
                            else:
                                p.op("dve", lambda e, ps=ps, tt=tt, idx=idx: e.tensor_copy(out=V[:, idx * 4 + tt, :], in_=ps[:, :]),
                                     reads=[pk], writes=[newkey("V")])
                    else:
                        for ct in range(4):
                            ps = psf[pf % 6]; pk = f"p1b_psf{pf % 6}"; pf += 1
                            for kt in range(8):
                                p.op("pe", lambda e, ps=ps, hb=hb, kt=kt, ct=ct: e.matmul(
                                    ps[:, :], lhsT=wQKV[:, kt, ct * 128:(ct + 1) * 128], rhs=hb[:, kt, :],
                                    start=(kt == 0), stop=(kt == 7)), reads=[f"{hbk}_{kt}", "wQKV"], writes=[pk])
                            p.op("act", lambda e, ps=ps, ct=ct, idx=idx: e.activation(out=QT[:, ct, idx * 512:(idx + 1) * 512], in_=ps[:, :], func=AF.Copy, scale=0.125),
                                 reads=[pk], writes=[newkey("QT")])
                    cast_step(2)
                phase_end([("KT", KT[:].rearrange("p a b -> p (a b)"), [128, 4 * T], BF16),
                           ("V", V[:].rearrange("p a b -> p (a b)"), [128, 32 * 512], BF16),
                           ("QT", QT[:].rearrange("p a b -> p (a b)"), [128, 4 * 2048], BF16)])

            with ExitStack() as st:
                atri = sbt(st, "atri", [128, 128], BF16)
                neguincl = sbt(st, "neguincl", [128, 128], BF16)
                neglow = sbt(st, "neglow", [128, 128], BF16)
                cload(atri[:], atri_d, "atri")
                cload(neguincl[:], neguincl_d, "neguincl")
                cload(neglow[:], neglow_d, "neglow")
                attnT = sbt(st, "attnT", [128, 4, 2048], BF16)
                mB = [sbt(st, f"mB{i}", [128, 8, 512], BF16) for i in range(2)]
                psP = [pst(st, f"psP{i}", [128, 512]) for i in range(2)]
                psO = pst(st, "psO", [128, 512])
                psZ = [pst(st, f"a_psZ{i}", [128, 512]) for i in range(4)]
                NE, NL, NP, NW = 3, 4, 2, 3
                eb = [[sbt(st, f"eb{h}_{i}", [128, 512], F32) for i in range(NE)] for h in range(2)]
                Lb = [[sbt(st, f"Lb{h}_{i}", [128, 512], BF16) for i in range(NL)] for h in range(2)]
                ePb = [[sbt(st, f"ePb{h}_{i}", [128, 512], F32) for i in range(NP)] for h in range(2)]
                Wb = [[sbt(st, f"Wb{h}_{i}", [128, 512], BF16) for i in range(NW)] for h in range(2)]
                zc = [0]
                for s in range(4):
                    mr = s % 2
                    p.dma(lambda e, mr=mr, s=s: e.dma_start(out=mB[mr][:], in_=maskB_d[:, s, :, :]), f"ld_mask{mr}", writes=[f"mB{mr}"])
                    KBs = 8 * (s + 1)
                    kbs = list(range(KBs - 1, -1, -1))
                    n = len(kbs)
                    for hp in range(4):
                        heads = (2 * hp, 2 * hp + 1)

                        def st_qk(t):
                            kb = kbs[t]
                            for hh in range(2):
                                lo, hi = hh * 64, hh * 64 + 64
                                zi = zc[0] % 4; zc[0] += 1
                                zb = psZ[zi]; zk = f"a_psZ{zi}"
                                masked = kb >= KBs - 8
                                p.op("pe", lambda e, zb=zb, lo=lo, hi=hi, kb=kb, masked=masked, hp=hp, s=s: e.matmul(
                                    zb[:, :], lhsT=KT[lo:hi, hp, kb * 128:(kb + 1) * 128], rhs=QT[lo:hi, hp, s * 512:(s + 1) * 512],
                                    start=True, stop=(not masked)), reads=[], writes=[zk])
                                if masked:
                                    p.op("pe", lambda e, zb=zb, kb=kb, mr=mr, KBs=KBs: e.matmul(
                                        zb[:, :], lhsT=atri[:], rhs=mB[mr][:, kb - (KBs - 8), :], start=False, stop=True),
                                         reads=["atri", f"mB{mr}"], writes=[zk])
                                ei = t % NE
                                p.op("act", lambda e, zb=zb, hh=hh, ei=ei: e.activation(out=eb[hh][ei][:], in_=zb[:, :], func=AF.Exp),
                                     reads=[zk], writes=[f"eb{hh}_{ei}"])
                            for hh in range(2):
                                ei = t % NE; li = t % NL
                                p.op("act", lambda e, hh=hh, ei=ei, li=li: e.activation(out=Lb[hh][li][:], in_=eb[hh][ei][:], func=AF.Ln, bias=1.0),
                                     reads=[f"eb{hh}_{ei}"], writes=[f"Lb{hh}_{li}"])

                        def st_p(t):
                            for hh in range(2):
                                li = t % NL
                                if t > 0:
                                    lpi = (t - 1) % NL
                                    p.op("pe", lambda e, hh=hh, lpi=lpi: e.matmul(
                                        psP[hh][:, :], lhsT=neglow[:], rhs=Lb[hh][lpi][:], start=False, stop=False, skip_group_check=True),
                                         reads=["neglow", f"Lb{hh}_{lpi}"], writes=[f"psP{hh}"])
                                p.op("pe", lambda e, hh=hh, li=li, t=t: e.matmul(
                                    psP[hh][:, :], lhsT=neguincl[:], rhs=Lb[hh][li][:], start=(t == 0), stop=False, skip_group_check=True),
                                     reads=["neguincl", f"Lb{hh}_{li}"], writes=[f"psP{hh}"])
                            for hh in range(2):
                                pi = t % NP; ei = t % NE; wi = t % NW
                                p.op("act", lambda e, hh=hh, pi=pi: e.activation(out=ePb[hh][pi][:], in_=psP[hh][:, :], func=AF.Exp),
                                     reads=[f"psP{hh}"], writes=[f"ePb{hh}_{pi}"])
                                p.op("dve", lambda e, hh=hh, pi=pi, ei=ei, wi=wi: e.tensor_tensor(
                                    out=Wb[hh][wi][:], in0=eb[hh][ei][:], in1=ePb[hh][pi][:], op=ALU.mult),
                                     reads=[f"eb{hh}_{ei}", f"ePb{hh}_{pi}"], writes=[f"Wb{hh}_{wi}"])

                        def st_pv(t):
                            kb = kbs[t]
                            for hh in range(2):
                                h = heads[hh]
                                lo, hi = hh * 64, hh * 64 + 64
                                wi = t % NW
                                p.op("pe", lambda e, hh=hh, h=h, lo=lo, hi=hi, kb=kb, wi=wi, t=t, n=n: e.matmul(
                                    psO[lo:hi, :], lhsT=V[:, kb, h * 64:(h + 1) * 64], rhs=Wb[hh][wi][:],
                                    start=(t == 0), stop=(t == n - 1)), reads=[f"Wb{hh}_{wi}"], writes=["psO"])

                        for step in range(n + 2):
                            if step < n:
                                st_qk(step)
                            if 1 <= step <= n:
                                st_p(step - 1)
                            if step >= 2:
                                st_pv(step - 2)
                        if hp % 2 == 0:
                            p.op("act", lambda e, s=s, hp=hp: e.activation(out=attnT[:, hp, s * 512:(s + 1) * 512], in_=psO[:, :], func=AF.Copy),
                                 reads=["psO"], writes=[f"attnT{s}_{hp}"])
                        else:
                            p.op("dve", lambda e, s=s, hp=hp: e.tensor_copy(out=attnT[:, hp, s * 512:(s + 1) * 512], in_=psO[:, :]),
                                 reads=["psO"], writes=[f"attnT{s}_{hp}"])
                        cast_step(2)
                p.dma(lambda e: e.dma_start(out=attn_s, in_=attnT[:]), "st_at",
                      reads=[f"attnT{s}_{hp}" for s in range(4) for hp in range(4)], writes=["attn_s"])
                phase_end([("attnT", attnT[:].rearrange("p a b -> p (a b)"), [128, 4 * 2048], BF16)])

        with ExitStack() as st:
            ctx = {"name": "p4a", "psb_ctr": [0],
                   "psb": [pst(st, f"p4a_psb{i}", [128, 1024], BF16) for i in range(2)]}
            psf = [pst(st, f"p4a_psf{i}", [128, 512]) for i in range(6)]
            xcA = [sbt(st, f"p4a_xc{i}", [128, 4, D], F32) for i in range(2)]
            xn4 = [sbt(st, f"p4a_xn{i}", [128, 4, D], BF16) for i in range(2)]
            hT = [sbt(st, f"p4a_hT{i}", [128, 8, 512], BF16) for i in range(1)]
            wgt = [sbt(st, f"wgt{i}", [128, 8, 512], BF16) for i in range(3)]
            wa_t = sbt(st, "wa_t", [128, 4, D], BF16)
            wb_t = sbt(st, "wb_t", [128, 4, D], BF16)
            wo_t = sbt(st, "wo_t", [128, 8, D], BF16)
            gT = sbt(st, "gT", [128, 16, 512], BF16)
            mT = sbt(st, "mT", [128, 8, 512], BF16)
            t12 = [sbt(st, f"t12_{i}", [128, 512], F32) for i in range(4)]
            sa = [sbt(st, f"sa{i}", [128, 2, 4, 512], BF16) for i in range(2)]
            wk = cast_until(["win_s", "wa_s", "wb_s", "wo_s"])
            load_w(wa_t[:], wa_s.rearrange("(kt p) c -> p kt c", p=128), "wa_t", wk, "ld_w4")
            load_w(wb_t[:], wb_s.rearrange("(kt p) c -> p kt c", p=128), "wb_t", wk, "ld_w5")
            load_w(wo_t[:], wo_s.rearrange("(kt p) c -> p kt c", p=128), "wo_t", wk, "ld_w0")
            win_v = win_s.rearrange("(kt p) c -> p kt c", p=128)
            pf = 0
            tc_ = 0
            wq = 0

            def xc_loader(r, s):
                p.dma(lambda e, r=r, s=s: e.dma_start(out=xcA[r][:], in_=x_own[s * 512:(s + 1) * 512, :].rearrange("(tt p) d -> p tt d", p=128)),
                      f"ld_x{r}", writes=[f"p4a_xc{r}"])
                return lambda tt, r=r: (xcA[r][:, tt, :], f"p4a_xc{r}")

            normA(xc_loader(0, 0), xn4[0], "p4a_xn0")
            for s in range(4):
                r = s % 2
                hb = hT[0]; hbk = "p4a_hT0"
                sat = sa[r]
                p.dma(lambda e, sat=sat, s=s: e.dma_start(out=sat[:, 0, :, :], in_=sglu_s[:, :, s * 512:(s + 1) * 512]), f"ld_sa{r}",
                      reads=["sglu_s"], writes=[f"sa{r}_0"])
                p.dma(lambda e, sat=sat, s=s: e.dma_start(out=sat[:, 1, :, :], in_=attn_s[:, :, s * 512:(s + 1) * 512]), f"ld_sb{r}",
                      reads=["attn_s"], writes=[f"sa{r}_1"])
                sak = [f"sa{r}_0", f"sa{r}_1"]
                normB(ctx, xn4[r], f"p4a_xn{r}", hb, hbk, geff1, "geff1", sh1, "modT0_1")
                if s + 1 < 4:
                    r2 = (s + 1) % 2
                    normA(xc_loader(r2, s + 1), xn4[r2], f"p4a_xn{r2}")
                for g4 in range(4):
                    wi = wq % 3; wq += 1
                    p.dma(lambda e, wi=wi, g4=g4: e.dma_start(out=wgt[wi][:], in_=win_v[:, :, 2048 + g4 * 512: 2048 + (g4 + 1) * 512]),
                          f"ld_wgt{wi}", reads=wk, writes=[f"wgt{wi}"])
                    for gl in range(4):
                        gi = g4 * 4 + gl
                        ps = psf[pf % 6]; pk = f"p4a_psf{pf % 6}"; pf += 1
                        for kt in range(8):
                            p.op("pe", lambda e, ps=ps, hb=hb, kt=kt, gl=gl, wi=wi: e.matmul(
                                ps[:, :], lhsT=wgt[wi][:, kt, gl * 128:(gl + 1) * 128], rhs=hb[:, kt, :], start=(kt == 0), stop=(kt == 7)),
                                 reads=[f"{hbk}_{kt}", f"wgt{wi}"], writes=[pk])
                        p.op("act", lambda e, ps=ps, gi=gi: e.activation(out=gT[:, gi, :], in_=ps[:, :], func=AF.Sigmoid),
                             reads=[pk], writes=[f"gT{gi}"])
                for ct in range(8):
                    psa = psf[pf % 6]; pka = f"p4a_psf{pf % 6}"; pf += 1
                    psb_ = psf[pf % 6]; pkb = f"p4a_psf{pf % 6}"; pf += 1
                    for kt in range(4):
                        p.op("pe", lambda e, psa=psa, kt=kt, ct=ct, sat=sat: e.matmul(
                            psa[:, :], lhsT=wa_t[:, kt, ct * 128:(ct + 1) * 128], rhs=sat[:, 0, kt, :],
                            start=(kt == 0), stop=(kt == 3)), reads=["wa_t"] + sak, writes=[pka])
                    for kt in range(4):
                        p.op("pe", lambda e, psb_=psb_, kt=kt, ct=ct, sat=sat: e.matmul(
                            psb_[:, :], lhsT=wb_t[:, kt, ct * 128:(ct + 1) * 128], rhs=sat[:, 1, kt, :],
                            start=(kt == 0), stop=(kt == 3)), reads=["wb_t"] + sak, writes=[pkb])
                    ta = t12[tc_ % 4]; tak = f"t12_{tc_ % 4}"; tc_ += 1
                    tb = t12[tc_ % 4]; tbk = f"t12_{tc_ % 4}"; tc_ += 1
                    p.op("dve", lambda e, psa=psa, ta=ta, ct=ct: e.tensor_tensor(out=ta[:], in0=psa[:, :], in1=gT[:, ct, :], op=ALU.mult),
                         reads=[pka, f"gT{ct}"], writes=[tak])
                    p.op("dve", lambda e, psb_=psb_, tb=tb, ct=ct: e.tensor_tensor(out=tb[:], in0=psb_[:, :], in1=gT[:, 8 + ct, :], op=ALU.mult),
                         reads=[pkb, f"gT{8 + ct}"], writes=[tbk])
                    p.op("dve", lambda e, ta=ta, tb=tb, ct=ct: e.tensor_tensor(out=mT[:, ct, :], in0=ta[:], in1=tb[:], op=ALU.add),
                         reads=[tak, tbk], writes=[f"mT{ct}"])
                mks = [f"mT{ct}" for ct in range(8)]
                xck = f"p4a_xc{r}"
                for tt in range(4):
                    for h in range(2):
                        ps = psf[pf % 6]; pk = f"p4a_psf{pf % 6}"; pf += 1
                        for kt in range(8):
                            p.op("pe", lambda e, ps=ps, kt=kt, tt=tt, h=h: e.matmul(
                                ps[:, :], lhsT=mT[:, kt, tt * 128:(tt + 1) * 128], rhs=wo_t[:, kt, h * 512:(h + 1) * 512],
                                start=(kt == 0), stop=(kt == 7)), reads=[f"mT{kt}", "wo_t"], writes=[pk])
                        ta = t12[tc_ % 4]; tak = f"t12_{tc_ % 4}"; tc_ += 1
                        p.op("dve", lambda e, ps=ps, ta=ta, h=h: e.tensor_tensor(out=ta[:], in0=ps[:, :], in1=g1bc[:, h * 512:(h + 1) * 512], op=ALU.mult),
                             reads=[pk, "g1bc"], writes=[tak])
                        p.op("dve", lambda e, ta=ta, tt=tt, h=h, r=r: e.tensor_tensor(
                            out=xcA[r][:, tt, h * 512:(h + 1) * 512], in0=ta[:], in1=xcA[r][:, tt, h * 512:(h + 1) * 512], op=ALU.add),
                             reads=[tak, xck], writes=[xck])
                p.dma(lambda e, r=r, s=s: e.dma_start(out=x1_s[s * 512:(s + 1) * 512, :].rearrange("(tt p) d -> p tt d", p=128), in_=xcA[r][:]),
                      f"st_x1{r}", reads=[xck], writes=[f"x1_s{s}"], q="pool")
                cast_step(8)
            cast_step(500)
            phase_end(exclude=())

        with ExitStack() as st:
            ctx = {"name": "p4b", "psb_ctr": [0],
                   "psb": [pst(st, f"p4b_psb{i}", [128, 1024], BF16) for i in range(2)]}
            psf = [pst(st, f"p4b_psf{i}", [128, 512]) for i in range(6)]
            xcB = [sbt(st, f"p4b_xc{i}", [128, 4, D], F32) for i in range(2)]
            xn4 = [sbt(st, f"p4b_xn{i}", [128, 4, D], BF16) for i in range(2)]
            hT = [sbt(st, f"p4b_hT{i}", [128, 8, 512], BF16) for i in range(1)]
            wd_t = sbt(st, "wd_t", [128, NFT, D], BF16)
            wgu = [sbt(st, f"wgu{i}", [128, 2, 8, 128], BF16) for i in range(3)]
            hid = sbt(st, "hid", [128, NFT, 512], BF16)
            sgf = [sbt(st, f"sgf{i}", [128, 512], F32) for i in range(2)]
            tf = [sbt(st, f"tf{i}", [128, 512], F32) for i in range(2)]
            x2 = [sbt(st, f"x2_{i}", [128, D], F32) for i in range(2)]
            ost = [sbt(st, f"ost{i}", [128, D], F32) for i in range(2)]
            gfbc = sbt(st, "gfbc", [128, D], F32)
            cload(gfbc[:], nfg_row.partition_broadcast(128), "gfbc")
            load_w(wd_t[:], wd_s.rearrange("(ft p) c -> p ft c", p=128), "wd_t", cast_done["wd_s"], "ld_w1")
            pf = 0
            wq = 0
            tcn = 0
            xq = 0

            def x1_loader(r, s):
                p.dma(lambda e, r=r, s=s: e.dma_start(out=xcB[r][:], in_=x1_s[s * 512:(s + 1) * 512, :].rearrange("(tt p) d -> p tt d", p=128)),
                      f"ld_x1{r}", reads=[f"x1_s{s}"], writes=[f"p4b_xc{r}"])
                return lambda tt, r=r: (xcB[r][:, tt, :], f"p4b_xc{r}")

            normA(x1_loader(0, 0), xn4[0], "p4b_xn0")
            for s in range(4):
                r = s % 2
                hb = hT[0]; hbk = "p4b_hT0"
                normB(ctx, xn4[r], f"p4b_xn{r}", hb, hbk, geff2, "geff2", sh2, "modT2_1")
                if s + 1 < 4:
                    r2 = (s + 1) % 2
                    normA(x1_loader(r2, s + 1), xn4[r2], f"p4b_xn{r2}")
                for ft in range(NFT):
                    wi = wq % 3; wq += 1
                    wt = wgu[wi]; wtk = f"wgu{wi}"
                    p.dma(lambda e, wt=wt, ft=ft: e.dma_start(out=wt[:, 0, :, :], in_=wg_s[ft, :, :, :]), f"ld_wgu{wi}",
                          reads=cast_done["wg_s"], writes=[wtk + "g"])
                    p.dma(lambda e, wt=wt, ft=ft: e.dma_start(out=wt[:, 1, :, :], in_=wu_s[ft, :, :, :]), f"ld_wgv{wi}",
                          reads=cast_done["wu_s"], writes=[wtk + "u"])
                    psg = psf[pf % 6]; pkg = f"p4b_psf{pf % 6}"; pf += 1
                    psu = psf[pf % 6]; pku = f"p4b_psf{pf % 6}"; pf += 1
                    for kt in range(8):
                        p.op("pe", lambda e, psg=psg, wt=wt, kt=kt, hb=hb: e.matmul(
                            psg[:, :], lhsT=wt[:, 0, kt, :], rhs=hb[:, kt, :], start=(kt == 0), stop=(kt == 7)),
                             reads=[wtk + "g", wtk + "u", f"{hbk}_{kt}"], writes=[pkg])
                    for kt in range(8):
                        p.op("pe", lambda e, psu=psu, wt=wt, kt=kt, hb=hb: e.matmul(
                            psu[:, :], lhsT=wt[:, 1, kt, :], rhs=hb[:, kt, :], start=(kt == 0), stop=(kt == 7)),
                             reads=[wtk + "g", wtk + "u", f"{hbk}_{kt}"], writes=[pku])
                    si = ft % 2
                    p.op("act", lambda e, psg=psg, si=si: e.activation(out=sgf[si][:], in_=psg[:, :], func=AF.Silu),
                         reads=[pkg], writes=[f"sgf{si}"])
                    p.op("dve", lambda e, psu=psu, si=si, ft=ft: e.tensor_tensor(out=hid[:, ft, :], in0=psu[:, :], in1=sgf[si][:], op=ALU.mult),
                         reads=[pku, f"sgf{si}"], writes=[f"hid{ft}"])
                for tt in range(4):
                    xi = xq % 2; xq += 1
                    for h in range(2):
                        ps = psf[pf % 6]; pk = f"p4b_psf{pf % 6}"; pf += 1
                        for ft in range(NFT):
                            p.op("pe", lambda e, ps=ps, ft=ft, tt=tt, h=h: e.matmul(
                                ps[:, :], lhsT=hid[:, ft, tt * 128:(tt + 1) * 128], rhs=wd_t[:, ft, h * 512:(h + 1) * 512],
                                start=(ft == 0), stop=(ft == NFT - 1)), reads=[f"hid{ft}", "wd_t"], writes=[pk])
                        ti = tcn % 2; tcn += 1
                        p.op("dve", lambda e, ps=ps, ti=ti, h=h: e.tensor_tensor(out=tf[ti][:], in0=ps[:, :], in1=g2bc[:, h * 512:(h + 1) * 512], op=ALU.mult),
                             reads=[pk, "g2bc"], writes=[f"tf{ti}"])
                        p.op("dve", lambda e, ti=ti, xi=xi, tt=tt, h=h, r=r: e.tensor_tensor(
                            out=x2[xi][:, h * 512:(h + 1) * 512], in0=tf[ti][:], in1=xcB[r][:, tt, h * 512:(h + 1) * 512], op=ALU.add),
                             reads=[f"tf{ti}", f"p4b_xc{r}"], writes=[f"x2_{xi}_{h}"])
                    ss, ssk = newvec(); sd, sdk = newvec(); rs, rsk = newvec()
                    x2k = [f"x2_{xi}_0", f"x2_{xi}_1"]
                    p.op("act", lambda e, xi=xi, ss=ss: e.activation(out=junk[:], in_=x2[xi][:], func=AF.Square, accum_out=ss),
                         reads=x2k, writes=[ssk, "junk"])
                    p.op("act", lambda e, ss=ss, sd=sd: e.activation(out=sd, in_=ss, func=AF.Sqrt, scale=1.0 / D, bias=epsv[:, 0:1]),
                         reads=[ssk, "epsv"], writes=[sdk])
                    p.op("dve", lambda e, sd=sd, rs=rs: e.reciprocal(out=rs, in_=sd), reads=[sdk], writes=[rsk])
                    p.op("dve", lambda e, xi=xi, rs=rs: e.scalar_tensor_tensor(out=ost[xi][:], in0=x2[xi][:], scalar=rs, in1=gfbc[:], op0=ALU.mult, op1=ALU.mult),
                         reads=x2k + [rsk, "gfbc"], writes=[f"ost{xi}"])
                    row0 = s * 512 + tt * 128
                    p.dma(lambda e, xi=xi, row0=row0: e.dma_start(out=out_d[row0:row0 + 128, :], in_=ost[xi][:]), f"st_out{xi}",
                          reads=[f"ost{xi}"], q="pool")
            p.final_wait("sp", [k for k in p.count if k not in ("pe", "act", "dve", "pool")])
        p.emit(block, sems)
    return nc, list(dbg.keys())


def _bf(a):
    return np.ascontiguousarray(a).astype(NPBF)


def prep_inputs(inp):
    f32 = np.float32
    x = np.asarray(inp["x"], f32)
    c = np.asarray(inp["c"], f32)
    L = 0
    shared = {}
    shared["w_ada"] = np.ascontiguousarray(np.asarray(inp["w_ada"], f32)[L])
    b_ada = np.asarray(inp["b_ada"], f32)[L]
    shared["b_adaT"] = np.ascontiguousarray(b_ada.reshape(48, 128).T)
    shared["b_ada_row"] = np.ascontiguousarray(b_ada.reshape(1, -1))
    shared["n1gT"] = np.ascontiguousarray(np.asarray(inp["norm1_g"], f32)[L].reshape(8, 128).T)
    shared["n2gT"] = np.ascontiguousarray(np.asarray(inp["norm2_g"], f32)[L].reshape(8, 128).T)
    shared["nfg_row"] = np.ascontiguousarray(np.asarray(inp["norm_f_g"], f32).reshape(1, D))
    shared["w_in"] = np.ascontiguousarray(np.asarray(inp["w_in"], f32)[L])

    def pairlay(a):
        return np.ascontiguousarray(a.reshape(16, 2, 64).transpose(1, 2, 0).reshape(128, 16))

    lam_re = np.asarray(inp["lam_re"], f32)[L]
    lam_im = np.asarray(inp["lam_im"], f32)[L]
    log_dt = np.asarray(inp["log_dt"], f32)[L]
    shared["lamre_p"] = pairlay(lam_re)
    shared["lamim_p"] = pairlay(lam_im)
    shared["logdt_p"] = pairlay(np.repeat(log_dt[:, None], 64, axis=1))

    def pairlay3(a):
        return np.ascontiguousarray(a.reshape(16, 2, 64, a.shape[2]).transpose(1, 2, 0, 3).reshape(128, 16, a.shape[2]))

    shared["bre_p"] = pairlay3(np.asarray(inp["b_re"], f32)[L])
    shared["bim_p"] = pairlay3(np.asarray(inp["b_im"], f32)[L])
    shared["cre_p"] = pairlay3(np.asarray(inp["c_re"], f32)[L].transpose(0, 2, 1))
    shared["cim_p"] = pairlay3(np.asarray(inp["c_im"], f32)[L].transpose(0, 2, 1))
    d_skip = np.asarray(inp["d_skip"], f32)[L]
    shared["dvec"] = np.ascontiguousarray(np.tile(d_skip.T, (8, 1)))
    shared["w_glu"] = np.ascontiguousarray(np.asarray(inp["w_glu"], f32)[L])
    shared["b_gluT"] = np.ascontiguousarray(np.asarray(inp["b_glu"], f32)[L].reshape(4, 128).T)
    shared["w_a"] = np.ascontiguousarray(np.asarray(inp["w_a"], f32)[L])
    shared["w_b"] = np.ascontiguousarray(np.asarray(inp["w_b"], f32)[L])
    shared["w_o"] = np.ascontiguousarray(np.asarray(inp["w_o"], f32)[L])
    shared["wg"] = np.ascontiguousarray(np.asarray(inp["w_ffn_gate"], f32)[L])
    shared["wu"] = np.ascontiguousarray(np.asarray(inp["w_ffn_up"], f32)[L])
    shared["wd"] = np.ascontiguousarray(np.asarray(inp["w_ffn_down"], f32)[L])
    shared["identf"] = np.eye(128, dtype=f32)
    jj = np.arange(128)
    shared["atri"] = _bf(np.where(jj[:, None] <= jj[None, :], NEG_BIG, 0.0).astype(f32))
    shared["neguincl"] = _bf(np.where(jj[:, None] >= jj[None, :], -1.0, 0.0).astype(f32))
    shared["neglow"] = _bf(np.where(jj[:, None] < jj[None, :], -1.0, 0.0).astype(f32))
    ii = jj // 16
    shared["cmask"] = (ii[None, :] >= ii[:, None]).astype(f32)
    kvals = np.concatenate([-np.arange(8), 7 - np.arange(8), np.arange(8), 1 + np.arange(8)]).astype(f32)
    shared["kv"] = np.ascontiguousarray(np.tile(kvals[None, :], (128, 1)))
    shared["ones_row"] = np.ones((1, 128), f32)
    hm = np.zeros((128, 2), f32)
    hm[:64, 0] = 1.0
    hm[64:, 1] = 1.0
    shared["hmask"] = hm
    in_maps = []
    for core in range(8):
        b, hf = core // 2, core % 2
        own = OWN[hf]
        m = dict(shared)
        m["x_all"] = np.ascontiguousarray(x[b])
        m["x_own"] = np.ascontiguousarray(np.concatenate([x[b, cc * 512:(cc + 1) * 512] for cc in own], axis=0))
        m["cT"] = np.ascontiguousarray(c[b].reshape(8, 128).T)
        maskB = np.zeros((128, 4, 8, 512), f32)
        sel = np.zeros((128, 4, 64), f32)
        qq = np.arange(512)
        for s in range(4):
            cs = own[s]
            for r in range(8):
                off = 512 * (cs - 2 * s) - 128 * r
                tq = qq + off
                allm = tq < 0
                part = (tq >= 0) & (tq <= 127)
                maskB[0, s, r, allm] = 1.0
                maskB[tq[part], s, r, qq[part]] = 1.0
            offn = (cs - 2 * s) * 64
            sel[np.arange(64) + offn, s, np.arange(64)] = 1.0
        m["maskB"] = _bf(maskB)
        m["sel"] = _bf(sel)
        in_maps.append(m)
    return in_maps


_CACHE = {}


def kernel(**inputs):
    if "nc" not in _CACHE:
        _CACHE["nc"] = build_program()
    nc, dbg_names = _CACHE["nc"]
    in_maps = prep_inputs(inputs)
    ncores = int(os.environ.get("K_NCORES", "8"))
    res = run_bass_kernel_spmd(nc, in_maps[:ncores], core_ids=list(range(ncores)))
    out = np.zeros((4, T, D), np.float32)
    for core in range(ncores):
        b, hf = core // 2, core % 2
        o = np.asarray(res.results[core]["out"], np.float32)
        for s, cc in enumerate(OWN[hf]):
            out[b, cc * 512:(cc + 1) * 512] = o[s * 512:(s + 1) * 512]
    _CACHE["last_results"] = res.results
    return out
```

```python
import math
import os
from contextlib import ExitStack

import numpy as np
import ml_dtypes
import concourse.bass as bass
import concourse.mybir as mybir
from concourse.bass_utils import run_bass_kernel_spmd

F32 = mybir.dt.float32
BF16 = mybir.dt.bfloat16
AF = mybir.ActivationFunctionType
ALU = mybir.AluOpType
NPBF = ml_dtypes.bfloat16

D = 1024
T = 4096
NB = 4
G = 32
FF = 2816
NFT = FF // 128
EPS = 1e-6
MAGIC = 12582912.0
C1 = 6.28125
C2 = 2.0 * math.pi - 6.28125
PI_LO = 3.1415925
NEG_BIG = -30000.0
OWN = {0: [0, 3, 4, 7], 1: [1, 2, 5, 6]}
DEBUG = False


class Prog:
    ENGS = ("pe", "act", "dve", "pool", "sp")

    def __init__(self):
        self.streams = {e: [] for e in self.ENGS}
        self.count = {}
        self.known = {e: {} for e in self.ENGS}
        self.lastw = {}
        self.readers = {}
        self.dma_sems = set()
        self.enabled = True
        self.phase = 0

    def _deps(self, eng, reads, writes):
        deps = {}

        def add(tok):
            if tok is None:
                return
            s, v = tok
            if deps.get(s, 0) < v:
                deps[s] = v

        for k in reads:
            add(self.lastw.get(k))
        for k in writes:
            add(self.lastw.get(k))
            for s, v in self.readers.get(k, {}).items():
                add((s, v))
        out = []
        for s, v in deps.items():
            if s == "pe" and eng == "pe":
                continue
            if self.known[eng].get(s, 0) >= v:
                continue
            self.known[eng][s] = v
            out.append((s, v))
        return out

    def _commit(self, tok, reads, writes):
        s, v = tok
        for k in reads:
            d = self.readers.setdefault(k, {})
            if d.get(s, 0) < v:
                d[s] = v
        for k in writes:
            self.lastw[k] = tok
            self.readers[k] = {}

    def op(self, eng, fn, reads=(), writes=()):
        if not self.enabled:
            return
        waits = self._deps(eng, reads, writes)
        v = self.count.get(eng, 0) + 1
        self.count[eng] = v
        self.streams[eng].append((waits, fn, (eng, 1)))
        self._commit((eng, v), reads, writes)

    def dma(self, fn, sem, reads=(), writes=(), q="sp"):
        if not self.enabled:
            return
        self.dma_sems.add(sem)
        waits = self._deps(q, reads, writes)
        v = self.count.get(sem, 0) + 16
        self.count[sem] = v
        self.streams[q].append((waits, fn, (sem, 16)))
        self._commit((sem, v), reads, writes)

    def barrier(self, exclude=()):
        if not self.enabled:
            return
        toks = [(s, v) for s, v in self.count.items() if s not in exclude and v > 0]
        for e in self.ENGS:
            if e in exclude:
                continue
            waits = []
            for s, v in toks:
                if s == e:
                    continue
                if self.known[e].get(s, 0) >= v:
                    continue
                self.known[e][s] = v
                waits.append((s, v))
            if waits:
                self.streams[e].append((waits, None, None))

    def final_wait(self, eng, semnames):
        waits = [(s, self.count[s]) for s in semnames if self.count.get(s, 0) > 0]
        self.streams[eng].append((waits, None, None))

    def emit(self, block, sems):
        engmap = {"pe": block.tensor, "act": block.scalar, "dve": block.vector,
                  "pool": block.gpsimd, "sp": block.sync}
        for e in self.ENGS:
            stream = self.streams[e]

            def body(engine, stream=stream):
                for waits, fn, inc in stream:
                    for s, v in waits:
                        engine.wait_ge(sems[s], v)
                    if fn is not None:
                        ins = fn(engine)
                        ins.then_inc(sems[inc[0]], inc[1])

            engmap[e](body)


def bc_last(ap, n):
    shp = list(ap.shape)
    if len(shp) == 2:
        return ap.rearrange("p (a o) -> p a o", o=1).to_broadcast([shp[0], shp[1], n])
    return ap.rearrange("p a (b o) -> p a b o", o=1).to_broadcast([shp[0], shp[1], shp[2], n])


def build_program():
    nc = bass.Bass("TRN2", target_bir_lowering=False)
    p = Prog()

    def din(name, shape, dt=F32):
        return nc.dram_tensor(name, list(shape), dt, kind="ExternalInput").ap()

    x_all = din("x_all", [T, D])
    x_own = din("x_own", [2048, D])
    cT_d = din("cT", [128, 8])
    w_ada = din("w_ada", [D, 6 * D])
    b_adaT_d = din("b_adaT", [128, 48])
    b_ada_row = din("b_ada_row", [1, 6 * D])
    n1gT_d = din("n1gT", [128, 8])
    n2gT_d = din("n2gT", [128, 8])
    nfg_row = din("nfg_row", [1, D])
    w_in = din("w_in", [D, 4096])
    lamre_d = din("lamre_p", [128, 16])
    lamim_d = din("lamim_p", [128, 16])
    logdt_d = din("logdt_p", [128, 16])
    bre_d = din("bre_p", [128, 16, 16])
    bim_d = din("bim_p", [128, 16, 16])
    cre_d = din("cre_p", [128, 16, 16])
    cim_d = din("cim_p", [128, 16, 16])
    dvec_d = din("dvec", [128, 32])
    w_glu = din("w_glu", [512, 512])
    b_gluT_d = din("b_gluT", [128, 4])
    w_a = din("w_a", [512, D])
    w_b = din("w_b", [512, D])
    w_o = din("w_o", [D, D])
    wg = din("wg", [D, FF])
    wu = din("wu", [D, FF])
    wd = din("wd", [FF, D])
    identf_d = din("identf", [128, 128])
    atri_d = din("atri", [128, 128], BF16)
    neguincl_d = din("neguincl", [128, 128], BF16)
    neglow_d = din("neglow", [128, 128], BF16)
    cmask_d = din("cmask", [128, 128])
    kv_d = din("kv", [128, 32])
    maskB_d = din("maskB", [128, 4, 8, 512], BF16)
    sel_d = din("sel", [128, 4, 64], BF16)
    ones_row_d = din("ones_row", [1, 128])
    hmask_d = din("hmask", [128, 2])
    out_d = nc.dram_tensor("out", [2048, D], F32, kind="ExternalOutput").ap()
    dbg = {}

    def scratch(name, shape, dt=BF16):
        return nc.dram_tensor(name, list(shape), dt).ap()

    win_s = scratch("win_s", [D, 4096])
    wglu_s = scratch("wglu_s", [512, 512])
    wa_s = scratch("wa_s", [512, D])
    wb_s = scratch("wb_s", [512, D])
    wo_s = scratch("wo_s", [D, D])
    wg_s = scratch("wg_s", [NFT, 128, 8, 128])
    wu_s = scratch("wu_s", [NFT, 128, 8, 128])
    wd_s = scratch("wd_s", [FF, D])
    x1_s = scratch("x1_s", [2048, D], F32)
    sglu_s = scratch("sglu_s", [128, 4, 2048])
    attn_s = scratch("attn_s", [128, 4, 2048])

    semnames = ["pe", "act", "dve", "pool", "st_dbg", "st_sg", "st_at"]
    RING_SEMS = {"ld_cast": 2, "ld_x": 8, "ld_w": 6, "ld_wada": 2, "ld_mask": 2, "ld_wgu": 3, "ld_x1": 2,
                 "st_cast": 8, "st_out": 2, "st_x1": 2, "ldc": 30, "ld_sa": 2, "ld_ada": 2, "ld_sb": 2, "ld_wgt": 3, "ld_wgv": 3}
    for k, n in RING_SEMS.items():
        for i in range(n):
            semnames.append(f"{k}{i}")
    CEX = ("pool", "ld_cast0", "ld_cast1") + tuple(f"st_cast{i}" for i in range(8))

    with ExitStack() as top:
        sems = {s: top.enter_context(nc.semaphore(s)) for s in semnames}
        block = top.enter_context(nc.Block())

        def sbt(st, name, shape, dt):
            return st.enter_context(nc.sbuf_tensor("sb_" + name, list(shape), dt))

        def pst(st, name, shape, dt=F32):
            return st.enter_context(nc.psum_tensor("pp_" + name, list(shape), dt))

        uid = [0]

        def newkey(prefix="k"):
            uid[0] += 1
            return f"{prefix}#{uid[0]}"

        def phase_end(dumps=(), exclude=CEX):
            p.barrier(exclude=exclude)
            p.phase += 1
            if p.phase >= int(os.environ.get("K_STOP", "99")):
                p.enabled = False
            if DEBUG and dumps:
                for (name, ap, shape, dt) in dumps:
                    o = nc.dram_tensor("dbg_" + name, list(shape), dt, kind="ExternalOutput").ap()
                    dbg[name] = o
                    p.dma(lambda e, o=o, ap=ap: e.dma_start(out=o, in_=ap), "st_dbg")
                p.barrier(exclude=exclude)

        subc = [0]

        def ckpt():
            subc[0] += 1
            if subc[0] >= int(os.environ.get("K_SUB", "999")):
                p.enabled = False

        cctr = [0]

        def cload(tile, src, key):
            i = cctr[0]
            cctr[0] += 1
            p.dma(lambda e: e.dma_start(out=tile, in_=src), f"ldc{i}", writes=[key])

        identf = sbt(top, "identf", [128, 128], F32)
        identb = sbt(top, "identb", [128, 128], BF16)
        cload(identf[:], identf_d, "identf")
        p.op("dve", lambda e: e.tensor_copy(out=identb[:], in_=identf[:]), reads=["identf"], writes=["identb"])
        modT = sbt(top, "modT", [128, 4, 8], F32)
        geff1 = sbt(top, "geff1", [128, 8], F32)
        geff2 = sbt(top, "geff2", [128, 8], F32)
        g1bc = sbt(top, "g1bc", [128, D], F32)
        g2bc = sbt(top, "g2bc", [128, D], F32)
        vecs = sbt(top, "vecs", [128, 96], F32)
        vctr = [0]

        def newvec():
            i = vctr[0] % 96
            vctr[0] += 1
            return vecs[:, i:i + 1], f"vec{i}"

        epsv = sbt(top, "epsv", [128, 1], F32)
        p.op("dve", lambda e: e.memset(epsv[:], EPS), writes=["epsv"])
        junk = sbt(top, "junk", [128, D], BF16)
        sh1 = modT[:, 0, :]
        sh2 = modT[:, 2, :]

        CW = 1408
        cst_in = [sbt(top, f"cst_in{i}", [128, CW], F32) for i in range(2)]
        cst_out = [sbt(top, f"cst_out{i}", [128, CW], BF16) for i in range(2)]
        cast_jobs = []
        cast_done = {}
        cast_pos = [0]

        def add_cast(src_ap, ncols, dst_fn, done_key):
            cast_jobs.append((src_ap, ncols, dst_fn, done_key))

        for kt in range(8):
            for h in range(4):
                add_cast(w_in[kt * 128:(kt + 1) * 128, h * 1024:(h + 1) * 1024], 1024,
                         (lambda kt=kt, h=h: win_s[kt * 128:(kt + 1) * 128, h * 1024:(h + 1) * 1024]), "win_s")
        for kt in range(4):
            add_cast(w_glu[kt * 128:(kt + 1) * 128, :], 512, (lambda kt=kt: wglu_s[kt * 128:(kt + 1) * 128, :]), "wglu_s")
        for kt in range(4):
            add_cast(w_a[kt * 128:(kt + 1) * 128, :], 1024, (lambda kt=kt: wa_s[kt * 128:(kt + 1) * 128, :]), "wa_s")
        for kt in range(4):
            add_cast(w_b[kt * 128:(kt + 1) * 128, :], 1024, (lambda kt=kt: wb_s[kt * 128:(kt + 1) * 128, :]), "wb_s")
        for kt in range(8):
            add_cast(w_o[kt * 128:(kt + 1) * 128, :], 1024, (lambda kt=kt: wo_s[kt * 128:(kt + 1) * 128, :]), "wo_s")
        for ft in range(NFT):
            add_cast(wd[ft * 128:(ft + 1) * 128, :], 1024, (lambda ft=ft: wd_s[ft * 128:(ft + 1) * 128, :]), "wd_s")
        for (wsrc, wdst, key) in ((wg, wg_s, "wg_s"), (wu, wu_s, "wu_s")):
            for kt in range(8):
                for h in range(2):
                    add_cast(wsrc[kt * 128:(kt + 1) * 128, h * 1408:(h + 1) * 1408], 1408,
                             (lambda kt=kt, h=h, wdst=wdst: wdst[h * 11:(h + 1) * 11, :, kt, :].rearrange("ft p f -> p ft f")),
                             key)

        def cast_step(n):
            for _ in range(n):
                j = cast_pos[0]
                if j >= len(cast_jobs):
                    return
                cast_pos[0] += 1
                src, ncols, dst_fn, key = cast_jobs[j]
                r = j % 8
                dst = dst_fn()
                if len(dst.shape) == 3:
                    srcv = src.rearrange("p (ft f) -> p ft f", f=128)
                else:
                    srcv = src
                deps = [cast_keys[j - 8]] if j >= 8 else []
                p.dma(lambda e, dst=dst, srcv=srcv: e.dma_start(out=dst, in_=srcv, max_dma_last_dim=1024), f"st_cast{r}",
                      reads=deps, writes=[f"{key}#{j}"], q="pool")
                cast_keys.append(f"{key}#{j}")
                cast_done.setdefault(key, []).append(f"{key}#{j}")

        cast_keys = []

        def cast_until(key_names):
            need = [i for i, jb in enumerate(cast_jobs) if jb[3] in key_names]
            if need:
                last = max(need)
                if cast_pos[0] <= last:
                    cast_step(last + 1 - cast_pos[0])
            ks = []
            for k in key_names:
                ks += cast_done.get(k, [])
            return ks

        def load_w(tile, src_ap, key, deps, semname):
            p.dma(lambda e: e.dma_start(out=tile, in_=src_ap), semname, reads=deps, writes=[key])

        cT = sbt(top, "cT", [128, 8], F32)
        condf = sbt(top, "condf", [128, 8], F32)
        cond_rep = sbt(top, "cond_rep", [128, 8, 128], BF16)
        b_adaT = sbt(top, "b_adaT", [128, 48], F32)
        n1gT = sbt(top, "n1gT", [128, 8], F32)
        n2gT = sbt(top, "n2gT", [128, 8], F32)
        ones_row = sbt(top, "ones_row", [1, 128], F32)
        cload(cT[:], cT_d, "cT")
        cload(b_adaT[:], b_adaT_d, "b_adaT")
        cload(n1gT[:], n1gT_d, "n1gT")
        cload(n2gT[:], n2gT_d, "n2gT")
        cload(ones_row[:], ones_row_d, "ones_row")
        p.op("act", lambda e: e.activation(out=condf[:], in_=cT[:], func=AF.Silu), reads=["cT"], writes=["condf"])
        p.op("dve", lambda e: e.tensor_copy(out=cond_rep[:], in_=bc_last(condf[:], 128)), reads=["condf"], writes=["cond_rep"])
        w_ada_v = w_ada.rearrange("(kt p) c -> p kt c", p=128)
        slot_of = {0: 0, 1: 1, 3: 2, 4: 3}

        def ada_half(grp, h, wt, wtk, ps, pk, brow, browk):
            if grp in slot_of:
                sl = slot_of[grp]
                for ct in range(4):
                    for kt in range(8):
                        p.op("pe", lambda e, ct=ct, kt=kt: e.matmul(
                            ps[:, 2 * ct:2 * ct + 2], lhsT=wt[:, kt, ct * 128:(ct + 1) * 128],
                            rhs=cond_rep[:, kt, 0:2], start=(kt == 0), stop=(kt == 7)),
                             reads=[wtk, "cond_rep"], writes=[pk])
                for ct in range(4):
                    col = grp * 8 + h * 4 + ct
                    p.op("act", lambda e, ct=ct, col=col: e.activation(
                        out=modT[:, sl, h * 4 + ct:h * 4 + ct + 1], in_=ps[:, 2 * ct:2 * ct + 1], func=AF.Identity, bias=b_adaT[:, col:col + 1]),
                         reads=[pk, "b_adaT"], writes=[f"modT{sl}_{h}" if ct == 3 else newkey("modT")])
            else:
                dst = g1bc if grp == 2 else g2bc
                dk = "g1bc" if grp == 2 else "g2bc"
                for kt in range(8):
                    p.op("pe", lambda e, kt=kt: e.matmul(
                        ps[:, :], lhsT=cond_rep[:, kt, :], rhs=wt[:, kt, :],
                        start=(kt == 0), stop=False), reads=[wtk, "cond_rep"], writes=[pk])
                p.op("pe", lambda e: e.matmul(
                    ps[:, :], lhsT=ones_row[0:1, :], rhs=brow[0:1, :],
                    start=False, stop=True), reads=["ones_row", browk], writes=[pk])
                p.op("act", lambda e: e.activation(out=dst[:, h * 512:(h + 1) * 512], in_=ps[:, :], func=AF.Copy),
                     reads=[pk], writes=[f"{dk}_{h}"])

        def ada_load(grp, h, wt, wtk, sem, brow=None, browk=None, bsem=None):
            p.dma(lambda e: e.dma_start(out=wt[:], in_=w_ada_v[:, :, grp * 1024 + h * 512: grp * 1024 + (h + 1) * 512]),
                  sem, writes=[wtk], q="pool")
            if grp not in slot_of:
                p.dma(lambda e: e.dma_start(out=brow[0:1, :], in_=b_ada_row[0:1, grp * 1024 + h * 512: grp * 1024 + (h + 1) * 512]),
                      bsem, writes=[browk])

        def normA(xtile, xn4, xnk, extra_reads=(), scale_eng="dve"):
            for tt in range(4):
                xa, xk = xtile(tt)
                ss, ssk = newvec()
                sd, sdk = newvec()
                rs, rsk = newvec()
                p.op("act", lambda e, xa=xa, ss=ss: e.activation(out=junk[:], in_=xa, func=AF.Square, accum_out=ss),
                     reads=[xk] + list(extra_reads), writes=[ssk, "junk"])
                p.op("act", lambda e, ss=ss, sd=sd: e.activation(out=sd, in_=ss, func=AF.Sqrt, scale=1.0 / D, bias=epsv[:, 0:1]),
                     reads=[ssk, "epsv"], writes=[sdk])
                p.op("dve", lambda e, sd=sd, rs=rs: e.reciprocal(out=rs, in_=sd), reads=[sdk], writes=[rsk])
                if scale_eng == "act":
                    p.op("act", lambda e, tt=tt, xa=xa, rs=rs: e.activation(out=xn4[:, tt, :], in_=xa, func=AF.Identity, scale=rs),
                         reads=[xk, rsk], writes=[f"{xnk}_{tt}"])
                else:
                    p.op("dve", lambda e, tt=tt, xa=xa, rs=rs: e.tensor_scalar(out=xn4[:, tt, :], in0=xa, scalar1=rs, scalar2=None, op0=ALU.mult),
                         reads=[xk, rsk], writes=[f"{xnk}_{tt}"])

        def normB(ctx, xn4, xnk, hT, hTk, geff, geffk, sh, shk, col0=0, perm_half=None):
            psb = ctx["psb"]
            for kp in range(4):
                b = ctx["psb_ctr"][0] % len(psb)
                ctx["psb_ctr"][0] += 1
                pb = psb[b]; pbk = f"{ctx['name']}psb{b}"
                for kk in range(2):
                    kt = 2 * kp + kk
                    for tt in range(4):
                        p.op("pe", lambda e, pb=pb, kk=kk, tt=tt, kt=kt: e.transpose(
                            pb[:, kk * 512 + tt * 128: kk * 512 + (tt + 1) * 128], xn4[:, tt, kt * 128:(kt + 1) * 128], identb[:]),
                             reads=[f"{xnk}_{tt}", "identb"], writes=[pbk])
                for kk in range(2):
                    kt = 2 * kp + kk
                    if perm_half is None:
                        oap = hT[:, kt, col0:col0 + 512]
                        iap = pb[:, kk * 512:(kk + 1) * 512]
                    else:
                        oap = hT[:, kt, :].rearrange("p (i n) -> p i n", i=8)[:, :, 64 * perm_half:64 * perm_half + 64]
                        iap = pb[:, kk * 512:(kk + 1) * 512].rearrange("p (n i) -> p i n", i=8)
                    if kp % 2 == 0:
                        p.op("dve", lambda e, oap=oap, iap=iap, kt=kt: e.tensor_scalar(
                            out=oap, in0=iap,
                            scalar1=geff[:, kt:kt + 1], scalar2=sh[:, kt:kt + 1], op0=ALU.mult, op1=ALU.add),
                             reads=[pbk, geffk, shk], writes=[f"{hTk}_{kt}"])
                    else:
                        p.op("act", lambda e, oap=oap, iap=iap, kt=kt: e.activation(
                            out=oap, in_=iap, func=AF.Identity,
                            scale=geff[:, kt:kt + 1], bias=sh[:, kt:kt + 1]),
                             reads=[pbk, geffk, shk], writes=[f"{hTk}_{kt}"])

        xring_ctr = [0]

        def make_xloader(xt_tiles, prefix, rows_ap):
            cache = {}

            def xtile(tt):
                if tt not in cache:
                    i = xring_ctr[0] % len(xt_tiles)
                    xring_ctr[0] += 1
                    key = f"{prefix}{i}"
                    p.dma(lambda e, i=i, tt=tt: e.dma_start(out=xt_tiles[i][:], in_=rows_ap[tt * 128:(tt + 1) * 128, :]),
                          f"ld_x{i}", writes=[key])
                    cache[tt] = (xt_tiles[i][:], key)
                return cache[tt]

            return xtile

        s5o = ExitStack()
        Mbf = sbt(s5o, "Mbf", [128, G, 128], BF16)
        Gre = sbt(s5o, "Gre", [128, G, 64], BF16)
        Gim = sbt(s5o, "Gim", [128, G, 64], BF16)
        Hre = sbt(s5o, "Hre", [128, 16, 128], BF16)
        Hni = sbt(s5o, "Hni", [128, 16, 128], BF16)
        dec8 = sbt(s5o, "dec8", [128, 16], F32)
        c8 = sbt(s5o, "c8", [128, 16], F32)
        s8 = sbt(s5o, "s8", [128, 16], F32)
        lamre = sbt(s5o, "lamre", [128, 16], F32)
        lamim = sbt(s5o, "lamim", [128, 16], F32)
        logdt = sbt(s5o, "logdt", [128, 16], F32)
        bre = sbt(s5o, "bre", [128, 16, 16], F32)
        bim = sbt(s5o, "bim", [128, 16, 16], F32)
        cre = sbt(s5o, "cre", [128, 16, 16], F32)
        cim = sbt(s5o, "cim", [128, 16, 16], F32)
        dvec = sbt(s5o, "dvec", [128, 32], F32)
        cmask = sbt(s5o, "cmask", [128, 128], F32)
        kv = sbt(s5o, "kv", [128, 32], F32)
        hmask = sbt(s5o, "hmask", [128, 2], F32)
        cload(hmask[:], hmask_d, "hmask")
        for t_, d_, k_ in ((lamre, lamre_d, "lamre"), (lamim, lamim_d, "lamim"), (logdt, logdt_d, "logdt"),
                           (bre, bre_d, "bre"), (bim, bim_d, "bim"), (cre, cre_d, "cre"), (cim, cim_d, "cim"),
                           (dvec, dvec_d, "dvec"), (cmask, cmask_d, "cmask"), (kv, kv_d, "kv")):
            cload(t_[:], d_, k_)
        p0s = ExitStack()
        if True:
            wst = [sbt(p0s, f"wst{i}", [128, 8, 512], BF16) for i in range(2)]
            psm = [pst(p0s, f"psm{i}", [128, 512]) for i in range(4)]
            halves1 = ((1, 0), (1, 1), (0, 0), (0, 1), (2, 0), (2, 1), (4, 0), (4, 1), (3, 0), (3, 1), (5, 0), (5, 1))
            brow = [sbt(p0s, f"brow{i}", [1, 512], F32) for i in range(2)]
            for gi in range(2):
                ada_load(halves1[gi][0], halves1[gi][1], wst[gi], f"wst{gi}", f"ld_ada{gi}", brow[gi], f"brow{gi}", f"ld_w{4 + gi}")

        def emit_adaln():
            for gi in range(12):
                ada_half(halves1[gi][0], halves1[gi][1], wst[gi % 2], f"wst{gi % 2}", psm[gi % 4], f"psm{gi % 4}", brow[gi % 2], f"brow{gi % 2}")
                if gi + 2 < 12:
                    ada_load(halves1[gi + 2][0], halves1[gi + 2][1], wst[gi % 2], f"wst{gi % 2}", f"ld_ada{gi % 2}", brow[gi % 2], f"brow{gi % 2}", f"ld_w{4 + gi % 2}")
                if gi == 3:
                    p.op("dve", lambda e: e.scalar_tensor_tensor(out=geff1[:], in0=modT[:, 1, :], scalar=1.0, in1=n1gT[:], op0=ALU.add, op1=ALU.mult),
                         reads=["modT1_0", "modT1_1", "n1gT"], writes=["geff1"])
            p.op("dve", lambda e: e.scalar_tensor_tensor(out=geff2[:], in0=modT[:, 3, :], scalar=1.0, in1=n2gT[:], op0=ALU.add, op1=ALU.mult),
                 reads=["modT3_0", "modT3_1", "n2gT"], writes=["geff2"])

        with ExitStack() as st:

            def small(name, shape=(128, 16)):
                return sbt(st, "s_" + name, list(shape), F32)

            dt_ = small("dt"); a_ = small("a"); phi = small("phi")
            p.op("act", lambda e: e.activation(out=dt_[:], in_=logdt[:], func=AF.Exp), reads=["logdt"], writes=["dt"])
            p.op("dve", lambda e: e.tensor_tensor(out=a_[:], in0=lamre[:], in1=dt_[:], op=ALU.mult), reads=["lamre", "dt"], writes=["a"])
            p.op("dve", lambda e: e.tensor_tensor(out=phi[:], in0=lamim[:], in1=dt_[:], op=ALU.mult), reads=["lamim", "dt"], writes=["phi"])
            AR = small("AR", (128, 16, 32)); ANG = small("ANG", (128, 16, 32)); RHO = small("RHO", (128, 16, 32))
            SINt = small("SINt", (128, 16, 32)); COSt = small("COSt", (128, 16, 32))
            PRE = small("PRE", (128, 16, 32)); PIM = small("PIM", (128, 16, 32))
            tA = small("tA", (128, 16, 32)); tB = small("tB", (128, 16, 32))
            kvb = kv[:].rearrange("p (o k) -> p o k", o=1).to_broadcast([128, 16, 32])
            p.op("dve", lambda e: e.tensor_tensor(out=AR[:], in0=bc_last(a_[:], 32), in1=kvb, op=ALU.mult), reads=["a", "kv"], writes=["AR"])
            p.op("dve", lambda e: e.tensor_tensor(out=ANG[:], in0=bc_last(phi[:], 32), in1=kvb, op=ALU.mult), reads=["phi", "kv"], writes=["ANG"])
            p.op("act", lambda e: e.activation(out=RHO[:], in_=AR[:], func=AF.Exp), reads=["AR"], writes=["RHO"])

            def range_reduce_sin(src, srck, dst, dstk, shift):
                p.op("dve", lambda e: e.tensor_scalar(out=tA[:], in0=src[:], scalar1=shift, scalar2=None, op0=ALU.add),
                     reads=[srck], writes=["tA"])
                p.op("dve", lambda e: e.tensor_scalar(out=tB[:], in0=tA[:], scalar1=1.0 / (2 * math.pi), scalar2=MAGIC, op0=ALU.mult, op1=ALU.add),
                     reads=["tA"], writes=["tB"])
                p.op("dve", lambda e: e.tensor_scalar(out=tB[:], in0=tB[:], scalar1=MAGIC, scalar2=None, op0=ALU.subtract),
                     reads=["tB"], writes=["tB"])
                p.op("dve", lambda e: e.scalar_tensor_tensor(out=tA[:], in0=tB[:], scalar=-C1, in1=tA[:], op0=ALU.mult, op1=ALU.add),
                     reads=["tB", "tA"], writes=["tA"])
                p.op("dve", lambda e: e.scalar_tensor_tensor(out=tA[:], in0=tB[:], scalar=-C2, in1=tA[:], op0=ALU.mult, op1=ALU.add),
                     reads=["tB", "tA"], writes=["tA"])
                p.op("dve", lambda e: e.tensor_scalar(out=tA[:], in0=tA[:], scalar1=-PI_LO, scalar2=PI_LO, op0=ALU.max, op1=ALU.min),
                     reads=["tA"], writes=["tA"])
                p.op("act", lambda e: e.activation(out=dst[:], in_=tA[:], func=AF.Sin), reads=["tA"], writes=[dstk])

            range_reduce_sin(ANG, "ANG", SINt, "SINt", 0.0)
            range_reduce_sin(ANG, "ANG", COSt, "COSt", math.pi / 2)
            p.op("dve", lambda e: e.tensor_tensor(out=PRE[:], in0=RHO[:], in1=COSt[:], op=ALU.mult), reads=["RHO", "COSt"], writes=["PRE"])
            p.op("dve", lambda e: e.tensor_tensor(out=PIM[:], in0=RHO[:], in1=SINt[:], op=ALU.mult), reads=["RHO", "SINt"], writes=["PIM"])
            ckpt()
            nr = small("nr"); den = small("den"); t1s = small("t1s"); t2s = small("t2s")
            bre_s = small("betare"); bim_s = small("betaim")
            lbre = PRE[:, :, 24]; lbim = PIM[:, :, 24]
            p.op("dve", lambda e: e.tensor_scalar(out=nr[:], in0=lbre, scalar1=-1.0, scalar2=None, op0=ALU.add), reads=["PRE"], writes=["nr"])
            p.op("dve", lambda e: e.tensor_tensor(out=den[:], in0=lamre[:], in1=lamre[:], op=ALU.mult), reads=["lamre"], writes=["den"])
            p.op("dve", lambda e: e.tensor_tensor(out=t1s[:], in0=lamim[:], in1=lamim[:], op=ALU.mult), reads=["lamim"], writes=["t1s"])
            p.op("dve", lambda e: e.tensor_tensor(out=den[:], in0=den[:], in1=t1s[:], op=ALU.add), reads=["den", "t1s"], writes=["den"])
            p.op("dve", lambda e: e.reciprocal(out=den[:], in_=den[:]), reads=["den"], writes=["den"])
            p.op("dve", lambda e: e.tensor_tensor(out=t1s[:], in0=nr[:], in1=lamre[:], op=ALU.mult), reads=["nr", "lamre"], writes=["t1s"])
            p.op("dve", lambda e: e.tensor_tensor(out=t2s[:], in0=lbim, in1=lamim[:], op=ALU.mult), reads=["PIM", "lamim"], writes=["t2s"])
            p.op("dve", lambda e: e.tensor_tensor(out=t1s[:], in0=t1s[:], in1=t2s[:], op=ALU.add), reads=["t1s", "t2s"], writes=["t1s"])
            p.op("dve", lambda e: e.tensor_tensor(out=bre_s[:], in0=t1s[:], in1=den[:], op=ALU.mult), reads=["t1s", "den"], writes=["betare"])
            p.op("dve", lambda e: e.tensor_tensor(out=t1s[:], in0=lbim, in1=lamre[:], op=ALU.mult), reads=["PIM", "lamre"], writes=["t1s"])
            p.op("dve", lambda e: e.tensor_tensor(out=t2s[:], in0=nr[:], in1=lamim[:], op=ALU.mult), reads=["nr", "lamim"], writes=["t2s"])
            p.op("dve", lambda e: e.tensor_tensor(out=t1s[:], in0=t1s[:], in1=t2s[:], op=ALU.subtract), reads=["t1s", "t2s"], writes=["t1s"])
            p.op("dve", lambda e: e.tensor_tensor(out=bim_s[:], in0=t1s[:], in1=den[:], op=ALU.mult), reads=["t1s", "den"], writes=["betaim"])
            big1 = sbt(st, "big1", [128, 16, 8, 16], F32)
            big2 = sbt(st, "big2", [128, 16, 8, 16], F32)

            def cmul(outre, outrek, outim, outimk, are, arek, aim, aimk, bre_, brek, bim_, bimk, shape, neg_im=False, eng="dve"):
                n = 1
                for s_ in shape[1:]:
                    n *= s_
                if len(shape) == 3:
                    t1 = big1[:].rearrange("p a b c -> p (a b c)")[:, 0:n].rearrange("p (a b) -> p a b", b=shape[2])
                    t2 = big2[:].rearrange("p a b c -> p (a b c)")[:, 0:n].rearrange("p (a b) -> p a b", b=shape[2])
                else:
                    t1 = big1[:]
                    t2 = big2[:]
                p.op(eng, lambda e: e.tensor_tensor(out=t1, in0=are, in1=bre_, op=ALU.mult), reads=[arek, brek], writes=["big1"])
                p.op(eng, lambda e: e.tensor_tensor(out=t2, in0=aim, in1=bim_, op=ALU.mult), reads=[aimk, bimk], writes=["big2"])
                p.op(eng, lambda e: e.tensor_tensor(out=outre, in0=t1, in1=t2, op=ALU.subtract), reads=["big1", "big2"], writes=[outrek])
                p.op(eng, lambda e: e.tensor_tensor(out=t1, in0=are, in1=bim_, op=ALU.mult), reads=[arek, bimk], writes=["big1"])
                p.op(eng, lambda e: e.tensor_tensor(out=t2, in0=aim, in1=bre_, op=ALU.mult), reads=[aimk, brek], writes=["big2"])
                if neg_im:
                    p.op(eng, lambda e: e.scalar_tensor_tensor(out=outim, in0=t1, scalar=-1.0, in1=t2, op0=ALU.mult, op1=ALU.subtract),
                         reads=["big1", "big2"], writes=[outimk])
                else:
                    p.op(eng, lambda e: e.tensor_tensor(out=outim, in0=t1, in1=t2, op=ALU.add), reads=["big1", "big2"], writes=[outimk])

            Bre = small("Bre", (128, 16, 16)); Bim = small("Bim", (128, 16, 16))
            cmul(Bre[:], "Bre", Bim[:], "Bim", bc_last(bre_s[:], 16), "betare", bc_last(bim_s[:], 16), "betaim",
                 bre[:], "bre", bim[:], "bim", (128, 16, 16))
            ckpt()
            Xre = sbt(st, "Xre", [128, 16, 8, 16], F32); Xim = sbt(st, "Xim", [128, 16, 8, 16], F32)
            XGre = sbt(st, "XGre", [128, 16, 8, 16], F32); XGim = sbt(st, "XGim", [128, 16, 8, 16], F32)
            Yre = sbt(st, "Yre", [128, 16, 8, 16], F32); nYim = sbt(st, "nYim", [128, 16, 8, 16], F32)
            sh4 = (128, 16, 8, 16)

            def pw(tab, lo):
                return bc_last(tab[:, :, lo:lo + 8], 16)

            def mid(t):
                return t[:].rearrange("p q (o c) -> p q o c", o=1).to_broadcast([128, 16, 8, 16])

            cmul(Xre[:], "Xre", Xim[:], "Xim", pw(PRE, 0), "PRE", pw(PIM, 0), "PIM", mid(Bre), "Bre", mid(Bim), "Bim", sh4)
            cmul(XGre[:], "XGre", XGim[:], "XGim", pw(PRE, 8), "PRE", pw(PIM, 8), "PIM", mid(Bre), "Bre", mid(Bim), "Bim", sh4)
            cmul(Yre[:], "Yre", nYim[:], "nYim", pw(PRE, 16), "PRE", pw(PIM, 16), "PIM", mid(cre), "cre", mid(cim), "cim", sh4, neg_im=True)
            Hre4 = Hre[:].rearrange("p q (j o) -> p q j o", o=16)
            Hni4 = Hni[:].rearrange("p q (j o) -> p q j o", o=16)
            cmul(Hre4, "Hre", Hni4, "Hni", pw(PRE, 24), "PRE", pw(PIM, 24), "PIM", mid(cre), "cre", mid(cim), "cim", sh4, neg_im=True)
            emit_adaln()
            psS = [pst(st, f"psS{i}", [128, 512]) for i in range(4)]
            pctr = 0
            b1f = big1[:].rearrange("p a b c -> p (a b c)")
            Ym = sbt(st, "Ym", [128, 16, 8, 16], F32)
            nYm = sbt(st, "nYm", [128, 16, 8, 16], F32)
            for hf in range(2):
                p.op("dve", lambda e, hf=hf: e.tensor_scalar(out=Ym[:], in0=Yre[:], scalar1=hmask[:, hf:hf + 1], scalar2=None, op0=ALU.mult),
                     reads=["Yre", "hmask"], writes=["Ym"])
                p.op("dve", lambda e, hf=hf: e.tensor_scalar(out=nYm[:], in0=nYim[:], scalar1=hmask[:, hf:hf + 1], scalar2=None, op0=ALU.mult),
                     reads=["nYim", "hmask"], writes=["nYm"])
                for q4 in range(4):
                    ps = psS[pctr % 4]; pk = f"psS{pctr % 4}"; pctr += 1
                    for gg in range(4):
                        q = q4 * 4 + gg
                        p.op("pe", lambda e, ps=ps, gg=gg, q=q: e.matmul(
                            ps[:, gg * 128:(gg + 1) * 128], lhsT=Xre[:, q, :, :].rearrange("p i c -> p (i c)"),
                            rhs=Ym[:, q, :, :].rearrange("p j o -> p (j o)"), start=True, stop=False),
                             reads=["Xre", "Ym"], writes=[pk])
                        p.op("pe", lambda e, ps=ps, gg=gg, q=q: e.matmul(
                            ps[:, gg * 128:(gg + 1) * 128], lhsT=Xim[:, q, :, :].rearrange("p i c -> p (i c)"),
                            rhs=nYm[:, q, :, :].rearrange("p j o -> p (j o)"), start=False, stop=True),
                             reads=["Xim", "nYm"], writes=[pk])
                    for gg in range(4):
                        q = q4 * 4 + gg
                        g = 2 * q + hf
                        p.op("dve", lambda e, ps=ps, gg=gg: e.tensor_tensor(
                            out=b1f[:, gg * 128:(gg + 1) * 128], in0=ps[:, gg * 128:(gg + 1) * 128], in1=cmask[:], op=ALU.mult),
                             reads=[pk, "cmask"], writes=["big1"])
                        p.op("dve", lambda e, g=g, gg=gg: e.scalar_tensor_tensor(
                            out=Mbf[:, g, :], in0=identf[:], scalar=dvec[:, g:g + 1],
                            in1=b1f[:, gg * 128:(gg + 1) * 128], op0=ALU.mult, op1=ALU.add),
                             reads=["identf", "dvec", "big1"], writes=["Mbf"])
            ckpt()
            for (src, srck, dstt, dstk) in ((XGre, "XGre", Gre, "Gre"), (XGim, "XGim", Gim, "Gim")):
                for q8 in range(4):
                    ps = psS[pctr % 4]; pk = f"psS{pctr % 4}"; pctr += 1
                    for qq in range(4):
                        q = q8 * 4 + qq
                        p.op("pe", lambda e, ps=ps, qq=qq, q=q, src=src: e.transpose(
                            ps[:, qq * 128:(qq + 1) * 128], src[:, q, :, :].rearrange("p i c -> p (i c)"), identf[:]),
                             reads=[srck, "identf"], writes=[pk])
                    p.op("act", lambda e, ps=ps, q8=q8, dstt=dstt: e.activation(
                        out=dstt[:, q8 * 8:(q8 + 1) * 8, :].rearrange("p g c -> p (g c)"), in_=ps[:, :], func=AF.Copy),
                         reads=[pk], writes=[dstk])
            ckpt()
            p.op("dve", lambda e: e.tensor_copy(out=dec8[:], in_=RHO[:, :, 31]), reads=["RHO"], writes=["dec8"])
            p.op("dve", lambda e: e.tensor_copy(out=c8[:], in_=COSt[:, :, 31]), reads=["COSt"], writes=["c8"])
            p.op("dve", lambda e: e.tensor_copy(out=s8[:], in_=SINt[:, :, 31]), reads=["SINt"], writes=["s8"])
            phase_end([("PRE", PRE[:].rearrange("p a b -> p (a b)"), [128, 512], F32),
                       ("PIM", PIM[:].rearrange("p a b -> p (a b)"), [128, 512], F32),
                       ("Mbf", Mbf[:].rearrange("p g c -> p (g c)"), [128, G * 128], BF16),
                       ("Gre", Gre[:].rearrange("p g c -> p (g c)"), [128, G * 64], BF16),
                       ("Hre", Hre[:].rearrange("p g c -> p (g c)"), [128, 16 * 128], BF16)])
        p0s.close()

        with ExitStack() as s5:
            Ytm = sbt(s5, "Ytm", [128, NB, 8, 512], BF16)
            with ExitStack() as s5a:
                U = sbt(s5a, "U", [128, NB, G, 8, 16], BF16)
                with ExitStack() as st:
                    ctx = {"name": "p1a", "psb_ctr": [0],
                           "psb": [pst(st, f"p1a_psb{i}", [128, 1024], BF16) for i in range(2)]}
                    psf = [pst(st, f"p1a_psf{i}", [128, 512]) for i in range(6)]
                    xt = [sbt(st, f"p1a_xt{i}", [128, D], F32) for i in range(8)]
                    xn4 = [sbt(st, f"p1a_xn{i}", [128, 4, D], BF16) for i in range(2)]
                    hT = [sbt(st, f"p1a_hT{i}", [128, 8, 1024], BF16) for i in range(1)]
                    wU = sbt(st, "wU", [128, 8, 512], BF16)
                    wUs = sbt(st, "wUs", [128, 4, 512], F32)
                    for hh in range(2):
                        p.dma(lambda e, hh=hh: e.dma_start(out=wUs[:], in_=w_in.rearrange("(kt p) c -> p kt c", p=128)[:, hh * 4:(hh + 1) * 4, 0:512]),
                              "ld_w0", writes=["wUs"])
                        p.op("act", lambda e, hh=hh: e.activation(out=wU[:, hh * 4:(hh + 1) * 4, :].rearrange("p a b -> p (a b)"),
                                                                  in_=wUs[:].rearrange("p a b -> p (a b)"), func=AF.Copy),
                             reads=["wUs"], writes=[f"wU{hh}"])
                    pf = 0
                    cast_step(32)
                    normA(make_xloader(xt, "p1a_xt", x_all[0:512, :]), xn4[0], "p1a_xn0")
                    ckpt()
                    for c in range(8):
                        nbk = c // 2
                        r = c % 2
                        hb = hT[0]; hbk = f"p1a_hT0_{c % 2}"
                        normB(ctx, xn4[r], f"p1a_xn{r}", hb, hbk, geff1, "geff1", sh1, "modT0_1", perm_half=(c % 2))
                        if c == 0:
                            ckpt()
                        if c + 1 < 8:
                            r2 = (c + 1) % 2
                            normA(make_xloader(xt, "p1a_xt", x_all[(c + 1) * 512:(c + 2) * 512, :]), xn4[r2], f"p1a_xn{r2}")
                        if c % 2 == 1:
                            hk0 = "p1a_hT0_0"; hk1 = "p1a_hT0_1"
                            for i in range(8):
                                ps = psf[pf % 6]; pk = f"p1a_psf{pf % 6}"; pf += 1
                                for kt in range(8):
                                    p.op("pe", lambda e, ps=ps, hb=hb, kt=kt, i=i: e.matmul(
                                        ps[:, :], lhsT=hb[:, kt, i * 128:(i + 1) * 128], rhs=wU[:, kt, :], start=(kt == 0), stop=(kt == 7)),
                                         reads=[f"{hk0}_{kt}", f"{hk1}_{kt}", "wU0", "wU1"], writes=[pk])
                                if i % 2 == 0:
                                    p.op("act", lambda e, ps=ps, nbk=nbk, i=i: e.activation(
                                        out=U[:, nbk, :, i, :], in_=ps[:, :].rearrange("p (g c) -> p g c", c=16), func=AF.Copy),
                                         reads=[pk], writes=[newkey("U")])
                                else:
                                    p.op("dve", lambda e, ps=ps, nbk=nbk, i=i: e.tensor_copy(
                                        out=U[:, nbk, :, i, :], in_=ps[:, :].rearrange("p (g c) -> p g c", c=16)),
                                         reads=[pk], writes=[newkey("U")])
                        if c == 1:
                            ckpt()
                        cast_step(2)
                    phase_end([("U", U[:].rearrange("p a g i c -> p (a g i c)"), [128, NB * G * 128], BF16)])

                CS = sbt(s5a, "CS", [128, 8, 512], F32)
                SN = sbt(s5a, "SN", [128, 8, 512], F32)
                with ExitStack() as st:
                    Ug = [[sbt(st, f"Ug{r}_{h}", [128, 512], BF16) for h in range(2)] for r in range(2)]
                    psb = [pst(st, f"p2_psb{i}", [128, 1024], BF16) for i in range(2)]
                    psZ = [pst(st, f"p2_psZ{i}", [128, 512]) for i in range(4)]
                    psY = [pst(st, f"p2_psY{i}", [128, 512]) for i in range(2)]
                    tmp = [sbt(st, f"p2_tmp{i}", [128, 512], F32) for i in range(4)]
                    Win = [sbt(st, f"Win{c}", [128, 512], F32) for c in range(2)]
                    Wst = [sbt(st, f"Wst{c}", [128, 512], F32) for c in range(2)]
                    Sp = [[sbt(st, f"Sp{r}_{c}", [128, 512], BF16) for c in range(2)] for r in range(2)]
                    dt1 = sbt(st, "dt1", [128, 8, 256], F32)
                    dt2 = sbt(st, "dt2", [128, 8, 256], F32)
                    cmt = [sbt(st, f"cmt{i}", [128, 8], F32) for i in range(2)]
                    smt = [sbt(st, f"smt{i}", [128, 8], F32) for i in range(2)]
                    sq1 = sbt(st, "sq1", [128, 8], F32)
                    sq2 = sbt(st, "sq2", [128, 8], F32)
                    for r in range(2):
                        for c in range(2):
                            p.op("dve", lambda e, r=r, c=c: e.memset(Sp[r][c][:, 0:1], 0.0), writes=[f"Sp{r}_{c}"])
                    yctr = 0
                    for half in range(2):
                        q0 = half * 8
                        p.op("dve", lambda e, q0=q0: e.tensor_copy(out=cmt[0][:], in_=c8[:, q0:q0 + 8]), reads=["c8"], writes=["cmt0"])
                        p.op("dve", lambda e, q0=q0: e.tensor_copy(out=smt[0][:], in_=s8[:, q0:q0 + 8]), reads=["s8"], writes=["smt0"])
                        p.op("dve", lambda e: e.memset(CS[:, :, 0:1], 1.0), writes=["CS"] + [f"CS{qq}" for qq in range(8)])
                        p.op("dve", lambda e: e.memset(SN[:, :, 0:1], 0.0), writes=["SN"] + [f"SN{qq}" for qq in range(8)])
                        m = 1
                        it = 0
                        while m < 512:
                            c_, s_ = cmt[it % 2], smt[it % 2]
                            ck, sk = f"cmt{it % 2}", f"smt{it % 2}"
                            if m < 32:
                                t1 = dt1[:, :, 0:m]
                                t2 = dt2[:, :, 0:m]
                                cb = bc_last(c_[:], m); sb_ = bc_last(s_[:], m)
                                p.op("dve", lambda e, t1=t1, cb=cb, m=m: e.tensor_tensor(out=t1, in0=CS[:, :, 0:m], in1=cb, op=ALU.mult), reads=["CS", ck], writes=["dt1"])
                                p.op("dve", lambda e, t2=t2, sb_=sb_, m=m: e.tensor_tensor(out=t2, in0=SN[:, :, 0:m], in1=sb_, op=ALU.mult), reads=["SN", sk], writes=["dt2"])
                                p.op("dve", lambda e, t1=t1, t2=t2, m=m: e.tensor_tensor(out=CS[:, :, m:2 * m], in0=t1, in1=t2, op=ALU.subtract), reads=["dt1", "dt2"], writes=["CS"])
                                p.op("dve", lambda e, t1=t1, cb=cb, m=m: e.tensor_tensor(out=t1, in0=SN[:, :, 0:m], in1=cb, op=ALU.mult), reads=["SN", ck], writes=["dt1"])
                                p.op("dve", lambda e, t2=t2, sb_=sb_, m=m: e.tensor_tensor(out=t2, in0=CS[:, :, 0:m], in1=sb_, op=ALU.mult), reads=["CS", sk], writes=["dt2"])
                                p.op("dve", lambda e, t1=t1, t2=t2, m=m: e.tensor_tensor(out=SN[:, :, m:2 * m], in0=t1, in1=t2, op=ALU.add), reads=["dt1", "dt2"], writes=["SN"])
                            else:
                                for qq in range(8):
                                    p.op("act", lambda e, qq=qq, m=m, s_=s_: e.activation(out=dt1[:, qq, 0:m], in_=SN[:, qq, 0:m], func=AF.Identity, scale=s_[:, qq:qq + 1]),
                                         reads=["SN", f"SN{qq}", sk, "dt1"], writes=[f"dt1_{qq}"])
                                    p.op("act", lambda e, qq=qq, m=m, s_=s_: e.activation(out=dt2[:, qq, 0:m], in_=CS[:, qq, 0:m], func=AF.Identity, scale=s_[:, qq:qq + 1]),
                                         reads=["CS", f"CS{qq}", sk, "dt2"], writes=[f"dt2_{qq}"])
                                for qq in range(8):
                                    p.op("dve", lambda e, qq=qq, m=m, c_=c_: e.scalar_tensor_tensor(
                                        out=CS[:, qq, m:2 * m], in0=CS[:, qq, 0:m], scalar=c_[:, qq:qq + 1], in1=dt1[:, qq, 0:m], op0=ALU.mult, op1=ALU.subtract),
                                         reads=["CS", f"CS{qq}", ck, f"dt1_{qq}"], writes=[f"CS{qq}"])
                                    p.op("dve", lambda e, qq=qq, m=m, c_=c_: e.scalar_tensor_tensor(
                                        out=SN[:, qq, m:2 * m], in0=SN[:, qq, 0:m], scalar=c_[:, qq:qq + 1], in1=dt2[:, qq, 0:m], op0=ALU.mult, op1=ALU.add),
                                         reads=["SN", f"SN{qq}", ck, f"dt2_{qq}"], writes=[f"SN{qq}"])
                            if 2 * m < 512:
                                cn, sn_ = cmt[(it + 1) % 2], smt[(it + 1) % 2]
                                cnk, snk = f"cmt{(it + 1) % 2}", f"smt{(it + 1) % 2}"
                                p.op("dve", lambda e, c_=c_: e.tensor_tensor(out=sq1[:], in0=c_[:], in1=c_[:], op=ALU.mult), reads=[ck], writes=["sq1"])
                                p.op("dve", lambda e, s_=s_: e.tensor_tensor(out=sq2[:], in0=s_[:], in1=s_[:], op=ALU.mult), reads=[sk], writes=["sq2"])
                                p.op("dve", lambda e, cn=cn: e.tensor_tensor(out=cn[:], in0=sq1[:], in1=sq2[:], op=ALU.subtract), reads=["sq1", "sq2"], writes=[cnk])
                                p.op("dve", lambda e, sn_=sn_, c_=c_, s_=s_: e.scalar_tensor_tensor(out=sn_[:], in0=c_[:], scalar=2.0, in1=s_[:], op0=ALU.mult, op1=ALU.mult),
                                     reads=[ck, sk], writes=[snk])
                            m *= 2
                            it += 1
                        for ql in range(8):
                            q = q0 + ql
                            r = q % 2
                            for hf in range(2):
                                g = 2 * q + hf
                                pb = psb[hf]; pbk = f"p2_psb{hf}"
                                for nbk in range(4):
                                    p.op("pe", lambda e, pb=pb, nbk=nbk, g=g: e.transpose(
                                        pb[:, nbk * 128:(nbk + 1) * 128], U[:, nbk, g, :, :].rearrange("p i c -> p (i c)"), identb[:]),
                                         reads=["identb"], writes=[pbk])
                                if hf == 0:
                                    p.op("act", lambda e, pb=pb, r=r, hf=hf: e.activation(out=Ug[r][hf][:], in_=pb[:, 0:512], func=AF.Copy),
                                         reads=[pbk], writes=[f"Ug{r}_{hf}"])
                                else:
                                    p.op("dve", lambda e, pb=pb, r=r, hf=hf: e.tensor_copy(out=Ug[r][hf][:], in_=pb[:, 0:512]),
                                         reads=[pbk], writes=[f"Ug{r}_{hf}"])
                            zre = psZ[(2 * q) % 4]; zrek = f"p2_psZ{(2 * q) % 4}"
                            zim = psZ[(2 * q + 1) % 4]; zimk = f"p2_psZ{(2 * q + 1) % 4}"
                            for hf in range(2):
                                g = 2 * q + hf
                                lo, hi = hf * 64, hf * 64 + 64
                                p.op("pe", lambda e, zre=zre, g=g, lo=lo, hi=hi, r=r, hf=hf: e.matmul(
                                    zre[lo:hi, :], lhsT=Gre[:, g, :], rhs=Ug[r][hf][:], start=True, stop=True),
                                     reads=["Gre", f"Ug{r}_{hf}"], writes=[zrek])
                                p.op("pe", lambda e, zim=zim, g=g, lo=lo, hi=hi, r=r, hf=hf: e.matmul(
                                    zim[lo:hi, :], lhsT=Gim[:, g, :], rhs=Ug[r][hf][:], start=True, stop=True),
                                     reads=["Gim", f"Ug{r}_{hf}"], writes=[zimk])
                            p.op("dve", lambda e, zre=zre, ql=ql: e.tensor_tensor(out=tmp[0][:], in0=zre[:, :], in1=CS[:, ql, :], op=ALU.mult), reads=[zrek, "CS", f"CS{ql}"], writes=["p2_tmp0"])
                            p.op("dve", lambda e, zim=zim, ql=ql: e.tensor_tensor(out=tmp[1][:], in0=zim[:, :], in1=SN[:, ql, :], op=ALU.mult), reads=[zimk, "SN", f"SN{ql}"], writes=["p2_tmp1"])
                            p.op("dve", lambda e, zim=zim, ql=ql: e.tensor_tensor(out=tmp[2][:], in0=zim[:, :], in1=CS[:, ql, :], op=ALU.mult), reads=[zimk, "CS", f"CS{ql}"], writes=["p2_tmp2"])
                            p.op("dve", lambda e, zre=zre, ql=ql: e.tensor_tensor(out=tmp[3][:], in0=zre[:, :], in1=SN[:, ql, :], op=ALU.mult), reads=[zrek, "SN", f"SN{ql}"], writes=["p2_tmp3"])
                            p.op("dve", lambda e: e.tensor_tensor(out=Win[0][:], in0=tmp[0][:], in1=tmp[1][:], op=ALU.add), reads=["p2_tmp0", "p2_tmp1"], writes=["Win0"])
                            p.op("dve", lambda e: e.tensor_tensor(out=Win[1][:], in0=tmp[2][:], in1=tmp[3][:], op=ALU.subtract), reads=["p2_tmp2", "p2_tmp3"], writes=["Win1"])
                            for c in range(2):
                                p.op("dve", lambda e, c=c, q=q: e.tensor_tensor_scan(
                                    out=Wst[c][:], data0=dec8[:, q:q + 1].to_broadcast([128, 512]), data1=Win[c][:],
                                    initial=0.0, op0=ALU.mult, op1=ALU.add), reads=["dec8", f"Win{c}"], writes=[f"Wst{c}"])
                            p.op("dve", lambda e, ql=ql: e.tensor_tensor(out=tmp[0][:, 0:511], in0=Wst[0][:, 0:511], in1=CS[:, ql, 0:511], op=ALU.mult), reads=["Wst0", "CS", f"CS{ql}"], writes=["p2_tmp0"])
                            p.op("dve", lambda e, ql=ql: e.tensor_tensor(out=tmp[1][:, 0:511], in0=Wst[1][:, 0:511], in1=SN[:, ql, 0:511], op=ALU.mult), reads=["Wst1", "SN", f"SN{ql}"], writes=["p2_tmp1"])
                            p.op("dve", lambda e, ql=ql: e.tensor_tensor(out=tmp[2][:, 0:511], in0=Wst[1][:, 0:511], in1=CS[:, ql, 0:511], op=ALU.mult), reads=["Wst1", "CS", f"CS{ql}"], writes=["p2_tmp2"])
                            p.op("dve", lambda e, ql=ql: e.tensor_tensor(out=tmp[3][:, 0:511], in0=Wst[0][:, 0:511], in1=SN[:, ql, 0:511], op=ALU.mult), reads=["Wst0", "SN", f"SN{ql}"], writes=["p2_tmp3"])
                            p.op("dve", lambda e, r=r: e.tensor_tensor(out=Sp[r][0][:, 1:512], in0=tmp[0][:, 0:511], in1=tmp[1][:, 0:511], op=ALU.subtract), reads=["p2_tmp0", "p2_tmp1"], writes=[f"Sp{r}_0"])
                            p.op("dve", lambda e, r=r: e.tensor_tensor(out=Sp[r][1][:, 1:512], in0=tmp[2][:, 0:511], in1=tmp[3][:, 0:511], op=ALU.add), reads=["p2_tmp2", "p2_tmp3"], writes=[f"Sp{r}_1"])
                            for nbk in range(4):
                                py = psY[yctr % 2]; pyk = f"p2_psY{yctr % 2}"; yctr += 1
                                for hf in range(2):
                                    g = 2 * q + hf
                                    lo, hi = hf * 64, hf * 64 + 64
                                    p.op("pe", lambda e, py=py, hf=hf, r=r, nbk=nbk, g=g: e.matmul(
                                        py[:, hf * 128:(hf + 1) * 128], lhsT=Ug[r][hf][:, nbk * 128:(nbk + 1) * 128], rhs=Mbf[:, g, :],
                                        start=True, stop=False), reads=[f"Ug{r}_{hf}"], writes=[pyk])
                                    p.op("pe", lambda e, py=py, hf=hf, r=r, nbk=nbk, q=q, lo=lo, hi=hi: e.matmul(
                                        py[:, hf * 128:(hf + 1) * 128], lhsT=Sp[r][0][lo:hi, nbk * 128:(nbk + 1) * 128], rhs=Hre[lo:hi, q, :],
                                        start=False, stop=False), reads=[f"Sp{r}_0"], writes=[pyk])
                                    p.op("pe", lambda e, py=py, hf=hf, r=r, nbk=nbk, q=q, lo=lo, hi=hi: e.matmul(
                                        py[:, hf * 128:(hf + 1) * 128], lhsT=Sp[r][1][lo:hi, nbk * 128:(nbk + 1) * 128], rhs=Hni[lo:hi, q, :],
                                        start=False, stop=True), reads=[f"Sp{r}_1"], writes=[pyk])
                                p.op("act", lambda e, py=py, nbk=nbk, q=q: e.activation(
                                    out=Ytm[:, nbk, :, 32 * q:32 * q + 32].rearrange("p j (h o) -> p j h o", h=2),
                                    in_=py[:, 0:256].rearrange("p (h j o) -> p j h o", h=2, j=8), func=AF.Gelu_apprx_tanh),
                                     reads=[pyk], writes=[newkey("Ytm")])
                            cast_step(1)
                    phase_end([("Ytm", Ytm[:].rearrange("p a j c -> p (a j c)"), [128, NB * 8 * 512], BF16),
                               ("Sp", Sp[1][0][:], [128, 512], BF16), ("CS", CS[:].rearrange("p a b -> p (a b)"), [128, 4096], F32)])

            with ExitStack() as st:
                yT = sbt(st, "yT", [128, 4, 2048], BF16)
                sgT = sbt(st, "sgT", [128, 4, 2048], BF16)
                selt = sbt(st, "selt", [128, 4, 64], BF16)
                wglu_t = sbt(st, "wglu_t", [128, 4, 512], BF16)
                bglu_t = sbt(st, "bglu_t", [128, 4], F32)
                sg = [sbt(st, f"sg{i}", [128, 512], BF16) for i in range(2)]
                psZ = [pst(st, f"p2b_ps{i}", [128, 512]) for i in range(4)]
                cload(selt[:], sel_d, "selt")
                cload(bglu_t[:], b_gluT_d, "bglu_t")
                wk = cast_until(["wglu_s"])
                load_w(wglu_t[:], wglu_s.rearrange("(kt p) c -> p kt c", p=128), "wglu_t", wk, "ld_w1")
                zc = 0
                for s in range(4):
                    for ct in range(4):
                        ps = psZ[zc % 4]; pk = f"p2b_ps{zc % 4}"; zc += 1
                        for j in range(8):
                            p.op("pe", lambda e, ps=ps, s=s, ct=ct, j=j: e.matmul(
                                ps[:, j * 64:(j + 1) * 64], lhsT=Ytm[:, s, j, ct * 128:(ct + 1) * 128], rhs=selt[:, s, :],
                                start=True, stop=True), reads=["selt"], writes=[pk])
                        if ct % 2 == 0:
                            p.op("act", lambda e, ps=ps, s=s, ct=ct: e.activation(
                                out=yT[:, ct, s * 512:(s + 1) * 512].rearrange("p (m j) -> p m j", j=8),
                                in_=ps[:, :].rearrange("p (j m) -> p m j", j=8), func=AF.Copy), reads=[pk], writes=[f"yT{s}_{ct}"])
                        else:
                            p.op("dve", lambda e, ps=ps, s=s, ct=ct: e.tensor_copy(
                                out=yT[:, ct, s * 512:(s + 1) * 512].rearrange("p (m j) -> p m j", j=8),
                                in_=ps[:, :].rearrange("p (j m) -> p m j", j=8)), reads=[pk], writes=[f"yT{s}_{ct}"])
                for s in range(4):
                    for ct in range(4):
                        ps = psZ[zc % 4]; pk = f"p2b_ps{zc % 4}"; zc += 1
                        for kt in range(4):
                            p.op("pe", lambda e, ps=ps, s=s, ct=ct, kt=kt: e.matmul(
                                ps[:, :], lhsT=wglu_t[:, kt, ct * 128:(ct + 1) * 128], rhs=yT[:, kt, s * 512:(s + 1) * 512],
                                start=(kt == 0), stop=(kt == 3)), reads=["wglu_t", f"yT{s}_{kt}"], writes=[pk])
                        sgi = zc % 2
                        p.op("act", lambda e, ps=ps, ct=ct, sgi=sgi: e.activation(
                            out=sg[sgi][:], in_=ps[:, :], func=AF.Sigmoid, bias=bglu_t[:, ct:ct + 1]),
                             reads=[pk, "bglu_t"], writes=[f"sg{sgi}"])
                        p.op("dve", lambda e, s=s, ct=ct, sgi=sgi: e.tensor_tensor(
                            out=sgT[:, ct, s * 512:(s + 1) * 512], in0=sg[sgi][:], in1=yT[:, ct, s * 512:(s + 1) * 512], op=ALU.mult),
                             reads=[f"sg{sgi}", f"yT{s}_{ct}"], writes=[f"sgT{s}_{ct}"])
                p.dma(lambda e: e.dma_start(out=sglu_s, in_=sgT[:]), "st_sg",
                      reads=[f"sgT{s}_{ct}" for s in range(4) for ct in range(4)], writes=["sglu_s"])
                phase_end([("sgluT", sgT[:].rearrange("p a b -> p (a b)"), [128, 4 * 2048], BF16)])

        s5o.close()
        with ExitStack() as at:
            KT = sbt(at, "KT", [128, 4, T], BF16)
            V = sbt(at, "V", [128, 32, 512], BF16)
            QT = sbt(at, "QT", [128, 4, 2048], BF16)
            with ExitStack() as st:
                ctx = {"name": "p1b", "psb_ctr": [0],
                       "psb": [pst(st, f"p1b_psb{i}", [128, 1024], BF16) for i in range(2)]}
                psf = [pst(st, f"p1b_psf{i}", [128, 512]) for i in range(6)]
                xt = [sbt(st, f"p1b_xt{i}", [128, D], F32) for i in range(8)]
                xn4 = [sbt(st, f"p1b_xn{i}", [128, 4, D], BF16) for i in range(2)]
                hT = [sbt(st, f"p1b_hT{i}", [128, 8, 512], BF16) for i in range(1)]
                wQKV = sbt(st, "wQKV", [128, 8, 1536], BF16)

                wk = cast_until(["win_s"])
                load_w(wQKV[:], win_s.rearrange("(kt p) c -> p kt c", p=128)[:, :, 512:2048], "wQKV", wk, "ld_w2")
                chunks = [("all", c) for c in range(8)] + [("own", s) for s in range(4)]
                pf = 0

                def rows(kind, i):
                    return (x_all if kind == "all" else x_own)[i * 512:(i + 1) * 512, :]

                normA(make_xloader(xt, "p1b_xt", rows(*chunks[0])), xn4[0], "p1b_xn0")
                for ci, (kind, idx) in enumerate(chunks):
                    r = ci % 2
                    hb = hT[0]; hbk = "p1b_hT0"
                    normB(ctx, xn4[r], f"p1b_xn{r}", hb, hbk, geff1, "geff1", sh1, "modT0_1")
                    if ci + 1 < len(chunks):
                        r2 = (ci + 1) % 2
                        normA(make_xloader(xt, "p1b_xt", rows(*chunks[ci + 1])), xn4[r2], f"p1b_xn{r2}")
                    if kind == "all":
                        for ct in range(4):
                            ps = psf[pf % 6]; pk = f"p1b_psf{pf % 6}"; pf += 1
                            for kt in range(8):
                                p.op("pe", lambda e, ps=ps, hb=hb, kt=kt, ct=ct: e.matmul(
                                    ps[:, :], lhsT=wQKV[:, kt, 512 + ct * 128: 512 + (ct + 1) * 128], rhs=hb[:, kt, :],
                                    start=(kt == 0), stop=(kt == 7)), reads=[f"{hbk}_{kt}", "wQKV"], writes=[pk])
                            if ct % 2 == 0:
                                p.op("act", lambda e, ps=ps, ct=ct, idx=idx: e.activation(out=KT[:, ct, idx * 512:(idx + 1) * 512], in_=ps[:, :], func=AF.Copy),
                                     reads=[pk], writes=[newkey("KT")])
                            else:
                                p.op("dve", lambda e, ps=ps, ct=ct, idx=idx: e.tensor_copy(out=KT[:, ct, idx * 512:(idx + 1) * 512], in_=ps[:, :]),
                                     reads=[pk], writes=[newkey("KT")])
                        for tt in range(4):
                            ps = psf[pf % 6]; pk = f"p1b_psf{pf % 6}"; pf += 1
                            for kt in range(8):
                                p.op("pe", lambda e, ps=ps, hb=hb, kt=kt, tt=tt: e.matmul(
                                    ps[:, :], lhsT=hb[:, kt, tt * 128:(tt + 1) * 128], rhs=wQKV[:, kt, 1024:1536],
                                    start=(kt == 0), stop=(kt == 7)), reads=[f"{hbk}_{kt}", "wQKV"], writes=[pk])
                            if tt % 2 == 0:
                                p.op("act", lambda e, ps=ps, tt=tt, idx=idx: e.activation(out=V[:, idx * 4 + tt, :], in_=ps[:, :], func=AF.Copy),
                                     reads=[pk], writes=[newkey("V")])
                            else:
                                p.op("dve", lambda e, ps=ps, tt=tt, idx=idx: e.tensor_copy(out=V[:, idx * 4 + tt, :], in_=ps[:, :]),
                                     reads=[pk], writes=[newkey("V")])
                    else:
                        for ct in range(4):
                            ps = psf[pf % 6]; pk = f"p1b_psf{pf % 6}"; pf += 1
                            for kt in range(8):
                                p.op("pe", lambda e, ps=ps, hb=hb, kt=kt, ct=ct: e.matmul(
                                    ps[:, :], lhsT=wQKV[:, kt, ct * 128:(ct + 1) * 128], rhs=hb[:, kt, :],
                                    start=(kt == 0), stop=(kt == 7)), reads=[f"{hbk}_{kt}", "wQKV"], writes=[pk])
                            p.op("act", lambda e, ps=ps, ct=ct, idx=idx: e.activation(out=QT[:, ct, idx * 512:(idx + 1) * 512], in_=ps[:, :], func=AF.Copy, scale=0.125),
                                 reads=[pk], writes=[newkey("QT")])
                    cast_step(2)
                phase_end([("KT", KT[:].rearrange("p a b -> p (a b)"), [128, 4 * T], BF16),
                           ("V", V[:].rearrange("p a b -> p (a b)"), [128, 32 * 512], BF16),
                           ("QT", QT[:].rearrange("p a b -> p (a b)"), [128, 4 * 2048], BF16)])

            with ExitStack() as st:
                atri = sbt(st, "atri", [128, 128], BF16)
                neguincl = sbt(st, "neguincl", [128, 128], BF16)
                neglow = sbt(st, "neglow", [128, 128], BF16)
                cload(atri[:], atri_d, "atri")
                cload(neguincl[:], neguincl_d, "neguincl")
                cload(neglow[:], neglow_d, "neglow")
                attnT = sbt(st, "attnT", [128, 4, 2048], BF16)
                mB = [sbt(st, f"mB{i}", [128, 8, 512], BF16) for i in range(2)]
                psP = [pst(st, f"psP{i}", [128, 512]) for i in range(2)]
                psO = pst(st, "psO", [128, 512])
                psZ = [pst(st, f"a_psZ{i}", [128, 512]) for i in range(4)]
                NE, NL, NP, NW = 3, 4, 2, 3
                eb = [[sbt(st, f"eb{h}_{i}", [128, 512], F32) for i in range(NE)] for h in range(2)]
                Lb = [[sbt(st, f"Lb{h}_{i}", [128, 512], BF16) for i in range(NL)] for h in range(2)]
                ePb = [[sbt(st, f"ePb{h}_{i}", [128, 512], F32) for i in range(NP)] for h in range(2)]
                Wb = [[sbt(st, f"Wb{h}_{i}", [128, 512], BF16) for i in range(NW)] for h in range(2)]
                zc = [0]
                for s in range(4):
                    mr = s % 2
                    p.dma(lambda e, mr=mr, s=s: e.dma_start(out=mB[mr][:], in_=maskB_d[:, s, :, :]), f"ld_mask{mr}", writes=[f"mB{mr}"])
                    KBs = 8 * (s + 1)
                    kbs = list(range(KBs - 1, -1, -1))
                    n = len(kbs)
                    for hp in range(4):
                        heads = (2 * hp, 2 * hp + 1)

                        def st_qk(t):
                            kb = kbs[t]
                            for hh in range(2):
                                lo, hi = hh * 64, hh * 64 + 64
                                zi = zc[0] % 4; zc[0] += 1
                                zb = psZ[zi]; zk = f"a_psZ{zi}"
                                masked = kb >= KBs - 8
                                p.op("pe", lambda e, zb=zb, lo=lo, hi=hi, kb=kb, masked=masked, hp=hp, s=s: e.matmul(
                                    zb[:, :], lhsT=KT[lo:hi, hp, kb * 128:(kb + 1) * 128], rhs=QT[lo:hi, hp, s * 512:(s + 1) * 512],
                                    start=True, stop=(not masked)), reads=[], writes=[zk])
                                if masked:
                                    p.op("pe", lambda e, zb=zb, kb=kb, mr=mr, KBs=KBs: e.matmul(
                                        zb[:, :], lhsT=atri[:], rhs=mB[mr][:, kb - (KBs - 8), :], start=False, stop=True),
                                         reads=["atri", f"mB{mr}"], writes=[zk])
                                ei = t % NE
                                p.op("act", lambda e, zb=zb, hh=hh, ei=ei: e.activation(out=eb[hh][ei][:], in_=zb[:, :], func=AF.Exp),
                                     reads=[zk], writes=[f"eb{hh}_{ei}"])
                            for hh in range(2):
                                ei = t % NE; li = t % NL
                                p.op("act", lambda e, hh=hh, ei=ei, li=li: e.activation(out=Lb[hh][li][:], in_=eb[hh][ei][:], func=AF.Ln, bias=1.0),
                                     reads=[f"eb{hh}_{ei}"], writes=[f"Lb{hh}_{li}"])

                        def st_p(t):
                            for hh in range(2):
                                li = t % NL
                                if t > 0:
                                    lpi = (t - 1) % NL
                                    p.op("pe", lambda e, hh=hh, lpi=lpi: e.matmul(
                                        psP[hh][:, :], lhsT=neglow[:], rhs=Lb[hh][lpi][:], start=False, stop=False, skip_group_check=True),
                                         reads=["neglow", f"Lb{hh}_{lpi}"], writes=[f"psP{hh}"])
                                p.op("pe", lambda e, hh=hh, li=li, t=t: e.matmul(
                                    psP[hh][:, :], lhsT=neguincl[:], rhs=Lb[hh][li][:], start=(t == 0), stop=False, skip_group_check=True),
                                     reads=["neguincl", f"Lb{hh}_{li}"], writes=[f"psP{hh}"])
                            for hh in range(2):
                                pi = t % NP; ei = t % NE; wi = t % NW
                                p.op("act", lambda e, hh=hh, pi=pi: e.activation(out=ePb[hh][pi][:], in_=psP[hh][:, :], func=AF.Exp),
                                     reads=[f"psP{hh}"], writes=[f"ePb{hh}_{pi}"])
                                p.op("dve", lambda e, hh=hh, pi=pi, ei=ei, wi=wi: e.tensor_tensor(
                                    out=Wb[hh][wi][:], in0=eb[hh][ei][:], in1=ePb[hh][pi][:], op=ALU.mult),
                                     reads=[f"eb{hh}_{ei}", f"ePb{hh}_{pi}"], writes=[f"Wb{hh}_{wi}"])

                        def st_pv(t):
                            kb = kbs[t]
                            for hh in range(2):
                                h = heads[hh]
                                lo, hi = hh * 64, hh * 64 + 64
                                wi = t % NW
                                p.op("pe", lambda e, hh=hh, h=h, lo=lo, hi=hi, kb=kb, wi=wi, t=t, n=n: e.matmul(
                                    psO[lo:hi, :], lhsT=V[:, kb, h * 64:(h + 1) * 64], rhs=Wb[hh][wi][:],
                                    start=(t == 0), stop=(t == n - 1)), reads=[f"Wb{hh}_{wi}"], writes=["psO"])

                        for step in range(n + 2):
                            if step < n:
                                st_qk(step)
                            if 1 <= step <= n:
                                st_p(step - 1)
                            if step >= 2:
                                st_pv(step - 2)
                        if hp % 2 == 0:
                            p.op("act", lambda e, s=s, hp=hp: e.activation(out=attnT[:, hp, s * 512:(s + 1) * 512], in_=psO[:, :], func=AF.Copy),
                                 reads=["psO"], writes=[f"attnT{s}_{hp}"])
                        else:
                            p.op("dve", lambda e, s=s, hp=hp: e.tensor_copy(out=attnT[:, hp, s * 512:(s + 1) * 512], in_=psO[:, :]),
                                 reads=["psO"], writes=[f"attnT{s}_{hp}"])
                        cast_step(2)
                p.dma(lambda e: e.dma_start(out=attn_s, in_=attnT[:]), "st_at",
                      reads=[f"attnT{s}_{hp}" for s in range(4) for hp in range(4)], writes=["attn_s"])
                phase_end([("attnT", attnT[:].rearrange("p a b -> p (a b)"), [128, 4 * 2048], BF16)])

        with ExitStack() as st:
            ctx = {"name": "p4a", "psb_ctr": [0],
                   "psb": [pst(st, f"p4a_psb{i}", [128, 1024], BF16) for i in range(2)]}
            psf = [pst(st, f"p4a_psf{i}", [128, 512]) for i in range(6)]
            xcA = [sbt(st, f"p4a_xc{i}", [128, 4, D], F32) for i in range(2)]
            xn4 = [sbt(st, f"p4a_xn{i}", [128, 4, D], BF16) for i in range(2)]
            hT = [sbt(st, f"p4a_hT{i}", [128, 8, 512], BF16) for i in range(1)]
            wgt = [sbt(st, f"wgt{i}", [128, 8, 512], BF16) for i in range(3)]
            wa_t = sbt(st, "wa_t", [128, 4, D], BF16)
            wb_t = sbt(st, "wb_t", [128, 4, D], BF16)
            wo_t = sbt(st, "wo_t", [128, 8, D], BF16)
            gT = sbt(st, "gT", [128, 16, 512], BF16)
            mT = sbt(st, "mT", [128, 8, 512], BF16)
            t12 = [sbt(st, f"t12_{i}", [128, 512], F32) for i in range(4)]
            sa = [sbt(st, f"sa{i}", [128, 2, 4, 512], BF16) for i in range(2)]
            wk = cast_until(["win_s", "wa_s", "wb_s", "wo_s"])
            load_w(wa_t[:], wa_s.rearrange("(kt p) c -> p kt c", p=128), "wa_t", wk, "ld_w4")
            load_w(wb_t[:], wb_s.rearrange("(kt p) c -> p kt c", p=128), "wb_t", wk, "ld_w5")
            load_w(wo_t[:], wo_s.rearrange("(kt p) c -> p kt c", p=128), "wo_t", wk, "ld_w0")
            win_v = win_s.rearrange("(kt p) c -> p kt c", p=128)
            pf = 0
            tc_ = 0
            wq = 0

            def xc_loader(r, s):
                p.dma(lambda e, r=r, s=s: e.dma_start(out=xcA[r][:], in_=x_own[s * 512:(s + 1) * 512, :].rearrange("(tt p) d -> p tt d", p=128)),
                      f"ld_x{r}", writes=[f"p4a_xc{r}"])
                return lambda tt, r=r: (xcA[r][:, tt, :], f"p4a_xc{r}")

            normA(xc_loader(0, 0), xn4[0], "p4a_xn0")
            for s in range(4):
                r = s % 2
                hb = hT[0]; hbk = "p4a_hT0"
                sat = sa[r]
                p.dma(lambda e, sat=sat, s=s: e.dma_start(out=sat[:, 0, :, :], in_=sglu_s[:, :, s * 512:(s + 1) * 512]), f"ld_sa{r}",
                      reads=["sglu_s"], writes=[f"sa{r}_0"])
                p.dma(lambda e, sat=sat, s=s: e.dma_start(out=sat[:, 1, :, :], in_=attn_s[:, :, s * 512:(s + 1) * 512]), f"ld_sb{r}",
                      reads=["attn_s"], writes=[f"sa{r}_1"])
                sak = [f"sa{r}_0", f"sa{r}_1"]
                normB(ctx, xn4[r], f"p4a_xn{r}", hb, hbk, geff1, "geff1", sh1, "modT0_1")
                if s + 1 < 4:
                    r2 = (s + 1) % 2
                    normA(xc_loader(r2, s + 1), xn4[r2], f"p4a_xn{r2}")
                for g4 in range(4):
                    wi = wq % 3; wq += 1
                    p.dma(lambda e, wi=wi, g4=g4: e.dma_start(out=wgt[wi][:], in_=win_v[:, :, 2048 + g4 * 512: 2048 + (g4 + 1) * 512]),
                          f"ld_wgt{wi}", reads=wk, writes=[f"wgt{wi}"])
                    for gl in range(4):
                        gi = g4 * 4 + gl
                        ps = psf[pf % 6]; pk = f"p4a_psf{pf % 6}"; pf += 1
                        for kt in range(8):
                            p.op("pe", lambda e, ps=ps, hb=hb, kt=kt, gl=gl, wi=wi: e.matmul(
                                ps[:, :], lhsT=wgt[wi][:, kt, gl * 128:(gl + 1) * 128], rhs=hb[:, kt, :], start=(kt == 0), stop=(kt == 7)),
                                 reads=[f"{hbk}_{kt}", f"wgt{wi}"], writes=[pk])
                        p.op("act", lambda e, ps=ps, gi=gi: e.activation(out=gT[:, gi, :], in_=ps[:, :], func=AF.Sigmoid),
                             reads=[pk], writes=[f"gT{gi}"])
                for ct in range(8):
                    psa = psf[pf % 6]; pka = f"p4a_psf{pf % 6}"; pf += 1
                    psb_ = psf[pf % 6]; pkb = f"p4a_psf{pf % 6}"; pf += 1
                    for kt in range(4):
                        p.op("pe", lambda e, psa=psa, kt=kt, ct=ct, sat=sat: e.matmul(
                            psa[:, :], lhsT=wa_t[:, kt, ct * 128:(ct + 1) * 128], rhs=sat[:, 0, kt, :],
                            start=(kt == 0), stop=(kt == 3)), reads=["wa_t"] + sak, writes=[pka])
                    for kt in range(4):
                        p.op("pe", lambda e, psb_=psb_, kt=kt, ct=ct, sat=sat: e.matmul(
                            psb_[:, :], lhsT=wb_t[:, kt, ct * 128:(ct + 1) * 128], rhs=sat[:, 1, kt, :],
                            start=(kt == 0), stop=(kt == 3)), reads=["wb_t"] + sak, writes=[pkb])
                    ta = t12[tc_ % 4]; tak = f"t12_{tc_ % 4}"; tc_ += 1
                    tb = t12[tc_ % 4]; tbk = f"t12_{tc_ % 4}"; tc_ += 1
                    p.op("dve", lambda e, psa=psa, ta=ta, ct=ct: e.tensor_tensor(out=ta[:], in0=psa[:, :], in1=gT[:, ct, :], op=ALU.mult),
                         reads=[pka, f"gT{ct}"], writes=[tak])
                    p.op("dve", lambda e, psb_=psb_, tb=tb, ct=ct: e.tensor_tensor(out=tb[:], in0=psb_[:, :], in1=gT[:, 8 + ct, :], op=ALU.mult),
                         reads=[pkb, f"gT{8 + ct}"], writes=[tbk])
                    p.op("dve", lambda e, ta=ta, tb=tb, ct=ct: e.tensor_tensor(out=mT[:, ct, :], in0=ta[:], in1=tb[:], op=ALU.add),
                         reads=[tak, tbk], writes=[f"mT{ct}"])
                mks = [f"mT{ct}" for ct in range(8)]
                xck = f"p4a_xc{r}"
                for tt in range(4):
                    for h in range(2):
                        ps = psf[pf % 6]; pk = f"p4a_psf{pf % 6}"; pf += 1
                        for kt in range(8):
                            p.op("pe", lambda e, ps=ps, kt=kt, tt=tt, h=h: e.matmul(
                                ps[:, :], lhsT=mT[:, kt, tt * 128:(tt + 1) * 128], rhs=wo_t[:, kt, h * 512:(h + 1) * 512],
                                start=(kt == 0), stop=(kt == 7)), reads=[f"mT{kt}", "wo_t"], writes=[pk])
                        ta = t12[tc_ % 4]; tak = f"t12_{tc_ % 4}"; tc_ += 1
                        p.op("dve", lambda e, ps=ps, ta=ta, h=h: e.tensor_tensor(out=ta[:], in0=ps[:, :], in1=g1bc[:, h * 512:(h + 1) * 512], op=ALU.mult),
                             reads=[pk, "g1bc"], writes=[tak])
                        p.op("dve", lambda e, ta=ta, tt=tt, h=h, r=r: e.tensor_tensor(
                            out=xcA[r][:, tt, h * 512:(h + 1) * 512], in0=ta[:], in1=xcA[r][:, tt, h * 512:(h + 1) * 512], op=ALU.add),
                             reads=[tak, xck], writes=[xck])
                p.dma(lambda e, r=r, s=s: e.dma_start(out=x1_s[s * 512:(s + 1) * 512, :].rearrange("(tt p) d -> p tt d", p=128), in_=xcA[r][:]),
                      f"st_x1{r}", reads=[xck], writes=[f"x1_s{s}"], q="pool")
                cast_step(8)
            cast_step(500)
            phase_end(exclude=())

        with ExitStack() as st:
            ctx = {"name": "p4b", "psb_ctr": [0],
                   "psb": [pst(st, f"p4b_psb{i}", [128, 1024], BF16) for i in range(2)]}
            psf = [pst(st, f"p4b_psf{i}", [128, 512]) for i in range(6)]
            xcB = [sbt(st, f"p4b_xc{i}", [128, 4, D], F32) for i in range(2)]
            xn4 = [sbt(st, f"p4b_xn{i}", [128, 4, D], BF16) for i in range(2)]
            hT = [sbt(st, f"p4b_hT{i}", [128, 8, 512], BF16) for i in range(1)]
            wd_t = sbt(st, "wd_t", [128, NFT, D], BF16)
            wgu = [sbt(st, f"wgu{i}", [128, 2, 8, 128], BF16) for i in range(3)]
            hid = sbt(st, "hid", [128, NFT, 512], BF16)
            sgf = [sbt(st, f"sgf{i}", [128, 512], F32) for i in range(2)]
            tf = [sbt(st, f"tf{i}", [128, 512], F32) for i in range(2)]
            x2 = [sbt(st, f"x2_{i}", [128, D], F32) for i in range(2)]
            ost = [sbt(st, f"ost{i}", [128, D], F32) for i in range(2)]
            gfbc = sbt(st, "gfbc", [128, D], F32)
            cload(gfbc[:], nfg_row.partition_broadcast(128), "gfbc")
            load_w(wd_t[:], wd_s.rearrange("(ft p) c -> p ft c", p=128), "wd_t", cast_done["wd_s"], "ld_w1")
            pf = 0
            wq = 0
            tcn = 0
            xq = 0

            def x1_loader(r, s):
                p.dma(lambda e, r=r, s=s: e.dma_start(out=xcB[r][:], in_=x1_s[s * 512:(s + 1) * 512, :].rearrange("(tt p) d -> p tt d", p=128)),
                      f"ld_x1{r}", reads=[f"x1_s{s}"], writes=[f"p4b_xc{r}"])
                return lambda tt, r=r: (xcB[r][:, tt, :], f"p4b_xc{r}")

            normA(x1_loader(0, 0), xn4[0], "p4b_xn0")
            for s in range(4):
                r = s % 2
                hb = hT[0]; hbk = "p4b_hT0"
                normB(ctx, xn4[r], f"p4b_xn{r}", hb, hbk, geff2, "geff2", sh2, "modT2_1")
                if s + 1 < 4:
                    r2 = (s + 1) % 2
                    normA(x1_loader(r2, s + 1), xn4[r2], f"p4b_xn{r2}")
                for ft in range(NFT):
                    wi = wq % 3; wq += 1
                    wt = wgu[wi]; wtk = f"wgu{wi}"
                    p.dma(lambda e, wt=wt, ft=ft: e.dma_start(out=wt[:, 0, :, :], in_=wg_s[ft, :, :, :]), f"ld_wgu{wi}",
                          reads=cast_done["wg_s"], writes=[wtk + "g"])
                    p.dma(lambda e, wt=wt, ft=ft: e.dma_start(out=wt[:, 1, :, :], in_=wu_s[ft, :, :, :]), f"ld_wgv{wi}",
                          reads=cast_done["wu_s"], writes=[wtk + "u"])
                    psg = psf[pf % 6]; pkg = f"p4b_psf{pf % 6}"; pf += 1
                    psu = psf[pf % 6]; pku = f"p4b_psf{pf % 6}"; pf += 1
                    for kt in range(8):
                        p.op("pe", lambda e, psg=psg, wt=wt, kt=kt, hb=hb: e.matmul(
                            psg[:, :], lhsT=wt[:, 0, kt, :], rhs=hb[:, kt, :], start=(kt == 0), stop=(kt == 7)),
                             reads=[wtk + "g", wtk + "u", f"{hbk}_{kt}"], writes=[pkg])
                    for kt in range(8):
                        p.op("pe", lambda e, psu=psu, wt=wt, kt=kt, hb=hb: e.matmul(
                            psu[:, :], lhsT=wt[:, 1, kt, :], rhs=hb[:, kt, :], start=(kt == 0), stop=(kt == 7)),
                             reads=[wtk + "g", wtk + "u", f"{hbk}_{kt}"], writes=[pku])
                    si = ft % 2
                    p.op("act", lambda e, psg=psg, si=si: e.activation(out=sgf[si][:], in_=psg[:, :], func=AF.Silu),
                         reads=[pkg], writes=[f"sgf{si}"])
                    p.op("dve", lambda e, psu=psu, si=si, ft=ft: e.tensor_tensor(out=hid[:, ft, :], in0=psu[:, :], in1=sgf[si][:], op=ALU.mult),
                         reads=[pku, f"sgf{si}"], writes=[f"hid{ft}"])
                for tt in range(4):
                    xi = xq % 2; xq += 1
                    for h in range(2):
                        ps = psf[pf % 6]; pk = f"p4b_psf{pf % 6}"; pf += 1
                        for ft in range(NFT):
                            p.op("pe", lambda e, ps=ps, ft=ft, tt=tt, h=h: e.matmul(
                                ps[:, :], lhsT=hid[:, ft, tt * 128:(tt + 1) * 128], rhs=wd_t[:, ft, h * 512:(h + 1) * 512],
                                start=(ft == 0), stop=(ft == NFT - 1)), reads=[f"hid{ft}", "wd_t"], writes=[pk])
                        ti = tcn % 2; tcn += 1
                        p.op("dve", lambda e, ps=ps, ti=ti, h=h: e.tensor_tensor(out=tf[ti][:], in0=ps[:, :], in1=g2bc[:, h * 512:(h + 1) * 512], op=ALU.mult),
                             reads=[pk, "g2bc"], writes=[f"tf{ti}"])
                        p.op("dve", lambda e, ti=ti, xi=xi, tt=tt, h=h, r=r: e.tensor_tensor(
                            out=x2[xi][:, h * 512:(h + 1) * 512], in0=tf[ti][:], in1=xcB[r][:, tt, h * 512:(h + 1) * 512], op=ALU.add),
                             reads=[f"tf{ti}", f"p4b_xc{r}"], writes=[f"x2_{xi}_{h}"])
                    ss, ssk = newvec(); sd, sdk = newvec(); rs, rsk = newvec()
                    x2k = [f"x2_{xi}_0", f"x2_{xi}_1"]
                    p.op("act", lambda e, xi=xi, ss=ss: e.activation(out=junk[:], in_=x2[xi][:], func=AF.Square, accum_out=ss),
                         reads=x2k, writes=[ssk, "junk"])
                    p.op("act", lambda e, ss=ss, sd=sd: e.activation(out=sd, in_=ss, func=AF.Sqrt, scale=1.0 / D, bias=epsv[:, 0:1]),
                         reads=[ssk, "epsv"], writes=[sdk])
                    p.op("dve", lambda e, sd=sd, rs=rs: e.reciprocal(out=rs, in_=sd), reads=[sdk], writes=[rsk])
                    p.op("dve", lambda e, xi=xi, rs=rs: e.scalar_tensor_tensor(out=ost[xi][:], in0=x2[xi][:], scalar=rs, in1=gfbc[:], op0=ALU.mult, op1=ALU.mult),
                         reads=x2k + [rsk, "gfbc"], writes=[f"ost{xi}"])
                    row0 = s * 512 + tt * 128
                    p.dma(lambda e, xi=xi, row0=row0: e.dma_start(out=out_d[row0:row0 + 128, :], in_=ost[xi][:]), f"st_out{xi}",
                          reads=[f"ost{xi}"], q="pool")
            p.final_wait("sp", [k for k in p.count if k not in ("pe", "act", "dve", "pool")])
        p.emit(block, sems)
    return nc, list(dbg.keys())


def _bf(a):
    return np.ascontiguousarray(a).astype(NPBF)


def prep_inputs(inp):
    f32 = np.float32
    x = np.asarray(inp["x"], f32)
    c = np.asarray(inp["c"], f32)
    L = 0
    shared = {}
    shared["w_ada"] = np.ascontiguousarray(np.asarray(inp["w_ada"], f32)[L])
    b_ada = np.asarray(inp["b_ada"], f32)[L]
    shared["b_adaT"] = np.ascontiguousarray(b_ada.reshape(48, 128).T)
    shared["b_ada_row"] = np.ascontiguousarray(b_ada.reshape(1, -1))
    shared["n1gT"] = np.ascontiguousarray(np.asarray(inp["norm1_g"], f32)[L].reshape(8, 128).T)
    shared["n2gT"] = np.ascontiguousarray(np.asarray(inp["norm2_g"], f32)[L].reshape(8, 128).T)
    shared["nfg_row"] = np.ascontiguousarray(np.asarray(inp["norm_f_g"], f32).reshape(1, D))
    shared["w_in"] = np.ascontiguousarray(np.asarray(inp["w_in"], f32)[L])

    def pairlay(a):
        return np.ascontiguousarray(a.reshape(16, 2, 64).transpose(1, 2, 0).reshape(128, 16))

    lam_re = np.asarray(inp["lam_re"], f32)[L]
    lam_im = np.asarray(inp["lam_im"], f32)[L]
    log_dt = np.asarray(inp["log_dt"], f32)[L]
    shared["lamre_p"] = pairlay(lam_re)
    shared["lamim_p"] = pairlay(lam_im)
    shared["logdt_p"] = pairlay(np.repeat(log_dt[:, None], 64, axis=1))

    def pairlay3(a):
        return np.ascontiguousarray(a.reshape(16, 2, 64, a.shape[2]).transpose(1, 2, 0, 3).reshape(128, 16, a.shape[2]))

    shared["bre_p"] = pairlay3(np.asarray(inp["b_re"], f32)[L])
    shared["bim_p"] = pairlay3(np.asarray(inp["b_im"], f32)[L])
    shared["cre_p"] = pairlay3(np.asarray(inp["c_re"], f32)[L].transpose(0, 2, 1))
    shared["cim_p"] = pairlay3(np.asarray(inp["c_im"], f32)[L].transpose(0, 2, 1))
    d_skip = np.asarray(inp["d_skip"], f32)[L]
    shared["dvec"] = np.ascontiguousarray(np.tile(d_skip.T, (8, 1)))
    shared["w_glu"] = np.ascontiguousarray(np.asarray(inp["w_glu"], f32)[L])
    shared["b_gluT"] = np.ascontiguousarray(np.asarray(inp["b_glu"], f32)[L].reshape(4, 128).T)
    shared["w_a"] = np.ascontiguousarray(np.asarray(inp["w_a"], f32)[L])
    shared["w_b"] = np.ascontiguousarray(np.asarray(inp["w_b"], f32)[L])
    shared["w_o"] = np.ascontiguousarray(np.asarray(inp["w_o"], f32)[L])
    shared["wg"] = np.ascontiguousarray(np.asarray(inp["w_ffn_gate"], f32)[L])
    shared["wu"] = np.ascontiguousarray(np.asarray(inp["w_ffn_up"], f32)[L])
    shared["wd"] = np.ascontiguousarray(np.asarray(inp["w_ffn_down"], f32)[L])
    shared["identf"] = np.eye(128, dtype=f32)
    jj = np.arange(128)
    shared["atri"] = _bf(np.where(jj[:, None] <= jj[None, :], NEG_BIG, 0.0).astype(f32))
    shared["neguincl"] = _bf(np.where(jj[:, None] >= jj[None, :], -1.0, 0.0).astype(f32))
    shared["neglow"] = _bf(np.where(jj[:, None] < jj[None, :], -1.0, 0.0).astype(f32))
    ii = jj // 16
    shared["cmask"] = (ii[None, :] >= ii[:, None]).astype(f32)
    kvals = np.concatenate([-np.arange(8), 7 - np.arange(8), np.arange(8), 1 + np.arange(8)]).astype(f32)
    shared["kv"] = np.ascontiguousarray(np.tile(kvals[None, :], (128, 1)))
    shared["ones_row"] = np.ones((1, 128), f32)
    hm = np.zeros((128, 2), f32)
    hm[:64, 0] = 1.0
    hm[64:, 1] = 1.0
    shared["hmask"] = hm
    in_maps = []
    for core in range(8):
        b, hf = core // 2, core % 2
        own = OWN[hf]
        m = dict(shared)
        m["x_all"] = np.ascontiguousarray(x[b])
        m["x_own"] = np.ascontiguousarray(np.concatenate([x[b, cc * 512:(cc + 1) * 512] for cc in own], axis=0))
        m["cT"] = np.ascontiguousarray(c[b].reshape(8, 128).T)
        maskB = np.zeros((128, 4, 8, 512), f32)
        sel = np.zeros((128, 4, 64), f32)
        qq = np.arange(512)
        for s in range(4):
            cs = own[s]
            for r in range(8):
                off = 512 * (cs - 2 * s) - 128 * r
                tq = qq + off
                allm = tq < 0
                part = (tq >= 0) & (tq <= 127)
                maskB[0, s, r, allm] = 1.0
                maskB[tq[part], s, r, qq[part]] = 1.0
            offn = (cs - 2 * s) * 64
            sel[np.arange(64) + offn, s, np.arange(64)] = 1.0
        m["maskB"] = _bf(maskB)
        m["sel"] = _bf(sel)
        in_maps.append(m)
    return in_maps


_CACHE = {}


def kernel(**inputs):
    if "nc" not in _CACHE:
        _CACHE["nc"] = build_program()
    nc, dbg_names = _CACHE["nc"]
    in_maps = prep_inputs(inputs)
    ncores = int(os.environ.get("K_NCORES", "8"))
    res = run_bass_kernel_spmd(nc, in_maps[:ncores], core_ids=list(range(ncores)))
    out = np.zeros((4, T, D), np.float32)
    for core in range(ncores):
        b, hf = core // 2, core % 2
        o = np.asarray(res.results[core]["out"], np.float32)
        for s, cc in enumerate(OWN[hf]):
            out[b, cc * 512:(cc + 1) * 512] = o[s * 512:(s + 1) * 512]
    _CACHE["last_results"] = res.results
    return out
```
